# Optimizing a Trainium2 kernel written in Bass

```python
import math
import jax, jax.numpy as jnp
from jax import lax
import numpy as np

D_MODEL = 4096
BATCH = 4
SEQ = 2048
DEPTH = 1

ATTN_HEADS = 16
HEAD_DIM = 128
ATTN_WIDTH = ATTN_HEADS * HEAD_DIM
DILATED_PATTERNS = ((128, 1), (512, 4), (2048, 16))
ROPE_THETA = 10000.0
SSM_WIDTH = D_MODEL // 2
SSM_GROUP = 16
SSM_GROUPS = SSM_WIDTH // SSM_GROUP
SSM_STATE = 64
N_DIRECTIONS = 2
DT_MIN = 1e-3
DT_MAX = 1e-1
D_FF = 3 * D_MODEL
CONV_WIDTH = 3
PLE_DIM = 256
N_BRANCHES = 2
IN_WIDTH = 3 * ATTN_WIDTH + SSM_WIDTH + N_BRANCHES * D_MODEL
EPS = 1e-6
NEG_BIG = -1e30

kernel_name = "hybrid_dilated_attn_s5_convffn_block"


def rms_norm(x, gain):
    xf = x.astype(jnp.float32)
    y = xf * lax.rsqrt(jnp.mean(xf * xf, axis=-1, keepdims=True) + EPS)
    return (y * gain.astype(jnp.float32)).astype(x.dtype)


def rope_tables(seq):
    pos = jnp.arange(seq, dtype=jnp.float32)
    inv_freq = ROPE_THETA ** (-jnp.arange(0, HEAD_DIM, 2, dtype=jnp.float32) / HEAD_DIM)
    ang = pos[:, None] * inv_freq[None, :]
    ang = jnp.concatenate([ang, ang], axis=-1)
    return jnp.cos(ang)[None, :, None, :], jnp.sin(ang)[None, :, None, :]


def apply_rope(t, cos, sin):
    tf = t.astype(jnp.float32)
    t1, t2 = jnp.split(tf, 2, axis=-1)
    return tf * cos + jnp.concatenate([-t2, t1], axis=-1) * sin


def _dilated_window_partial(q, k, v, dilation, n_side):
    bsz, seq, heads, hd = q.shape
    sub_len = seq // dilation
    n_sub = bsz * dilation

    def to_sub(t):
        return t.reshape(bsz, sub_len, dilation, heads, hd).transpose(0, 2, 1, 3, 4).reshape(n_sub, sub_len, heads, hd)

    blk = n_side
    n_blk = -(-sub_len // blk)
    pad = n_blk * blk - sub_len
    qb = jnp.pad(to_sub(q), ((0, 0), (0, pad), (0, 0), (0, 0))).reshape(n_sub, n_blk, blk, heads, hd)

    def band(t):
        tp = jnp.pad(to_sub(t), ((0, 0), (blk, pad + blk), (0, 0), (0, 0))).reshape(n_sub, n_blk + 2, blk, heads, hd)
        return jnp.concatenate([tp[:, :-2], tp[:, 1:-1], tp[:, 2:]], axis=2)

    kw, vw = band(k), band(v)
    s = jnp.einsum('nbqhd,nbkhd->nbhqk', qb, kw, preferred_element_type=jnp.float32)
    q_idx = jnp.arange(n_blk)[:, None, None] * blk + jnp.arange(blk)[None, :, None]
    k_idx = (jnp.arange(n_blk)[:, None, None] - 1) * blk + jnp.arange(3 * blk)[None, None, :]
    valid = (jnp.abs(k_idx - q_idx) <= n_side) & (k_idx >= 0) & (k_idx < sub_len)
    s = jnp.where(valid[None, :, None], s, NEG_BIG)
    m = jnp.max(s, axis=-1)
    e = jnp.exp(s - m[..., None])
    l = jnp.sum(e, axis=-1)
    o = jnp.einsum('nbhqk,nbkhd->nbqhd', e, vw.astype(jnp.float32))

    def from_sub(t):
        t = t[:, :sub_len]
        rest = t.shape[2:]
        t = t.reshape((bsz, dilation, sub_len) + rest)
        t = t.transpose((0, 2, 1) + tuple(range(3, 3 + len(rest))))
        return t.reshape((bsz, seq) + rest)

    o = from_sub(o.reshape(n_sub, n_blk * blk, heads, hd))
    m = from_sub(m.transpose(0, 1, 3, 2).reshape(n_sub, n_blk * blk, heads))
    l = from_sub(l.transpose(0, 1, 3, 2).reshape(n_sub, n_blk * blk, heads))
    return m, l, o


def dilated_mixture_attention(q, k, v):
    parts = [_dilated_window_partial(q, k, v, d, w // (2 * d)) for (w, d) in DILATED_PATTERNS]
    m_max = jnp.max(jnp.stack([pm for (pm, _, _) in parts]), axis=0)
    num = jnp.zeros(q.shape, jnp.float32)
    den = jnp.zeros(q.shape[:-1], jnp.float32)
    for (pm, pl, po) in parts:
        w = jnp.exp(pm - m_max)
        num = num + w[..., None] * po
        den = den + w * pl
    return num / den[..., None]


def _complex_affine_combine(e1, e2):
    a1r, a1i, b1r, b1i = e1
    a2r, a2i, b2r, b2i = e2
    return (a2r * a1r - a2i * a1i,
            a2r * a1i + a2i * a1r,
            a2r * b1r - a2i * b1i + b2r,
            a2r * b1i + a2i * b1r + b2i)


def s5_bidirectional_glu(u, lam_re, lam_im, log_dt, b_re, b_im, c_re, c_im, d_skip, w_glu, b_glu):
    bsz, seq, width = u.shape
    uf = u.astype(jnp.float32)
    ug = uf.reshape(bsz, seq, SSM_GROUPS, SSM_GROUP)
    y = uf * d_skip.astype(jnp.float32)
    for direction in range(N_DIRECTIONS):
        lr = lam_re[direction].astype(jnp.float32)
        li = lam_im[direction].astype(jnp.float32)
        dt = jnp.exp(log_dt[direction].astype(jnp.float32))[:, None]
        mag = jnp.exp(lr * dt)
        ar = mag * jnp.cos(li * dt)
        ai = mag * jnp.sin(li * dt)
        den = lr * lr + li * li
        fr = ((ar - 1.0) * lr + ai * li) / den
        fi = (ai * lr - (ar - 1.0) * li) / den
        br = b_re[direction].astype(jnp.float32)
        bi = b_im[direction].astype(jnp.float32)
        bbr = fr[..., None] * br - fi[..., None] * bi
        bbi = fr[..., None] * bi + fi[..., None] * br
        src = ug if direction == 0 else jnp.flip(ug, axis=1)
        xr = jnp.einsum('bsgc,gpc->sbgp', src, bbr)
        xi = jnp.einsum('bsgc,gpc->sbgp', src, bbi)
        a_r = jnp.broadcast_to(ar[None, None], (seq, 1, SSM_GROUPS, SSM_STATE))
        a_i = jnp.broadcast_to(ai[None, None], (seq, 1, SSM_GROUPS, SSM_STATE))
        _, _, hr, hi = lax.associative_scan(_complex_affine_combine, (a_r, a_i, xr, xi), axis=0)
        out = (jnp.einsum('sbgp,gcp->bsgc', hr, c_re[direction].astype(jnp.float32))
               - jnp.einsum('sbgp,gcp->bsgc', hi, c_im[direction].astype(jnp.float32)))
        if direction == 1:
            out = jnp.flip(out, axis=1)
        y = y + out.reshape(bsz, seq, width)
    z = jax.nn.gelu(y.astype(u.dtype), approximate=True)
    return z * jax.nn.sigmoid(z @ w_glu + b_glu)


def centred_depthwise_conv(a, w, b):
    half = CONV_WIDTH // 2
    seq = a.shape[1]
    ap = jnp.pad(a, ((0, 0), (half, CONV_WIDTH - 1 - half), (0, 0)))
    out = b
    for j in range(CONV_WIDTH):
        out = out + ap[:, j:j + seq] * w[j]
    return out


def setup_inputs(seed: int = 0) -> dict:
    key = jax.random.key(seed)
    ks = jax.random.split(key, 32)
    f32 = jnp.float32

    def nrm(k, shape, scale):
        return jax.random.normal(k, shape, f32) * scale

    def gain(k):
        return 1.0 + nrm(k, (DEPTH, D_MODEL), 0.02)

    G, P, CG = SSM_GROUPS, SSM_STATE, SSM_GROUP
    lam_re = -0.5 + nrm(ks[10], (DEPTH, N_DIRECTIONS, G, P), 0.01)
    lam_im = math.pi * jnp.arange(P, dtype=f32) + nrm(ks[11], (DEPTH, N_DIRECTIONS, G, P), 0.01)
    log_dt = jax.random.uniform(ks[12], (DEPTH, N_DIRECTIONS, G), f32, math.log(DT_MIN), math.log(DT_MAX))
    return {
        "x": nrm(ks[0], (BATCH, SEQ, D_MODEL), 1.0),
        "p": nrm(ks[1], (DEPTH, BATCH, SEQ, PLE_DIM), 1.0),
        "g_mix_pre": gain(ks[2]),
        "g_mix_post": gain(ks[3]),
        "w_in": nrm(ks[4], (DEPTH, D_MODEL, IN_WIDTH), D_MODEL ** -0.5),
        "w_branch_attn": nrm(ks[5], (DEPTH, ATTN_WIDTH, D_MODEL), ATTN_WIDTH ** -0.5),
        "w_branch_ssm": nrm(ks[6], (DEPTH, SSM_WIDTH, D_MODEL), SSM_WIDTH ** -0.5),
        "w_out": nrm(ks[7], (DEPTH, D_MODEL, D_MODEL), D_MODEL ** -0.5),
        "ssm_lambda_re": lam_re,
        "ssm_lambda_im": lam_im,
        "ssm_log_dt": log_dt,
        "ssm_b_re": nrm(ks[13], (DEPTH, N_DIRECTIONS, G, P, CG), (2 * CG) ** -0.5),
        "ssm_b_im": nrm(ks[14], (DEPTH, N_DIRECTIONS, G, P, CG), (2 * CG) ** -0.5),
        "ssm_c_re": nrm(ks[15], (DEPTH, N_DIRECTIONS, G, CG, P), P ** -0.5),
        "ssm_c_im": nrm(ks[16], (DEPTH, N_DIRECTIONS, G, CG, P), P ** -0.5),
        "ssm_d": nrm(ks[17], (DEPTH, SSM_WIDTH), 1.0),
        "w_glu": nrm(ks[18], (DEPTH, SSM_WIDTH, SSM_WIDTH), SSM_WIDTH ** -0.5),
        "b_glu": nrm(ks[19], (DEPTH, SSM_WIDTH), 0.01),
        "g_ffn_pre": gain(ks[20]),
        "g_ffn_post": gain(ks[21]),
        "w_up": nrm(ks[22], (DEPTH, D_MODEL, 2 * D_FF), D_MODEL ** -0.5),
        "conv_w": nrm(ks[23], (DEPTH, CONV_WIDTH, D_FF), CONV_WIDTH ** -0.5),
        "conv_b": nrm(ks[24], (DEPTH, D_FF), 0.01),
        "w_down": nrm(ks[25], (DEPTH, D_FF, D_MODEL), D_FF ** -0.5),
        "w_ple": nrm(ks[26], (DEPTH, PLE_DIM, D_MODEL), PLE_DIM ** -0.5),
        "g_ple": gain(ks[27]),
        "w_ple_gate": nrm(ks[28], (DEPTH, D_MODEL, D_MODEL), D_MODEL ** -0.5),
    }


def reference(x, p, g_mix_pre, g_mix_post, w_in, w_branch_attn, w_branch_ssm, w_out,
              ssm_lambda_re, ssm_lambda_im, ssm_log_dt, ssm_b_re, ssm_b_im, ssm_c_re, ssm_c_im,
              ssm_d, w_glu, b_glu, g_ffn_pre, g_ffn_post, w_up, conv_w, conv_b, w_down,
              w_ple, g_ple, w_ple_gate):
    bsz, seq, _ = x.shape
    cos, sin = rope_tables(seq)
    splits = [ATTN_WIDTH, 2 * ATTN_WIDTH, 3 * ATTN_WIDTH, 3 * ATTN_WIDTH + SSM_WIDTH,
              3 * ATTN_WIDTH + SSM_WIDTH + D_MODEL]
    h = x
    for i in range(DEPTH):
        hn = rms_norm(h, g_mix_pre[i])
        proj = hn @ w_in[i]
        q, k, v, u, gate_a, gate_b = jnp.split(proj, splits, axis=-1)
        q = apply_rope(q.reshape(bsz, seq, ATTN_HEADS, HEAD_DIM), cos, sin) * (HEAD_DIM ** -0.5)
        k = apply_rope(k.reshape(bsz, seq, ATTN_HEADS, HEAD_DIM), cos, sin)
        v = v.reshape(bsz, seq, ATTN_HEADS, HEAD_DIM).astype(jnp.float32)
        y_a = dilated_mixture_attention(q, k, v).reshape(bsz, seq, ATTN_WIDTH).astype(h.dtype)
        y_b = s5_bidirectional_glu(u, ssm_lambda_re[i], ssm_lambda_im[i], ssm_log_dt[i],
                                   ssm_b_re[i], ssm_b_im[i], ssm_c_re[i], ssm_c_im[i],
                                   ssm_d[i], w_glu[i], b_glu[i])
        merged = (jax.nn.sigmoid(gate_a) * (y_a @ w_branch_attn[i])
                  + jax.nn.sigmoid(gate_b) * (y_b @ w_branch_ssm[i]))
        h = h + rms_norm(merged @ w_out[i], g_mix_post[i])
        hn = rms_norm(h, g_ffn_pre[i])
        a, b = jnp.split(hn @ w_up[i], [D_FF], axis=-1)
        a = centred_depthwise_conv(a, conv_w[i], conv_b[i])
        h = h + rms_norm((jax.nn.gelu(a, approximate=True) * b) @ w_down[i], g_ffn_post[i])
        ple = rms_norm(p[i] @ w_ple[i], g_ple[i])
        h = h + jax.nn.sigmoid(h @ w_ple_gate[i]) * ple
    return h
```

```python
import math
from contextlib import ExitStack
import numpy as np
import concourse.bass as bass
import concourse.mybir as mybir
from concourse.bass_utils import run_bass_kernel_spmd

F32 = mybir.dt.float32
BF16 = mybir.dt.bfloat16
I32 = mybir.dt.int32
ALU = mybir.AluOpType
AF = mybir.ActivationFunctionType

NDS = 40
EPS = 1e-6
HALO = 8
TWO_PI = float(2 * np.pi)

FULL_CFG = dict(D=4096, T=2048, H=16, G=128, DFF=12288, PLE=256,
                patterns=((128, 1), (512, 4), (2048, 16)), nb=4)


class Buf:
    __slots__ = ("name", "w", "r", "excl")

    def __init__(self, name="", excl=False):
        self.name = name
        self.w = None
        self.r = []
        self.excl = excl


class Sched:
    ENG = ("pe", "dve", "act", "pool", "sp")

    def __init__(self, nc, stack):
        self.nc = nc
        self.csem = {k: stack.enter_context(nc.semaphore("c_" + k)) for k in self.ENG}
        self.cnt = {k: 0 for k in self.ENG}
        self.dsem = [stack.enter_context(nc.semaphore("d%d" % i)) for i in range(NDS)]
        self.dcnt = [0] * NDS
        self.dnext = 0
        self.waited = {}
        self.ops = {k: [] for k in self.ENG}

    def _deps(self, eng, reads, writes):
        evs = []
        for b in reads:
            if b.w is not None:
                evs.append(b.w)
            if b.excl:
                evs.extend(b.r)
        for b in writes:
            if b.w is not None:
                evs.append(b.w)
            evs.extend(b.r)
        waits = {}
        for (key, sem, val, e) in evs:
            if e == eng and eng == "pe":
                continue
            if self.waited.get((eng, key), 0) >= val:
                continue
            if waits.get(key, (None, 0))[1] < val:
                waits[key] = (sem, val)
        for key, (sem, val) in waits.items():
            self.waited[(eng, key)] = val
        return list(waits.values())

    def _record(self, ev, reads, writes):
        for b in reads:
            b.r.append(ev)
            if len(b.r) > 24:
                b.r = b.r[-24:] if False else b.r
        for b in writes:
            b.w = ev
            b.r = []

    def op(self, eng, fn, reads=(), writes=()):
        waits = self._deps(eng, reads, writes)
        self.cnt[eng] += 1
        ev = ("c_" + eng, self.csem[eng], self.cnt[eng], eng)
        self._record(ev, reads, writes)
        self.ops[eng].append((waits, fn, self.csem[eng], 1))

    def dma(self, eng, out, in_, reads=(), writes=()):
        i = self.dnext
        self.dnext = (self.dnext + 1) % NDS
        waits = self._deps(eng, reads, writes)
        key = "d%d" % i
        if self.dcnt[i] > 0 and self.waited.get((eng, key), 0) < self.dcnt[i]:
            waits.append((self.dsem[i], self.dcnt[i]))
            self.waited[(eng, key)] = self.dcnt[i]
        self.dcnt[i] += 16
        ev = (key, self.dsem[i], self.dcnt[i], "dma")
        self._record(ev, reads, writes)

        def fn(e, out=out, in_=in_):
            return e.dma_start(out=out, in_=in_)

        self.ops[eng].append((waits, fn, self.dsem[i], 16))

    def barrier(self):
        for eng in self.ENG:
            waits = []
            for k in self.ENG:
                if k == eng or self.cnt[k] == 0:
                    continue
                key = "c_" + k
                if self.waited.get((eng, key), 0) < self.cnt[k]:
                    waits.append((self.csem[k], self.cnt[k]))
                    self.waited[(eng, key)] = self.cnt[k]
            for i in range(NDS):
                key = "d%d" % i
                if self.dcnt[i] > 0 and self.waited.get((eng, key), 0) < self.dcnt[i]:
                    waits.append((self.dsem[i], self.dcnt[i]))
                    self.waited[(eng, key)] = self.dcnt[i]
            if waits:
                self.ops[eng].append((waits, None, None, 0))

    def emit(self):
        nc = self.nc
        ops = self.ops
        self.ops = {k: [] for k in self.ENG}

        def run(e, lst):
            for (waits, fn, sem, inc) in lst:
                for (sm, v) in waits:
                    e.wait_ge(sm, v)
                if fn is not None:
                    fn(e).then_inc(sem, inc)

        with nc.Block() as block:
            if ops["pe"]:
                @block.tensor
                def _(e):
                    run(e, ops["pe"])
            if ops["dve"]:
                @block.vector
                def _(e):
                    run(e, ops["dve"])
            if ops["act"]:
                @block.scalar
                def _(e):
                    run(e, ops["act"])
            if ops["pool"]:
                @block.gpsimd
                def _(e):
                    run(e, ops["pool"])
            if ops["sp"]:
                @block.sync
                def _(e):
                    run(e, ops["sp"])


class Prog:
    def __init__(self, nc, s):
        self.nc = nc
        self.s = s
        self.rr = 0

    def mm(self, out, lhsT, rhs, start, stop, r=(), w=()):
        self.s.op("pe", lambda e: e.matmul(out, lhsT=lhsT, rhs=rhs, start=start, stop=stop), r, w)

    def act(self, out, in_, func, r=(), w=(), bias=None, scale=1.0):
        if bias is None:
            self.s.op("act", lambda e: e.activation(out=out, in_=in_, func=func, scale=scale), r, w)
        else:
            self.s.op("act", lambda e: e.activation(out=out, in_=in_, func=func, bias=bias, scale=scale), r, w)

    def tt(self, eng, out, in0, in1, op, r=(), w=()):
        self.s.op(eng, lambda e: e.tensor_tensor(out=out, in0=in0, in1=in1, op=op), r, w)

    def ts(self, eng, out, in0, s1, s2, op0, op1=None, r=(), w=()):
        if op1 is None:
            self.s.op(eng, lambda e: e.tensor_scalar(out=out, in0=in0, scalar1=s1, scalar2=None, op0=op0), r, w)
        else:
            self.s.op(eng, lambda e: e.tensor_scalar(out=out, in0=in0, scalar1=s1, scalar2=s2, op0=op0, op1=op1), r, w)

    def stt(self, eng, out, in0, scalar, in1, op0, op1, r=(), w=()):
        self.s.op(eng, lambda e: e.scalar_tensor_tensor(out=out, in0=in0, scalar=scalar, in1=in1, op0=op0, op1=op1), r, w)

    def copy(self, eng, out, in_, r=(), w=()):
        if eng == "act":
            self.s.op("act", lambda e: e.activation(out=out, in_=in_, func=AF.Copy), r, w)
        else:
            self.s.op(eng, lambda e: e.tensor_copy(out=out, in_=in_), r, w)

    def recip(self, out, in_, r=(), w=()):
        self.s.op("dve", lambda e: e.reciprocal(out=out, in_=in_), r, w)

    def memset(self, eng, ap, val, w=()):
        self.s.op(eng, lambda e: e.memset(ap, val), (), w)

    def dma(self, eng, out, in_, r=(), w=()):
        self.s.dma(eng, out, in_, r, w)

    def phase_end(self):
        self.s.barrier()
        self.s.emit()


def split_blocks(n, bs=512):
    out = []
    t = 0
    while t < n:
        w = min(bs, n - t)
        out.append((t, w))
        t += w
    return out


def bcast_ap(t, offset, dims, rowlen, nparts=128, pbase=0):
    return bass.AP(t, pbase * rowlen + offset, [[rowlen, nparts]] + [[a, b] for (a, b) in dims])


def build_program(cfg, dbg=False):
    D, T, H, G, DFF, PLE = cfg["D"], cfg["T"], cfg["H"], cfg["G"], cfg["DFF"], cfg["PLE"]
    TO = T // 2
    TE = TO + HALO
    DC = D // 128
    AW = H * 128
    SW = G * 16
    SC = SW // 128
    AC = AW // 128
    FC = DFF // 128
    PC = PLE // 128
    NK = T // 8
    NKO = TO // 8 + 1
    GH = min(G, 64)
    blocks_ext = split_blocks(TO) + [(TO, HALO)]
    blocks_own = split_blocks(TO)
    blocks_full = split_blocks(T)
    WMAX = max(w // 2 for (w, d) in cfg["patterns"])
    X0 = T - 128
    MW = T - 128 + TE
    att_scale = 128 ** -0.5

    nc = bass.Bass("TRN2", target_bir_lowering=False)
    uid = [0]

    def SBT(name, shape, dt):
        uid[0] += 1
        return nc.sbuf_tensor("%s_%d" % (name, uid[0]), shape, dt)

    def PST(name, shape, dt):
        uid[0] += 1
        return nc.psum_tensor("%s_%d" % (name, uid[0]), shape, dt)

    def din(name, shape, dt=F32):
        return nc.dram_tensor(name, list(shape), dt, kind="ExternalInput").ap()

    def dscr(name, shape, dt):
        return nc.dram_tensor(name, list(shape), dt, kind=("ExternalOutput" if dbg else "Internal")).ap()

    xT = din("xT", [D, T])
    xTt = din("xTt", [T // 128, 128, DC, 128])
    pT = din("pT", [PLE, TO])
    cosT = din("cosT", [128, T])
    sinT = din("sinT", [128, T])
    gains = din("gains", [128, 5, DC])
    w_qk = din("w_qk", [2 * H, 128, DC, 128])
    w_u = din("w_u", [SC, 128, DC, 128])
    w_g = din("w_g", [2 * DC, 128, DC, 128])
    w_v = din("w_v", [AW // 512 if AW >= 512 else 1, 128, DC, min(AW, 512)])
    w_ba = din("w_ba", [DC, 128, AC, 128])
    w_bs = din("w_bs", [DC, 128, SC, 128])
    w_out = din("w_out", [DC, 128, DC, 128])
    w_glu = din("w_glu", [SC, 128, SC, 128])
    b_glu = din("b_glu", [128, SC])
    w_up = din("w_up", [2 * FC, 128, DC, 128])
    conv_w = din("conv_w", [128, 3, FC])
    conv_b = din("conv_b", [128, FC])
    w_down = din("w_down", [DC, 128, FC, 128])
    w_ple = din("w_ple", [DC, 128, PC, 128])
    w_pg = din("w_pg", [DC, 128, DC, 128])
    s_lam = din("s_lam", [128, 3, G])
    s_B = din("s_B", [128, 2, G, 16])
    s_C = din("s_C", [128, 2, G, 16])
    s_D = din("s_D", [128, G])
    c_ntab = din("c_ntab", [128, 26])
    c_mask = din("c_mask", [128, 2, 128])
    c_ident = din("c_ident", [128, 128])
    c_perm = din("c_perm", [128, 128])
    c_sel = din("c_sel", [128, 64, 128])
    c_selT = din("c_selT", [128, 64, 128])
    c_mtab = din("c_mtab", [128, MW])

    outT = nc.dram_tensor("outT", [D, TO], F32, kind="ExternalOutput").ap()

    qT = dscr("qT", [AW, TE], BF16)
    kT = dscr("kT", [AW, T], BF16)
    vS = dscr("vS", [T, AW], BF16)
    uT = dscr("uT", [SW, T], BF16)
    sgT = dscr("sgT", [2 * D, TE], BF16)
    yaT = dscr("yaT", [AW, TE], BF16)
    zT = dscr("zT", [SW, TE], BF16)
    oT = dscr("oT", [D, TE], F32)
    h1T = dscr("h1T", [D, TE], F32)
    actT = dscr("actT", [DFF, TO], BF16)
    dT = dscr("dT", [D, TO], F32)
    h2T = dscr("h2T", [D, TO], F32)
    hn2T = dscr("hn2T", [D, TE], BF16)
    rs2T = dscr("rs2T", [128, TE], F32)

    with ExitStack() as top:
        s = Sched(nc, top)
        P = Prog(nc, s)

        def rr_eng(engs=("dve", "act", "pool")):
            P.rr += 1
            return engs[P.rr % len(engs)]

        def make_consts(ph):
            ones = ph.enter_context(SBT("ones", [128, 128], BF16))
            Bo = Buf("ones")
            P.memset("pool", ones[:], 1.0, w=[Bo])
            epsT = ph.enter_context(SBT("epsT", [128, 1], F32))
            Be = Buf("eps")
            P.memset("pool", epsT[:], EPS, w=[Be])
            return ones, Bo, epsT, Be

        def psum_pool(ph, n, name="ps"):
            tiles = []
            for i in range(n):
                t = ph.enter_context(PST("%s%d" % (name, i), [128, 512], F32))
                tiles.append((t, Buf("%s%d" % (name, i), True)))
            return tiles

        class Rot:
            def __init__(self, items):
                self.items = items
                self.i = 0

            def next(self):
                it = self.items[self.i % len(self.items)]
                self.i += 1
                return it

        def sb_pool(ph, n, name, shape, dt):
            return Rot([(ph.enter_context(SBT("%s%d" % (name, i), shape, dt)), Buf("%s%d" % (name, i)))
                        for i in range(n)])

        def linear(wd, m_list, KC, mw, rhs_fn, rhs_bufs, blocks_of, epilogue, wpool, pspool, post_tile=None):
            tiles = {}

            def load(m):
                wt, wB = wpool.next()
                P.dma("pool", wt[:], wd[m], w=[wB])
                tiles[m] = (wt, wB)

            m_list = list(m_list)
            load(m_list[0])
            for mi, m in enumerate(m_list):
                if mi + 1 < len(m_list):
                    load(m_list[mi + 1])
                wt, wB = tiles.pop(m)
                for (b0, bn) in blocks_of(m):
                    for sub in range(mw // 128):
                        ps, psB = pspool.next()
                        for c in range(KC):
                            P.mm(ps[:, 0:bn], wt[:, c, sub * 128:(sub + 1) * 128], rhs_fn(c, b0, bn),
                                 c == 0, c == KC - 1, r=[wB] + list(rhs_bufs), w=[psB])
                        epilogue(m, sub, b0, bn, ps, psB)
                if post_tile is not None:
                    post_tile(m)

        def rstd_from(ph, ssq_list, blocks, TN, epsT, Be, name):
            rstd = ph.enter_context(SBT(name, [128, TN], F32))
            B = Buf(name)
            base = blocks[0][0]
            for (sp, spB), (b0, bn) in zip(ssq_list, blocks):
                P.act(rstd[:, b0 - base:b0 - base + bn], sp[:, 0:bn], AF.Sqrt, r=[spB, Be], w=[B], bias=epsT[:, 0:1], scale=1.0 / D)
            P.recip(rstd[:], rstd[:], r=[B], w=[B])
            return rstd, B

        def proj_ssq(ph, tmp, wd, KC, rhs_fn, rhs_bufs, blocks, dst_dram, ones, Bo, tag):
            ssq = [(ph.enter_context(PST("ssq%s%d" % (tag, i), [128, 512], F32)), Buf("ssq", True)) for i in range(len(blocks))]
            wpool = sb_pool(tmp, 3, "wo" + tag, [128, KC, 128], BF16)
            pspool = Rot(psum_pool(tmp, 4, "po" + tag))
            TN = sum(bn for _, bn in blocks)
            base = blocks[0][0]
            ost = sb_pool(tmp, 2, "ost" + tag, [128, TN], F32)
            sqp = sb_pool(tmp, 3, "osq" + tag, [128, 512], BF16)
            pending = []
            state = {}
            n_mt = wd.shape[0]

            def flush():
                while pending:
                    pending.pop(0)()

            def ep(m, sub, b0, bn, ps, psB):
                if b0 == base:
                    state["o"] = ost.next()
                o, Bos = state["o"]
                bi = [b for b, _ in blocks].index(b0)
                P.copy("dve", o[:, b0 - base:b0 - base + bn], ps[:, 0:bn], r=[psB], w=[Bos])
                sq, Bsq = sqp.next()
                P.act(sq[:, 0:bn], ps[:, 0:bn], AF.Square, r=[psB], w=[Bsq])
                flush()
                sp, spB = ssq[bi]
                pending.append(lambda: P.mm(sp[:, 0:bn], ones[:], sq[:, 0:bn], m == 0, m == n_mt - 1, r=[Bo, Bsq], w=[spB]))

            def post(m):
                o, Bos = state["o"]
                P.dma("sp", dst_dram[m * 128:(m + 1) * 128, :], o[:, 0:TN], r=[Bos])

            linear(wd, range(n_mt), KC, 128, rhs_fn, rhs_bufs, lambda m: blocks, ep, wpool, pspool, post)
            flush()
            return ssq

        def ssq_tiles(ph, n, tag):
            return [(ph.enter_context(PST("sq%s%d" % (tag, i), [128, 512], F32)), Buf("sq" + tag, True)) for i in range(n)]

        with ExitStack() as ph:
            ones, Bo, epsT, Be = make_consts(ph)
            hnT = ph.enter_context(SBT("hnT", [128, DC, T], BF16))
            Bhn = Buf("hnT")
            gn = ph.enter_context(SBT("gn", [128, 5, DC], F32))
            Bg = Buf("gn")
            P.dma("sp", gn[:], gains, w=[Bg])
            with ExitStack() as ph0:
                xs_pool = sb_pool(ph0, 2, "xs", [128, DC, 128], F32)
                sq_pool = sb_pool(ph0, 2, "sq", [128, DC, 128], BF16)
                rs_pool = sb_pool(ph0, 2, "rs", [128, 128], F32)
                pp = Rot(psum_pool(ph0, 2, "pn"))
                xTv = xT.rearrange("(c p) t -> p c t", p=128)
                nxt = None
                for tb in range(T // 128):
                    t0 = tb * 128
                    if nxt is None:
                        nxt = xs_pool.next()
                        P.dma("sp", nxt[0][:], xTt[tb], w=[nxt[1]])
                    xs, Bx = nxt
                    if tb + 1 < T // 128:
                        nxt = xs_pool.next()
                        P.dma("sp", nxt[0][:], xTt[tb + 1], w=[nxt[1]])
                    sq, Bs = sq_pool.next()
                    rs, Br = rs_pool.next()
                    ps, psB = pp.next()
                    P.act(sq[:], xs[:], AF.Square, r=[Bx], w=[Bs])
                    for c in range(DC):
                        P.mm(ps[:, 0:128], ones[:], sq[:, c, :], c == 0, c == DC - 1, r=[Bo, Bs], w=[psB])
                    P.act(rs[:], ps[:, 0:128], AF.Sqrt, r=[psB, Be], w=[Br], bias=epsT[:, 0:1], scale=1.0 / D)
                    P.recip(rs[:], rs[:], r=[Br], w=[Br])
                    P.tt("dve", xs[:], xs[:], bcast_ap(rs, 0, [(0, DC), (1, 128)], 128), ALU.mult, r=[Bx, Br], w=[Bx])
                    P.tt("pool", hnT[:, :, t0:t0 + 128], xs[:], bcast_ap(gn, 0, [(1, DC), (0, 128)], 5 * DC),
                         ALU.mult, r=[Bx, Bg], w=[Bhn])
                P.phase_end()

            def hn_rhs(c, b0, bn):
                return hnT[:, c, b0:b0 + bn]

            with ExitStack() as ph1:
                cs = ph1.enter_context(SBT("cs", [128, 2, T], F32))
                Bcs = Buf("cs")
                P.dma("sp", cs[:, 0, :], cosT, w=[Bcs])
                P.dma("sp", cs[:, 1, :], sinT, w=[Bcs])
                permb = ph1.enter_context(SBT("permb", [128, 128], BF16))
                Bperm = Buf("permb")
                P.dma("pool", permb[:], c_perm, w=[Bperm])
                wqpool = sb_pool(ph1, 3, "wqk", [128, DC, 128], BF16)
                pspool = Rot(psum_pool(ph1, 4, "pa"))
                ps2pool = Rot(psum_pool(ph1, 3, "pa2"))
                stg = sb_pool(ph1, 2, "stg", [128, T], BF16)
                qbp = sb_pool(ph1, 3, "qb", [128, 512], BF16)
                t1p = sb_pool(ph1, 3, "t1", [128, 512], F32)
                t2p = sb_pool(ph1, 3, "t2", [128, 512], F32)
                state = {}
                pending = []

                def flush_qk():
                    while pending:
                        pending.pop(0)()

                def ep_qk(m, sub, b0, bn, ps, psB):
                    if b0 == 0:
                        state["stg"] = stg.next()
                    st_, Bst = state["stg"]
                    qb, Bqb = qbp.next()
                    t1, B1 = t1p.next()
                    P.copy("act", qb[:, 0:bn], ps[:, 0:bn], r=[psB], w=[Bqb])
                    P.tt("dve", t1[:, 0:bn], ps[:, 0:bn], cs[:, 0, b0:b0 + bn], ALU.mult, r=[psB, Bcs], w=[B1])
                    flush_qk()

                    def fin(st_=st_, Bst=Bst, qb=qb, Bqb=Bqb, t1=t1, B1=B1, b0=b0, bn=bn):
                        ps2, ps2B = ps2pool.next()
                        P.mm(ps2[:, 0:bn], permb[:], qb[:, 0:bn], True, True, r=[Bperm, Bqb], w=[ps2B])
                        t2, B2 = t2p.next()
                        P.tt("dve", t2[:, 0:bn], ps2[:, 0:bn], cs[:, 1, b0:b0 + bn], ALU.mult, r=[ps2B, Bcs], w=[B2])
                        P.tt("dve", st_[:, b0:b0 + bn], t1[:, 0:bn], t2[:, 0:bn], ALU.add, r=[B1, B2], w=[Bst])
                    pending.append(fin)

                def post_qk(m):
                    flush_qk()
                    st_, Bst = state["stg"]
                    h = m % H
                    if m < H:
                        P.dma("sp", qT[h * 128:(h + 1) * 128, :], st_[:, 0:TE], r=[Bst])
                    else:
                        P.dma("sp", kT[h * 128:(h + 1) * 128, :], st_[:, 0:T], r=[Bst])

                linear(w_qk, range(2 * H), DC, 128, hn_rhs, [Bhn], lambda m: (blocks_ext if m < H else blocks_full),
                       ep_qk, wqpool, pspool, post_qk)
                P.phase_end()

            with ExitStack() as ph1:
                w1pool = sb_pool(ph1, 3, "w1", [128, DC, 128], BF16)
                pspool = Rot(psum_pool(ph1, 6, "pb"))
                stg = sb_pool(ph1, 2, "stgb", [128, T], BF16)
                state = {}

                def ep_u(m, sub, b0, bn, ps, psB):
                    if b0 == 0:
                        state["stg"] = stg.next()
                    st_, Bst = state["stg"]
                    P.copy("act", st_[:, b0:b0 + bn], ps[:, 0:bn], r=[psB], w=[Bst])

                def post_u(m):
                    st_, Bst = state["stg"]
                    P.dma("sp", uT[m * 128:(m + 1) * 128, :], st_[:, 0:T], r=[Bst])

                linear(w_u, range(SC), DC, 128, hn_rhs, [Bhn], lambda m: blocks_full, ep_u, w1pool, pspool, post_u)

                def ep_g(m, sub, b0, bn, ps, psB):
                    if b0 == 0:
                        state["stg"] = stg.next()
                    st_, Bst = state["stg"]
                    P.act(st_[:, b0:b0 + bn], ps[:, 0:bn], AF.Sigmoid, r=[psB], w=[Bst])

                def post_g(m):
                    st_, Bst = state["stg"]
                    P.dma("sp", sgT[m * 128:(m + 1) * 128, :], st_[:, 0:TE], r=[Bst])

                linear(w_g, range(2 * DC), DC, 128, hn_rhs, [Bhn], lambda m: blocks_ext, ep_g, w1pool, pspool, post_g)
                P.phase_end()

            with ExitStack() as ph2:
                VW = min(AW, 512)
                wv = sb_pool(ph2, 2 if DC <= 16 else 1, "wv", [128, DC, VW], BF16)
                pspool = Rot(psum_pool(ph2, 4, "pv"))
                vst = sb_pool(ph2, 3, "vst", [128, VW], BF16)
                for cb in range(AW // VW):
                    wt, wB = wv.next()
                    P.dma("pool", wt[:], w_v[cb], w=[wB])
                    for tt_ in range(T // 128):
                        ps, psB = pspool.next()
                        for c in range(DC):
                            P.mm(ps[:, 0:VW], hnT[:, c, tt_ * 128:(tt_ + 1) * 128], wt[:, c, :], c == 0, c == DC - 1,
                                 r=[wB, Bhn], w=[psB])
                        st_, Bst = vst.next()
                        P.copy(rr_eng(("act", "dve")), st_[:], ps[:, 0:VW], r=[psB], w=[Bst])
                        P.dma("sp", vS[tt_ * 128:(tt_ + 1) * 128, cb * VW:(cb + 1) * VW], st_[:], r=[Bst])
                P.phase_end()

        if cfg.get('stop') == 'A':
            return nc
        with ExitStack() as ph:
            ones, Bo, epsT, Be = make_consts(ph)
            mt = ph.enter_context(SBT("mtab", [128, MW], BF16))
            Bm = Buf("mtab")
            P.dma("pool", mt[:], c_mtab, w=[Bm])
            qp = sb_pool(ph, 2, "qh", [128, TE], BF16)
            kp = sb_pool(ph, 2, "kh", [128, T], BF16)
            vp = sb_pool(ph, 2, "vh", [128, T // 128, 128], BF16)
            pS = Rot(psum_pool(ph, 4, "pS"))
            pO = Rot(psum_pool(ph, 2, "pO"))
            pD = Rot(psum_pool(ph, 2, "pD"))
            pe_ = sb_pool(ph, 4, "pe", [128, 512], BF16)
            pm_ = sb_pool(ph, 4, "pm", [128, 512], BF16)
            rd_ = sb_pool(ph, 2, "rd", [128, 512], F32)
            ya_ = sb_pool(ph, 2, "ya", [128, TE], BF16)
            vSv = vS.rearrange("(kt p) f -> p kt f", p=128)

            def load_head(h):
                qh, Bq = qp.next()
                kh, Bk = kp.next()
                vh, Bv = vp.next()
                P.dma("sp", qh[:], qT[h * 128:(h + 1) * 128, :], w=[Bq])
                P.dma("sp", kh[:], kT[h * 128:(h + 1) * 128, :], w=[Bk])
                P.dma("sp", vh[:], vSv[:, :, h * 128:(h + 1) * 128], w=[Bv])
                return (qh, Bq, kh, Bk, vh, Bv)

            nxt = load_head(0)
            for h in range(H):
                qh, Bq, kh, Bk, vh, Bv = nxt
                if h + 1 < H:
                    nxt = load_head(h + 1)
                ya, Bya = ya_.next()
                for (b0, bn) in blocks_ext:
                    kts = []
                    for kt in range(T // 128):
                        dmin = kt * 128 - (b0 + bn - 1)
                        dmax = kt * 128 + 127 - b0
                        if dmin > WMAX or dmax < -WMAX:
                            continue
                        kts.append(kt)
                    po, poB = pO.next()
                    pd, pdB = pD.next()
                    pend = []
                    BD = cfg.get("bdbg", 9)
                    if BD < 1:
                        continue
                    for i, kt in enumerate(kts):
                        ps, psB = pS.next()
                        P.mm(ps[:, 0:bn], kh[:, kt * 128:(kt + 1) * 128], qh[:, b0:b0 + bn], True, True, r=[Bk, Bq], w=[psB])
                        pe, Bpe = pe_.next()
                        pm, Bpm = pm_.next()
                        if BD < 2:
                            continue
                        P.act(pe[:, 0:bn], ps[:, 0:bn], AF.Exp, r=[psB], w=[Bpe], scale=att_scale)
                        if BD < 3:
                            continue
                        xo = b0 - kt * 128 + X0
                        P.tt("dve", pm[:, 0:bn], pe[:, 0:bn], mt[:, xo:xo + bn], ALU.mult, r=[Bpe, Bm], w=[Bpm])
                        if BD < 4:
                            continue
                        while len(pend) > 1:
                            pend.pop(0)()

                        def f(i=i, kt=kt, pm=pm, Bpm=Bpm, po=po, poB=poB, pd=pd, pdB=pdB, bn=bn, nk=len(kts), vh=vh, Bv=Bv):
                            P.mm(po[:, 0:bn], vh[:, kt, :], pm[:, 0:bn], i == 0, i == nk - 1, r=[Bv, Bpm], w=[poB])
                            P.mm(pd[:, 0:bn], ones[:], pm[:, 0:bn], i == 0, i == nk - 1, r=[Bo, Bpm], w=[pdB])
                        pend.append(f)
                    while pend:
                        pend.pop(0)()
                    if BD < 5:
                        continue
                    rd, Brd = rd_.next()
                    P.recip(rd[:, 0:bn], pd[:, 0:bn], r=[pdB], w=[Brd])
                    P.tt("dve", ya[:, b0:b0 + bn], po[:, 0:bn], rd[:, 0:bn], ALU.mult, r=[poB, Brd], w=[Bya])
                P.dma("sp", yaT[h * 128:(h + 1) * 128, :], ya[:], r=[Bya])
            P.phase_end()

        if cfg.get('stop') == 'B':
            return nc
        KB = max(1, min(16, 512 // (2 * GH)))
        NJ = 8
        NW = 24
        GC = min(GH, 16)
        PI_LO = 3.1415925
        for gh in range(G // GH):
            g0 = gh * GH
            with ExitStack() as ph:
                ident = ph.enter_context(SBT("ident", [128, 128], F32))
                identb = ph.enter_context(SBT("identb", [128, 128], BF16))
                Bid = Buf("ident")
                Bidb = Buf("identb")
                P.dma("sp", ident[:], c_ident, w=[Bid])
                P.dma("pool", identb[:], c_ident, w=[Bidb])
                Ws = ph.enter_context(SBT("Ws", [128, GH, 2, 128], BF16))
                BWs = Buf("Ws")
                Tz = ph.enter_context(SBT("Tz", [128, GH, 128], BF16))
                BTz = Buf("Tz")
                Et = ph.enter_context(SBT("Et", [128, GH, 2, 128], BF16))
                BEt = Buf("Et")
                Ac = ph.enter_context(SBT("Ac", [128, 2, GH, 2], F32))
                BAc = Buf("Ac")
                with ExitStack() as pt:
                    lam = pt.enter_context(SBT("lam", [128, 3, GH], F32))
                    Bl = Buf("lam")
                    P.dma("sp", lam[:], s_lam[:, :, g0:g0 + GH], w=[Bl])
                    ntab = pt.enter_context(SBT("ntab", [128, 26], F32))
                    Bnt = Buf("ntab")
                    P.dma("sp", ntab[:], c_ntab, w=[Bnt])
                    Bt = pt.enter_context(SBT("Bt", [128, 2, GH, 16], F32))
                    Ct = pt.enter_context(SBT("Ct", [128, 2, GH, 16], F32))
                    BBt, BCt = Buf("Bt"), Buf("Ct")
                    P.dma("sp", Bt[:], s_B[:, :, g0:g0 + GH, :], w=[BBt])
                    P.dma("sp", Ct[:], s_C[:, :, g0:g0 + GH, :], w=[BCt])
                    msk = pt.enter_context(SBT("msk", [128, 2, 128], F32))
                    Bmk = Buf("msk")
                    P.dma("sp", msk[:], c_mask, w=[Bmk])
                    Dd = pt.enter_context(SBT("Dd", [128, GH], F32))
                    BDd = Buf("Dd")
                    P.dma("sp", Dd[:], s_D[:, g0:g0 + GH], w=[BDd])
                    sm = pt.enter_context(SBT("sm", [128, 8, GH], F32))
                    Bsm = Buf("sm")
                    P.act(sm[:, 0, :], lam[:, 2, :], AF.Exp, r=[Bl], w=[Bsm])
                    P.tt("dve", sm[:, 1, :], lam[:, 0, :], sm[:, 0, :], ALU.mult, r=[Bl, Bsm], w=[Bsm])
                    P.tt("dve", sm[:, 2, :], lam[:, 1, :], sm[:, 0, :], ALU.mult, r=[Bl, Bsm], w=[Bsm])

                    wk = pt.enter_context(SBT("wk", [128, 4, GH, NW], F32))
                    wki = pt.enter_context(SBT("wki", [128, GH, NW], I32))
                    Bwk = Buf("wk")
                    pwm = pt.enter_context(SBT("pwm", [128, 2, GH, NW], F32))
                    pw2 = pt.enter_context(SBT("pw2", [128, 2, GH, 2], F32))
                    pws = pt.enter_context(SBT("pws", [128, 2, GH, NJ], F32))
                    Bpwm, Bpw2, Bpws = Buf("pwm"), Buf("pw2"), Buf("pws")

                    def cpow(col0, nj, dst, rowlen, Bout):
                        def v(i):
                            return bcast_ap(wk, i * GH * NW, [(NW, GH), (1, nj)], 4 * GH * NW)

                        def o(ri):
                            return bcast_ap(dst, ri * GH * rowlen, [(rowlen, GH), (1, nj)], 2 * GH * rowlen)
                        lrdt_b = bcast_ap(sm, 1 * GH, [(1, GH), (0, nj)], 8 * GH)
                        th_b = bcast_ap(sm, 2 * GH, [(1, GH), (0, nj)], 8 * GH)
                        nt_b = bcast_ap(ntab, col0, [(0, GH), (1, nj)], 26)
                        wki_v = bcast_ap(wki, 0, [(NW, GH), (1, nj)], GH * NW)
                        P.tt("dve", o(0), lrdt_b, nt_b, ALU.mult, r=[Bsm, Bnt], w=[Bout])
                        P.act(o(0), o(0), AF.Exp, r=[Bout], w=[Bout])
                        P.tt("dve", v(0), th_b, nt_b, ALU.mult, r=[Bsm, Bnt], w=[Bwk])
                        for (dsti, shift) in ((3, 0.0), (2, float(np.pi / 2))):
                            P.ts("dve", v(1), v(0), shift, None, ALU.add, r=[Bwk], w=[Bwk])
                            P.ts("dve", wki_v, v(1), 1.0 / TWO_PI, None, ALU.mult, r=[Bwk], w=[Bwk])
                            P.copy("dve", v(2), wki_v, r=[Bwk], w=[Bwk])
                            P.stt("dve", v(1), v(2), -TWO_PI, v(1), ALU.mult, ALU.add, r=[Bwk], w=[Bwk])
                            P.ts("dve", v(1), v(1), -PI_LO, PI_LO, ALU.max, ALU.min, r=[Bwk], w=[Bwk])
                            P.act(v(dsti), v(1), AF.Sin, r=[Bwk], w=[Bwk])
                        P.tt("dve", o(1), o(0), v(3), ALU.mult, r=[Bwk, Bout], w=[Bout])
                        P.tt("dve", o(0), o(0), v(2), ALU.mult, r=[Bwk, Bout], w=[Bout])

                    cpow(24, 2, pw2, 2, Bpw2)
                    cpow(0, NW, pwm, NW, Bpwm)
                    ar = bcast_ap(pw2, 0, [(2, GH)], 2 * GH * 2)
                    ai = bcast_ap(pw2, GH * 2, [(2, GH)], 2 * GH * 2)
                    a8r = bcast_ap(pw2, 1, [(2, GH)], 2 * GH * 2)
                    a8i = bcast_ap(pw2, GH * 2 + 1, [(2, GH)], 2 * GH * 2)
                    P.tt("dve", sm[:, 3, :], lam[:, 0, :], lam[:, 0, :], ALU.mult, r=[Bl], w=[Bsm])
                    P.tt("dve", sm[:, 6, :], lam[:, 1, :], lam[:, 1, :], ALU.mult, r=[Bl], w=[Bsm])
                    P.tt("dve", sm[:, 3, :], sm[:, 3, :], sm[:, 6, :], ALU.add, r=[Bsm], w=[Bsm])
                    P.recip(sm[:, 3, :], sm[:, 3, :], r=[Bsm], w=[Bsm])
                    P.ts("dve", sm[:, 6, :], ar, -1.0, None, ALU.add, r=[Bpw2], w=[Bsm])
                    P.tt("dve", sm[:, 4, :], sm[:, 6, :], lam[:, 0, :], ALU.mult, r=[Bsm, Bl], w=[Bsm])
                    P.tt("dve", sm[:, 7, :], ai, lam[:, 1, :], ALU.mult, r=[Bpw2, Bl], w=[Bsm])
                    P.tt("dve", sm[:, 4, :], sm[:, 4, :], sm[:, 7, :], ALU.add, r=[Bsm], w=[Bsm])
                    P.tt("dve", sm[:, 4, :], sm[:, 4, :], sm[:, 3, :], ALU.mult, r=[Bsm], w=[Bsm])
                    P.tt("dve", sm[:, 5, :], ai, lam[:, 0, :], ALU.mult, r=[Bpw2, Bl], w=[Bsm])
                    P.tt("dve", sm[:, 7, :], sm[:, 6, :], lam[:, 1, :], ALU.mult, r=[Bsm, Bl], w=[Bsm])
                    P.tt("dve", sm[:, 5, :], sm[:, 5, :], sm[:, 7, :], ALU.subtract, r=[Bsm], w=[Bsm])
                    P.tt("dve", sm[:, 5, :], sm[:, 5, :], sm[:, 3, :], ALU.mult, r=[Bsm], w=[Bsm])

                    def Acv(k, ri):
                        return bcast_ap(Ac, k * GH * 2 + ri, [(2, GH)], 2 * GH * 2)
                    P.copy("dve", Acv(0, 0), a8r, r=[Bpw2], w=[BAc])
                    P.copy("dve", Acv(0, 1), a8r, r=[Bpw2], w=[BAc])
                    P.copy("dve", Acv(1, 1), a8i, r=[Bpw2], w=[BAc])
                    P.ts("dve", Acv(1, 0), a8i, -1.0, None, ALU.mult, r=[Bpw2], w=[BAc])
                    fr_b = bcast_ap(sm, 4 * GH, [(1, GH), (0, NJ)], 8 * GH)
                    fi_b = bcast_ap(sm, 5 * GH, [(1, GH), (0, NJ)], 8 * GH)

                    def pm_(ri):
                        return bcast_ap(pwm, ri * GH * NW, [(NW, GH), (1, NJ)], 2 * GH * NW)

                    def ps_(ri):
                        return bcast_ap(pws, ri * GH * NJ, [(NJ, GH), (1, NJ)], 2 * GH * NJ)

                    def wv(i):
                        return bcast_ap(wk, i * GH * NW, [(NW, GH), (1, NJ)], 4 * GH * NW)
                    P.tt("dve", wv(0), pm_(0), fr_b, ALU.mult, r=[Bpwm, Bsm], w=[Bwk])
                    P.tt("dve", wv(1), pm_(1), fi_b, ALU.mult, r=[Bpwm, Bsm], w=[Bwk])
                    P.tt("dve", ps_(0), wv(0), wv(1), ALU.subtract, r=[Bwk], w=[Bpws])
                    P.tt("dve", wv(0), pm_(0), fi_b, ALU.mult, r=[Bpwm, Bsm], w=[Bwk])
                    P.tt("dve", wv(1), pm_(1), fr_b, ALU.mult, r=[Bpwm, Bsm], w=[Bwk])
                    P.tt("dve", ps_(1), wv(0), wv(1), ALU.add, r=[Bwk], w=[Bpws])

                    Xc = pt.enter_context(SBT("Xc", [128, GC, 2, 128], BF16))
                    Yc = pt.enter_context(SBT("Yc", [128, GC, 2, 128], BF16))
                    BXc, BYc = Buf("Xc"), Buf("Yc")
                    tmpx = sb_pool(pt, 1, "ctmpx", [128, 2, GC, 128], F32)
                    tmpy = sb_pool(pt, 1, "ctmpy", [128, 2, GC, 128], F32)

                    def cprod(eng, tmp, ptab, prow, pj0, gc, src, Bsrc, Bp, dst, doff, drow, Bdst, neg_imag):
                        tm, Btm = tmp.next()

                        def pv(ri):
                            return bcast_ap(ptab, ri * GH * prow + gc * GC * prow + pj0, [(prow, GC), (1, 8), (0, 16)], 2 * GH * prow)

                        def sv(ri):
                            return bcast_ap(src, ri * GH * 16 + gc * GC * 16, [(16, GC), (0, 8), (1, 16)], 2 * GH * 16)

                        def tv(i):
                            return bcast_ap(tm, i * GC * 128, [(128, GC), (16, 8), (1, 16)], 2 * GC * 128)

                        def dv(ri):
                            return bcast_ap(dst, doff + ri * 128, [(256, GC), (16, 8), (1, 16)], drow)
                        P.tt(eng, tv(0), pv(0), sv(0), ALU.mult, r=[Bp, Bsrc], w=[Btm])
                        P.tt(eng, tv(1), pv(1), sv(1), ALU.mult, r=[Bp, Bsrc], w=[Btm])
                        P.tt(eng, dv(0), tv(0), tv(1), ALU.subtract, r=[Btm], w=[Bdst])
                        P.tt(eng, tv(0), pv(0), sv(1), ALU.mult, r=[Bp, Bsrc], w=[Btm])
                        P.tt(eng, tv(1), pv(1), sv(0), ALU.mult, r=[Bp, Bsrc], w=[Btm])
                        if neg_imag and eng == "dve":
                            P.stt(eng, dv(1), tv(0), -1.0, tv(1), ALU.mult, ALU.subtract, r=[Btm], w=[Bdst])
                        elif neg_imag:
                            P.ts(eng, tv(0), tv(0), -1.0, None, ALU.mult, r=[Btm], w=[Btm])
                            P.tt(eng, dv(1), tv(0), tv(1), ALU.subtract, r=[Btm], w=[Bdst])
                        else:
                            P.tt(eng, dv(1), tv(0), tv(1), ALU.add, r=[Btm], w=[Bdst])

                    pst = Rot(psum_pool(pt, 4, "pt"))
                    tz1 = sb_pool(pt, 2, "tz1", [128, 4, 128], F32)
                    tz2 = sb_pool(pt, 2, "tz2", [128, 4, 128], F32)
                    for gc in range(GH // GC):
                        cprod("dve", tmpx, pws, NJ, 0, gc, Bt, BBt, Bpws, Xc, 0, GC * 256, BXc, False)
                        cprod("pool", tmpy, pwm, NW, 8, gc, Ct, BCt, Bpwm, Yc, 0, GC * 256, BYc, True)
                        cprod("dve", tmpx, pwm, NW, 16, gc, Ct, BCt, Bpwm, Et, gc * GC * 256, GH * 256, BEt, True)
                        for q4 in range(GC * 2 // 4):
                            ps, psB = pst.next()
                            for i in range(4):
                                idx = q4 * 4 + i
                                gl_, ri = idx // 2, idx % 2
                                P.mm(ps[:, i * 128:(i + 1) * 128], Xc[:, gl_, ri, :], identb[:], True, True, r=[BXc, Bidb], w=[psB])
                            P.copy(rr_eng(("act", "dve")), bcast_ap(Ws, gc * GC * 256 + q4 * 512, [(1, 512)], GH * 256), ps[:, 0:512],
                                   r=[psB], w=[BWs])
                        for g4 in range(GC // 4):
                            psd = []
                            for d in range(2):
                                ps, psB = pst.next()
                                lo, hi = d * 64, (d + 1) * 64
                                for i in range(4):
                                    gl_ = g4 * 4 + i
                                    P.mm(ps[:, i * 128:(i + 1) * 128], Xc[lo:hi, gl_, 0, :], Yc[lo:hi, gl_, 0, :], True, False, r=[BXc, BYc], w=[psB])
                                    P.mm(ps[:, i * 128:(i + 1) * 128], Xc[lo:hi, gl_, 1, :], Yc[lo:hi, gl_, 1, :], False, True, r=[BXc, BYc], w=[psB])
                                psd.append((ps, psB))
                            a1, B1 = tz1.next()
                            a2, B2 = tz2.next()
                            mk0 = bcast_ap(msk, 0, [(0, 4), (1, 128)], 256)
                            mk1 = bcast_ap(msk, 128, [(0, 4), (1, 128)], 256)
                            pv0 = bass.AP(psd[0][0], 0, [[512, 128], [128, 4], [1, 128]])
                            pv1 = bass.AP(psd[1][0], 0, [[512, 128], [128, 4], [1, 128]])
                            P.tt("dve", a1[:], pv0, mk0, ALU.mult, r=[psd[0][1], Bmk], w=[B1])
                            P.tt("dve", a2[:], pv1, mk1, ALU.mult, r=[psd[1][1], Bmk], w=[B2])
                            P.tt("pool", a1[:], a1[:], a2[:], ALU.add, r=[B1, B2], w=[B1])
                            for i in range(4):
                                g = gc * GC + g4 * 4 + i
                                P.stt("dve", Tz[:, g, :], ident[:], Dd[:, g:g + 1], a1[:, i, :], ALU.mult, ALU.add,
                                      r=[Bid, BDd, B1], w=[BTz])
                    P.phase_end()

                Ut = ph.enter_context(SBT("Ut", [128, GH, NK], BF16))
                BUt = Buf("Ut")
                with ExitStack() as pu:
                    sel = pu.enter_context(SBT("sel", [128, 64, 128], BF16))
                    Bsel = Buf("sel")
                    P.dma("pool", sel[:], c_sel, w=[Bsel])
                    utp = sb_pool(pu, 2, "utile", [128, T], BF16)
                    pst = Rot(psum_pool(pu, 4, "pu"))

                    def load_ut(t8):
                        ut, But = utp.next()
                        ft = (g0 // 8) + t8
                        P.dma("sp", ut[:], uT[ft * 128:(ft + 1) * 128, :], w=[But])
                        return ut, But
                    nxt = load_ut(0)
                    for t8 in range(GH // 8):
                        ut, But = nxt
                        if t8 + 1 < GH // 8:
                            nxt = load_ut(t8 + 1)
                        for gl in range(8):
                            g = t8 * 8 + gl
                            ps, psB = pst.next()
                            for j in range(8):
                                P.mm(ps[:, 0:NK], sel[:, gl * 8 + j, :], bcast_ap(ut, j, [(8, NK)], T), j == 0, j == 7,
                                     r=[Bsel, But], w=[psB])
                            P.copy(rr_eng(("act", "dve")), Ut[:, g, :], ps[:, 0:NK], r=[psB], w=[BUt])
                    P.phase_end()
                hist = ph.enter_context(SBT("hist", [128, GH, 2, NKO], BF16))
                Bhist = Buf("hist")

                with ExitStack() as pr:
                    KB = 16
                    gpb = 512 // (2 * KB)
                    nbk = GH * 2 * KB // 512
                    pssets = Rot([([pr.enter_context(PST("psS%d_%d" % (a_, b_), [128, 512], F32)) for b_ in range(nbk)], Buf("psS", True))
                                  for a_ in range(2)])
                    Ssb = {d: [pr.enter_context(SBT("Ssb%d_%d" % (d, s_), [128, GH, 2, KB], F32)) for s_ in range(2)] for d in (0, 1)}
                    BSsb = {d: [Buf("Ssb"), Buf("Ssb")] for d in (0, 1)}
                    st = pr.enter_context(SBT("st", [128, 2, GH, 2], F32))
                    w1 = pr.enter_context(SBT("w1r", [128, GH, 2], F32))
                    w2 = pr.enter_context(SBT("w2r", [128, GH, 2], F32))
                    QG = GH // 4
                    Bst = {(d, pp, h_): Buf("st") for d in (0, 1) for pp in (0, 1) for h_ in range(4)}
                    Bw1 = {(d, h_): Buf("w1") for d in (0, 1) for h_ in range(4)}
                    Bw2 = {(d, h_): Buf("w2") for d in (0, 1) for h_ in range(4)}
                    Bhd = {0: Buf("hist0"), 1: Buf("hist1")}
                    P.memset("dve", st[:], 0.0, w=list(Bst.values()))

                    def build_steps(d):
                        lo, hi = d * 64, (d + 1) * 64
                        if d == 1:
                            kblocks = [(kb, min(KB, NK - kb)) for kb in range(0, NK, KB)][::-1]
                        else:
                            kblocks = [(kb, min(KB, NKO - kb)) for kb in range(0, NKO, KB)]

                        def mms(bi):
                            kb, kn = kblocks[bi]
                            pset, pB = pssets.next()
                            for g in range(GH):
                                for ri in range(2):
                                    off = (g % gpb) * 2 * KB + ri * KB
                                    P.mm(pset[g // gpb][:, off:off + kn], Ws[:, g, ri, :], Ut[:, g, kb:kb + kn], True, True,
                                         r=[BWs, BUt], w=[pB])
                            S_, BS_ = Ssb[d][bi % 2], BSsb[d][bi % 2]
                            for j_ in range(nbk):
                                P.copy("act", bcast_ap(S_, j_ * 512, [(1, 512)], GH * 2 * KB, 64, lo), pset[j_][lo:hi, 0:512],
                                       r=[pB], w=[BS_])
                        steps = []
                        cur = 0
                        for bi, (kb, kn) in enumerate(kblocks):
                            S_, BS_ = Ssb[d][bi % 2], BSsb[d][bi % 2]
                            ks = list(range(kb, kb + kn))
                            if d == 1:
                                ks = ks[::-1]
                            for ki, k in enumerate(ks):
                                kk = k - kb
                                fns = []
                                if ki == 0:
                                    if bi == 0:
                                        fns.append(lambda: mms(0))
                                    if bi + 1 < len(kblocks):
                                        fns.append(lambda bi=bi: mms(bi + 1))
                                if k >= NKO:
                                    parts = [(q_ * QG, (q_ + 1) * QG, (q_,)) for q_ in range(4)]
                                elif d == 1:
                                    parts = [(0, 2 * QG, (0, 1)), (2 * QG, GH, (2, 3))]
                                else:
                                    parts = [(0, GH, (0, 1, 2, 3))]
                                c_, n_ = cur, 1 - cur
                                if k < NKO:
                                    fns.append(lambda k=k, c_=c_: P.copy(
                                        "pool", bcast_ap(hist, k, [(2 * NKO, GH), (NKO, 2)], GH * 2 * NKO, 64, lo),
                                        st[lo:hi, c_, :, :], r=[Bst[(d, c_, q_)] for q_ in range(4)], w=[Bhd[d]]))

                                def bl(dct, key, h_):
                                    return [dct[key + (q_,)] for q_ in h_]
                                for (gs, ge, h_) in parts:
                                    fns.append(lambda gs=gs, ge=ge, h_=h_, c_=c_: P.tt(
                                        "dve", w1[lo:hi, gs:ge, :], st[lo:hi, c_, gs:ge, :], Ac[lo:hi, 0, gs:ge, :], ALU.mult,
                                        r=bl(Bst, (d, c_), h_) + [BAc], w=bl(Bw1, (d,), h_)))
                                for (gs, ge, h_) in parts:
                                    fns.append(lambda gs=gs, ge=ge, h_=h_, c_=c_: P.tt(
                                        "dve", w2[lo:hi, gs:ge, :],
                                        bcast_ap(st, c_ * GH * 2 + gs * 2 + 1, [(2, ge - gs), (-1, 2)], 2 * GH * 2, 64, lo),
                                        Ac[lo:hi, 1, gs:ge, :], ALU.mult,
                                        r=bl(Bst, (d, c_), h_) + [BAc], w=bl(Bw2, (d,), h_)))
                                for (gs, ge, h_) in parts:
                                    fns.append(lambda gs=gs, ge=ge, h_=h_: P.tt(
                                        "dve", w1[lo:hi, gs:ge, :], w1[lo:hi, gs:ge, :], w2[lo:hi, gs:ge, :], ALU.add,
                                        r=bl(Bw1, (d,), h_) + bl(Bw2, (d,), h_), w=bl(Bw1, (d,), h_)))
                                for (gs, ge, h_) in parts:
                                    fns.append(lambda gs=gs, ge=ge, h_=h_, n_=n_, kk=kk, S_=S_, BS_=BS_: P.tt(
                                        "dve", st[lo:hi, n_, gs:ge, :], w1[lo:hi, gs:ge, :],
                                        bcast_ap(S_, gs * 2 * KB + kk, [(2 * KB, ge - gs), (KB, 2)], GH * 2 * KB, 64, lo), ALU.add,
                                        r=bl(Bw1, (d,), h_) + [BS_], w=bl(Bst, (d, n_), h_)))
                                steps.append((k, fns))
                                cur = n_
                        return steps

                    sa = build_steps(1)
                    sb = build_steps(0)
                    pre = [s_ for s_ in sa if s_[0] >= NKO]
                    pa = [s_ for s_ in sa if s_[0] < NKO]
                    for (_, fns) in pre:
                        for fn in fns:
                            fn()
                    assert len(pa) == len(sb) == NKO
                    for i in range(NKO):
                        fa, fb = pa[i][1], sb[i][1]
                        for j in range(max(len(fa), len(fb))):
                            if j < len(fa):
                                fa[j]()
                            if j < len(fb):
                                fb[j]()
                    P.phase_end()

                with ExitStack() as po_:
                    selT = po_.enter_context(SBT("selT", [128, 64, 128], BF16))
                    BselT = Buf("selT")
                    P.dma("pool", selT[:], c_selT, w=[BselT])
                    Zs = po_.enter_context(SBT("Zs", [128, GH, NKO], BF16))
                    BZs = Buf("Zs")
                    psy = Rot(psum_pool(po_, 3, "py"))
                    for g in range(GH):
                        ps, psB = psy.next()
                        P.mm(ps[:, 0:NKO], Tz[:, g, :], Ut[:, g, 0:NKO], True, False, r=[BTz, BUt], w=[psB])
                        P.mm(ps[:, 0:NKO], Et[:, g, 0, :], hist[:, g, 0, :], False, False, r=[BEt, Bhist], w=[psB])
                        P.mm(ps[:, 0:NKO], Et[:, g, 1, :], hist[:, g, 1, :], False, True, r=[BEt, Bhist], w=[psB])
                        P.act(Zs[:, g, :], ps[:, 0:NKO], AF.Gelu_apprx_tanh, r=[psB], w=[BZs])
                    psz = Rot(psum_pool(po_, 3, "pz"))
                    zst = sb_pool(po_, 2, "zst", [128, TE], BF16)
                    for t8 in range(GH // 8):
                        zs_, Bzs_ = zst.next()
                        for (b0, bn) in blocks_ext:
                            ps, psB = psz.next()
                            k0, kn = b0 // 8, bn // 8
                            for j in range(8):
                                for gl in range(8):
                                    g = t8 * 8 + gl
                                    P.mm(bass.AP(ps, j, [[512, 128], [8, kn]]), selT[:, gl * 8 + j, :], Zs[:, g, k0:k0 + kn],
                                         gl == 0, gl == 7, r=[BselT, BZs], w=[psB])
                            P.copy(rr_eng(("act", "dve")), zs_[:, b0:b0 + bn], ps[:, 0:bn], r=[psB], w=[Bzs_])
                        ft = (g0 // 8) + t8
                        P.dma("sp", zT[ft * 128:(ft + 1) * 128, :], zs_[:], r=[Bzs_])
                    P.phase_end()

        if cfg.get('stop') == 'C':
            return nc
        with ExitStack() as ph:
            ones, Bo, epsT, Be = make_consts(ph)
            gn = ph.enter_context(SBT("gn", [128, 5, DC], F32))
            Bg = Buf("gn")
            P.dma("sp", gn[:], gains, w=[Bg])
            mg = ph.enter_context(SBT("mg", [128, DC, TE], BF16))
            Bmg = Buf("mg")
            with ExitStack() as p0:
                ybT = p0.enter_context(SBT("ybT", [128, SC, TE], BF16))
                Byb = Buf("ybT")
                with ExitStack() as p1:
                    zTs = p1.enter_context(SBT("zTs", [128, SC, TE], BF16))
                    Bz = Buf("zTs")
                    P.dma("sp", zTs[:], zT.rearrange("(c p) t -> p c t", p=128), w=[Bz])
                    bg = p1.enter_context(SBT("bg", [128, SC], F32))
                    Bbg = Buf("bg")
                    P.dma("sp", bg[:], b_glu, w=[Bbg])
                    wpool = sb_pool(p1, 3, "wgl", [128, SC, 128], BF16)
                    pspool = Rot(psum_pool(p1, 4, "pg"))
                    sgp = sb_pool(p1, 2, "sgl", [128, 512], F32)

                    def ep_glu(m, sub, b0, bn, ps, psB):
                        sg, Bsg = sgp.next()
                        P.act(sg[:, 0:bn], ps[:, 0:bn], AF.Sigmoid, r=[psB, Bbg], w=[Bsg], bias=bg[:, m:m + 1])
                        P.tt("dve", ybT[:, m, b0:b0 + bn], sg[:, 0:bn], zTs[:, m, b0:b0 + bn], ALU.mult, r=[Bsg, Bz], w=[Byb])

                    linear(w_glu, range(SC), SC, 128, lambda c, b0, bn: zTs[:, c, b0:b0 + bn], [Bz], lambda m: blocks_ext,
                           ep_glu, wpool, pspool)
                    P.phase_end()

                with ExitStack() as p2:
                    yaS = p2.enter_context(SBT("yaS", [128, AC, TE], BF16))
                    Bya = Buf("yaS")
                    P.dma("sp", yaS[:], yaT.rearrange("(c p) t -> p c t", p=128), w=[Bya])
                    wa = sb_pool(p2, 2, "wa", [128, AC, 128], BF16)
                    wb = sb_pool(p2, 2, "wb", [128, SC, 128], BF16)
                    sgp = sb_pool(p2, 2, "sgab", [128, 2, TE], BF16)
                    pspool = Rot(psum_pool(p2, 6, "pm"))
                    t1p = sb_pool(p2, 2, "m1", [128, 512], F32)
                    t2p = sb_pool(p2, 2, "m2", [128, 512], F32)

                    def load_m(m):
                        wat, BwA = wa.next()
                        wbt, BwB = wb.next()
                        sg, Bsg = sgp.next()
                        P.dma("pool", wat[:], w_ba[m], w=[BwA])
                        P.dma("pool", wbt[:], w_bs[m], w=[BwB])
                        P.dma("sp", sg[:, 0, :], sgT[m * 128:(m + 1) * 128, :], w=[Bsg])
                        P.dma("sp", sg[:, 1, :], sgT[D + m * 128:D + (m + 1) * 128, :], w=[Bsg])
                        return (wat, BwA, wbt, BwB, sg, Bsg)

                    nxt = load_m(0)
                    for m in range(DC):
                        wat, BwA, wbt, BwB, sg, Bsg = nxt
                        if m + 1 < DC:
                            nxt = load_m(m + 1)
                        for (b0, bn) in blocks_ext:
                            psA, BA = pspool.next()
                            psB_, BB = pspool.next()
                            for c in range(AC):
                                P.mm(psA[:, 0:bn], wat[:, c, :], yaS[:, c, b0:b0 + bn], c == 0, c == AC - 1, r=[BwA, Bya], w=[BA])
                            for c in range(SC):
                                P.mm(psB_[:, 0:bn], wbt[:, c, :], ybT[:, c, b0:b0 + bn], c == 0, c == SC - 1, r=[BwB, Byb], w=[BB])
                            t1, B1 = t1p.next()
                            t2, B2 = t2p.next()
                            P.tt("dve", t1[:, 0:bn], psA[:, 0:bn], sg[:, 0, b0:b0 + bn], ALU.mult, r=[BA, Bsg], w=[B1])
                            P.tt("dve", t2[:, 0:bn], psB_[:, 0:bn], sg[:, 1, b0:b0 + bn], ALU.mult, r=[BB, Bsg], w=[B2])
                            P.tt("pool", mg[:, m, b0:b0 + bn], t1[:, 0:bn], t2[:, 0:bn], ALU.add, r=[B1, B2], w=[Bmg])
                    P.phase_end()

            with ExitStack() as ptmp:
                ssq = proj_ssq(ph, ptmp, w_out, DC, lambda c, b0, bn: mg[:, c, b0:b0 + bn], [Bmg], blocks_ext, oT, ones, Bo, "o")
                P.phase_end()
            rstd1, Br1 = rstd_from(ph, ssq, blocks_ext, TE, epsT, Be, "rstd1")
            ssq2 = ssq_tiles(ph, len(blocks_ext), "h")
            with ExitStack() as p3:
                op_ = sb_pool(p3, 2, "o_in", [128, TE], F32)
                xp_ = sb_pool(p3, 2, "x_in", [128, TE], F32)
                sqp = sb_pool(p3, 3, "hsq", [128, 512], BF16)
                hb_ = sb_pool(p3, 2, "h_bf", [128, TE], BF16)
                pending = []

                def load_r(m):
                    o, Bo_ = op_.next()
                    x, Bx_ = xp_.next()
                    P.dma("sp", o[:], oT[m * 128:(m + 1) * 128, :], w=[Bo_])
                    P.dma("sp", x[:], xT[m * 128:(m + 1) * 128, 0:TE], w=[Bx_])
                    return (o, Bo_, x, Bx_)

                nxt = load_r(0)
                for m in range(DC):
                    o, Bo_, x, Bx_ = nxt
                    if m + 1 < DC:
                        nxt = load_r(m + 1)
                    P.stt("dve", o[:], o[:], gn[:, 1, m:m + 1], rstd1[:], ALU.mult, ALU.mult, r=[Bo_, Bg, Br1], w=[Bo_])
                    P.tt("pool", o[:], o[:], x[:], ALU.add, r=[Bo_, Bx_], w=[Bo_])
                    P.dma("sp", h1T[m * 128:(m + 1) * 128, :], o[:], r=[Bo_])
                    hb, Bhb = hb_.next()
                    P.act(hb[:], o[:], AF.Copy, r=[Bo_, Bg], w=[Bhb], scale=gn[:, 2, m:m + 1])
                    P.dma("sp", hn2T[m * 128:(m + 1) * 128, :], hb[:], r=[Bhb])
                    for bi, (b0, bn) in enumerate(blocks_ext):
                        sq, Bsq = sqp.next()
                        P.act(sq[:, 0:bn], o[:, b0:b0 + bn], AF.Square, r=[Bo_], w=[Bsq])
                        while len(pending) > 2:
                            pending.pop(0)()
                        sp_, spB = ssq2[bi]
                        pending.append(lambda sp_=sp_, spB=spB, sq=sq, Bsq=Bsq, bn=bn, m=m:
                                       P.mm(sp_[:, 0:bn], ones[:], sq[:, 0:bn], m == 0, m == DC - 1, r=[Bo, Bsq], w=[spB]))
                while pending:
                    pending.pop(0)()
                P.phase_end()
            rstd2, Br2 = rstd_from(ph, ssq2, blocks_ext, TE, epsT, Be, "rstd2")
            P.dma("sp", rs2T, rstd2[:], r=[Br2])
            P.phase_end()

        if cfg.get('stop') == 'D':
            return nc
        with ExitStack() as ph:
            hn2 = ph.enter_context(SBT("hn2", [128, DC, TE], BF16))
            Bh2 = Buf("hn2")
            P.dma("sp", hn2[:], hn2T.rearrange("(c p) t -> p c t", p=128), w=[Bh2])
            cw = ph.enter_context(SBT("cw", [128, 3, FC], F32))
            cb = ph.enter_context(SBT("cb", [128, FC], F32))
            Bcw = Buf("cw")
            P.dma("sp", cw[:], conv_w, w=[Bcw])
            P.dma("sp", cb[:], conv_b, w=[Bcw])
            rs2 = ph.enter_context(SBT("rs2", [128, TE], F32))
            Brs2 = Buf("rs2")
            P.dma("sp", rs2[:], rs2T, w=[Brs2])
            wpool = sb_pool(ph, 4, "wup", [128, DC, 128], BF16)
            pspool = Rot(psum_pool(ph, 8, "pu"))
            Ap = sb_pool(ph, 2, "Asb", [128, TE + 2], F32)
            for (a_, Ba_) in Ap.items:
                P.memset("pool", a_[:], 0.0, w=[Ba_])
            cp_ = sb_pool(ph, 2, "cv", [128, TO], F32)
            gp_ = sb_pool(ph, 2, "gl", [128, TO], F32)
            op_ = sb_pool(ph, 2, "actst", [128, TO], BF16)

            def load_w(j):
                wa_, BwA = wpool.next()
                wb_, BwB = wpool.next()
                P.dma("pool", wa_[:], w_up[j], w=[BwA])
                P.dma("pool", wb_[:], w_up[FC + j], w=[BwB])
                return (wa_, BwA, wb_, BwB)

            nxt = load_w(0)
            for j in range(FC):
                wa_, BwA, wb_, BwB = nxt
                if j + 1 < FC:
                    nxt = load_w(j + 1)
                A, BA = Ap.next()
                for (b0, bn) in blocks_ext:
                    ps, psB = pspool.next()
                    for c in range(DC):
                        P.mm(ps[:, 0:bn], wa_[:, c, :], hn2[:, c, b0:b0 + bn], c == 0, c == DC - 1, r=[BwA, Bh2], w=[psB])
                    P.tt("dve", A[:, 1 + b0:1 + b0 + bn], ps[:, 0:bn], rs2[:, b0:b0 + bn], ALU.mult, r=[psB, Brs2], w=[BA])
                bps = []
                for (b0, bn) in blocks_own:
                    ps, psB = pspool.next()
                    for c in range(DC):
                        P.mm(ps[:, 0:bn], wb_[:, c, :], hn2[:, c, b0:b0 + bn], c == 0, c == DC - 1, r=[BwB, Bh2], w=[psB])
                    bps.append((ps, psB, b0, bn))
                cv, Bcv = cp_.next()
                gl, Bgl = gp_.next()
                ot, Bot = op_.next()
                P.ts("dve", cv[:], A[:, 0:TO], cw[:, 0, j:j + 1], None, ALU.mult, r=[BA, Bcw], w=[Bcv])
                P.stt("dve", cv[:], A[:, 1:TO + 1], cw[:, 1, j:j + 1], cv[:], ALU.mult, ALU.add, r=[BA, Bcw, Bcv], w=[Bcv])
                P.stt("dve", cv[:], A[:, 2:TO + 2], cw[:, 2, j:j + 1], cv[:], ALU.mult, ALU.add, r=[BA, Bcw, Bcv], w=[Bcv])
                P.act(gl[:], cv[:], AF.Gelu_apprx_tanh, r=[Bcv, Bcw], w=[Bgl], bias=cb[:, j:j + 1])
                P.tt("pool", gl[:], gl[:], rs2[:, 0:TO], ALU.mult, r=[Bgl, Brs2], w=[Bgl])
                for (ps, psB, b0, bn) in bps:
                    P.tt("dve", ot[:, b0:b0 + bn], ps[:, 0:bn], gl[:, b0:b0 + bn], ALU.mult, r=[psB, Bgl], w=[Bot])
                P.dma("sp", actT[j * 128:(j + 1) * 128, :], ot[:], r=[Bot])
            P.phase_end()

        if cfg.get('stop') == 'E':
            return nc
        with ExitStack() as ph:
            ones, Bo, epsT, Be = make_consts(ph)
            gn = ph.enter_context(SBT("gn", [128, 5, DC], F32))
            Bg = Buf("gn")
            P.dma("sp", gn[:], gains, w=[Bg])
            NBK = len(blocks_own)
            BW = max(bn for _, bn in blocks_own)
            ssqF = [(ph.enter_context(PST("ssqF%d" % i, [128, 512], F32)), Buf("ssqF", True)) for i in range(NBK)]
            rsF = [(ph.enter_context(SBT("rsF%d" % i, [128, BW], F32)), Buf("rsF")) for i in range(NBK)]
            aS = ph.enter_context(SBT("aS", [128, FC, BW], BF16))
            BaS = Buf("aS")
            wpool = sb_pool(ph, 3, "wdn", [128, FC, 128], BF16)
            pspool = Rot(psum_pool(ph, 4, "pf"))
            ost = sb_pool(ph, 2, "ostF", [128, BW], F32)
            sqp = sb_pool(ph, 3, "osqF", [128, 512], BF16)
            dp_ = sb_pool(ph, 3, "d_in", [128, BW], F32)
            hp_ = sb_pool(ph, 3, "h_in", [128, BW], F32)
            BdT = {}
            actTv = actT.rearrange("(c p) t -> p c t", p=128)

            def f2_load(bi, m):
                b0, bn = blocks_own[bi]
                d_, Bd_ = dp_.next()
                h_, Bh_ = hp_.next()
                P.dma("sp", d_[:, 0:bn], dT[m * 128:(m + 1) * 128, b0:b0 + bn], r=[BdT[(bi, m)]], w=[Bd_])
                P.dma("sp", h_[:, 0:bn], h1T[m * 128:(m + 1) * 128, b0:b0 + bn], w=[Bh_])
                return (d_, Bd_, h_, Bh_)

            def f2_fin(bi, m, ld):
                b0, bn = blocks_own[bi]
                d_, Bd_, h_, Bh_ = ld
                rs, Brs = rsF[bi]
                P.stt("dve", d_[:, 0:bn], d_[:, 0:bn], gn[:, 3, m:m + 1], rs[:, 0:bn], ALU.mult, ALU.mult, r=[Bd_, Bg, Brs], w=[Bd_])
                P.tt("pool", d_[:, 0:bn], d_[:, 0:bn], h_[:, 0:bn], ALU.add, r=[Bd_, Bh_], w=[Bd_])
                P.dma("sp", h2T[m * 128:(m + 1) * 128, b0:b0 + bn], d_[:, 0:bn], r=[Bd_])

            for bi, (b0, bn) in enumerate(blocks_own):
                P.dma("sp", aS[:, :, 0:bn], actTv[:, :, b0:b0 + bn], w=[BaS])
                pending = []
                state = {}
                sp, spB = ssqF[bi]

                def ep(m, sub, b0_, bn_, ps, psB, sp=sp, spB=spB):
                    o, Bos = ost.next()
                    state["o"] = (o, Bos)
                    P.copy("dve", o[:, 0:bn_], ps[:, 0:bn_], r=[psB], w=[Bos])
                    sq, Bsq = sqp.next()
                    P.act(sq[:, 0:bn_], ps[:, 0:bn_], AF.Square, r=[psB], w=[Bsq])
                    while pending:
                        pending.pop(0)()
                    pending.append(lambda: P.mm(sp[:, 0:bn_], ones[:], sq[:, 0:bn_], m == 0, m == DC - 1, r=[Bo, Bsq], w=[spB]))

                def post(m, bi=bi, b0=b0, bn=bn):
                    o, Bos = state["o"]
                    BdT[(bi, m)] = Buf("dT")
                    P.dma("sp", dT[m * 128:(m + 1) * 128, b0:b0 + bn], o[:, 0:bn], r=[Bos], w=[BdT[(bi, m)]])
                    if bi > 0:
                        if "ld" in state:
                            f2_fin(bi - 1, m - 1, state.pop("ld"))
                        state["ld"] = f2_load(bi - 1, m)

                linear(w_down, range(DC), FC, 128, lambda c, b0_, bn_: aS[:, c, 0:bn_], [BaS], lambda m, b0=b0, bn=bn: [(b0, bn)],
                       ep, wpool, pspool, post)
                while pending:
                    pending.pop(0)()
                if "ld" in state:
                    f2_fin(bi - 1, DC - 1, state.pop("ld"))
                rs, Brs = rsF[bi]
                P.act(rs[:, 0:bn], sp[:, 0:bn], AF.Sqrt, r=[spB, Be], w=[Brs], bias=epsT[:, 0:1], scale=1.0 / D)
                P.recip(rs[:, 0:bn], rs[:, 0:bn], r=[Brs], w=[Brs])
            last = NBK - 1
            ld = f2_load(last, 0)
            for m in range(DC):
                nx = f2_load(last, m + 1) if m + 1 < DC else None
                f2_fin(last, m, ld)
                ld = nx
            P.phase_end()

        if cfg.get('stop') == 'F':
            return nc
        with ExitStack() as ph:
            ones, Bo, epsT, Be = make_consts(ph)
            gn = ph.enter_context(SBT("gn", [128, 5, DC], F32))
            Bg = Buf("gn")
            P.dma("sp", gn[:], gains, w=[Bg])
            h2b = ph.enter_context(SBT("h2b", [128, DC, TO], BF16))
            Bh2b = Buf("h2b")
            P.dma("pool", h2b[:], h2T.rearrange("(c p) t -> p c t", p=128), w=[Bh2b])
            pTb = ph.enter_context(SBT("pTb", [128, PC, TO], BF16))
            BpT = Buf("pTb")
            P.dma("pool", pTb[:], pT.rearrange("(c p) t -> p c t", p=128), w=[BpT])
            wpl = ph.enter_context(SBT("wpl", [128, DC, PC, 128], BF16))
            Bwpl = Buf("wpl")
            P.dma("pool", wpl[:], w_ple.rearrange("m p c j -> p m c j"), w=[Bwpl])
            ssq = ssq_tiles(ph, len(blocks_own), "p")
            pspool = Rot(psum_pool(ph, 5, "pp"))
            sqp = sb_pool(ph, 3, "psq", [128, 512], BF16)
            pending = []
            for m in range(DC):
                for bi, (b0, bn) in enumerate(blocks_own):
                    ps, psB = pspool.next()
                    for c in range(PC):
                        P.mm(ps[:, 0:bn], wpl[:, m, c, :], pTb[:, c, b0:b0 + bn], c == 0, c == PC - 1, r=[Bwpl, BpT], w=[psB])
                    sq, Bsq = sqp.next()
                    P.act(sq[:, 0:bn], ps[:, 0:bn], AF.Square, r=[psB], w=[Bsq])
                    while len(pending) > 1:
                        pending.pop(0)()
                    sp_, spB = ssq[bi]
                    pending.append(lambda sp_=sp_, spB=spB, sq=sq, Bsq=Bsq, bn=bn, m=m:
                                   P.mm(sp_[:, 0:bn], ones[:], sq[:, 0:bn], m == 0, m == DC - 1, r=[Bo, Bsq], w=[spB]))
            while pending:
                pending.pop(0)()
            rstdp, Brp = rstd_from(ph, ssq, blocks_own, TO, epsT, Be, "rstdp")
            wpool = sb_pool(ph, 3, "wpg", [128, DC, 128], BF16)
            hp_ = sb_pool(ph, 2, "h2in", [128, TO], F32)
            sgp = sb_pool(ph, 2, "sgp", [128, 512], F32)
            tp_ = sb_pool(ph, 2, "tpl", [128, 512], F32)
            state = {}

            def ep_pg(m, sub, b0, bn, ps, psB):
                if b0 == 0:
                    h_, Bh_ = hp_.next()
                    P.dma("sp", h_[:], h2T[m * 128:(m + 1) * 128, :], w=[Bh_])
                    state["h"] = (h_, Bh_)
                h_, Bh_ = state["h"]
                ps2, ps2B = pspool.next()
                for c in range(PC):
                    P.mm(ps2[:, 0:bn], wpl[:, m, c, :], pTb[:, c, b0:b0 + bn], c == 0, c == PC - 1, r=[Bwpl, BpT], w=[ps2B])
                sg, Bsg = sgp.next()
                tp, Btp = tp_.next()
                P.act(sg[:, 0:bn], ps[:, 0:bn], AF.Sigmoid, r=[psB], w=[Bsg])
                P.tt("dve", tp[:, 0:bn], ps2[:, 0:bn], rstdp[:, b0:b0 + bn], ALU.mult, r=[ps2B, Brp], w=[Btp])
                P.tt("pool", tp[:, 0:bn], tp[:, 0:bn], sg[:, 0:bn], ALU.mult, r=[Btp, Bsg], w=[Btp])
                P.stt("dve", h_[:, b0:b0 + bn], tp[:, 0:bn], gn[:, 4, m:m + 1], h_[:, b0:b0 + bn], ALU.mult, ALU.add,
                      r=[Btp, Bg, Bh_], w=[Bh_])

            def post_pg(m):
                h_, Bh_ = state["h"]
                P.dma("sp", outT[m * 128:(m + 1) * 128, :], h_[:], r=[Bh_])

            linear(w_pg, range(DC), DC, 128, lambda c, b0, bn: h2b[:, c, b0:b0 + bn], [Bh2b], lambda m: blocks_own,
                   ep_pg, wpool, pspool, post_pg)
            P.phase_end()
    return nc


def tile_w(W, mw=128):
    K, N = W.shape
    return np.ascontiguousarray(W.reshape(K // 128, 128, N // mw, mw).transpose(2, 1, 0, 3))


def pvec(v):
    return np.ascontiguousarray(np.asarray(v, np.float32).reshape(-1, 128).T)


def make_constants(cfg):
    T = cfg["T"]
    TO = T // 2
    TE = TO + HALO
    c = {}
    j = np.arange(8, dtype=np.float32)
    n1 = np.zeros((128, 8), np.float32)
    n1[:64] = 7 - j
    n1[64:] = j
    nt = np.zeros((128, 26), np.float32)
    nt[:, 0:8] = n1
    nt[:, 8:16] = -n1
    nt[:, 16:24] = 8 - n1
    nt[:, 24] = 1
    nt[:, 25] = 8
    c["c_ntab"] = nt
    jj = np.arange(128) // 16
    mk = np.zeros((128, 2, 128), np.float32)
    mk[:, 0, :] = (jj[:, None] <= jj[None, :])
    mk[:, 1, :] = (jj[:, None] >= jj[None, :])
    c["c_mask"] = mk
    c["c_ident"] = np.eye(128, dtype=np.float32)
    pm = np.zeros((128, 128), np.float32)
    pm[(np.arange(128) + 64) % 128, np.arange(128)] = 1
    c["c_perm"] = pm
    sel = np.zeros((128, 64, 128), np.float32)
    selT = np.zeros((128, 64, 128), np.float32)
    for gl in range(8):
        for jx in range(8):
            for cc in range(16):
                sel[16 * gl + cc, gl * 8 + jx, 16 * jx + cc] = 1
                selT[16 * jx + cc, gl * 8 + jx, 16 * gl + cc] = 1
    c["c_sel"] = sel
    c["c_selT"] = selT
    X0 = T - 128
    MW = T - 128 + TE
    delta = np.arange(128)[:, None] - (np.arange(MW)[None, :] - X0)
    w = np.zeros_like(delta, dtype=np.float32)
    for (win, dil) in cfg["patterns"]:
        ns = win // (2 * dil)
        w += ((delta % dil == 0) & (np.abs(delta) <= ns * dil)).astype(np.float32)
    c["c_mtab"] = w
    return c


def prep_core_inputs(cfg, inp, b, half, shared):
    D, T, H, G = cfg["D"], cfg["T"], cfg["H"], cfg["G"]
    TO = T // 2
    idx = np.arange(T) if half == 0 else np.arange(T)[::-1]
    m = dict(shared)
    m["xT"] = np.ascontiguousarray(inp["x"][b][idx].T)
    m["xTt"] = np.ascontiguousarray(m["xT"].reshape(D // 128, 128, T // 128, 128).transpose(2, 1, 0, 3))
    m["pT"] = np.ascontiguousarray(inp["p"][0, b][idx[:TO]].T)
    inv_freq = (10000.0 ** (-np.arange(0, 128, 2, dtype=np.float32) / 128)).astype(np.float32)
    ang = idx.astype(np.float32)[:, None] * inv_freq[None, :]
    ang = np.concatenate([ang, ang], axis=-1)
    sgn = np.concatenate([-np.ones(64, np.float32), np.ones(64, np.float32)])
    m["cosT"] = np.ascontiguousarray(np.cos(ang).T.astype(np.float32))
    m["sinT"] = np.ascontiguousarray((np.sin(ang) * sgn[None, :]).T.astype(np.float32))
    dd = [0, 1] if half == 0 else [1, 0]
    lam = np.zeros((128, 3, G), np.float32)
    sB = np.zeros((128, 2, G, 16), np.float32)
    sC = np.zeros((128, 2, G, 16), np.float32)
    for d in range(2):
        s_ = dd[d]
        lam[d * 64:(d + 1) * 64, 0] = inp["ssm_lambda_re"][0, s_].T
        lam[d * 64:(d + 1) * 64, 1] = inp["ssm_lambda_im"][0, s_].T
        lam[d * 64:(d + 1) * 64, 2] = inp["ssm_log_dt"][0, s_][None, :]
        sB[d * 64:(d + 1) * 64, 0] = inp["ssm_b_re"][0, s_].transpose(1, 0, 2)
        sB[d * 64:(d + 1) * 64, 1] = inp["ssm_b_im"][0, s_].transpose(1, 0, 2)
        sC[d * 64:(d + 1) * 64, 0] = inp["ssm_c_re"][0, s_].transpose(2, 0, 1)
        sC[d * 64:(d + 1) * 64, 1] = inp["ssm_c_im"][0, s_].transpose(2, 0, 1)
    m["s_lam"], m["s_B"], m["s_C"] = lam, sB, sC
    cwv = inp["conv_w"][0]
    if half == 1:
        cwv = cwv[::-1]
    m["conv_w"] = np.ascontiguousarray(np.stack([pvec(cwv[t]) for t in range(3)], axis=1))
    return m


def prep_shared(cfg, inp):
    D, T, H, G, DFF = cfg["D"], cfg["T"], cfg["H"], cfg["G"], cfg["DFF"]
    AW, SW = H * 128, G * 16
    sh = dict(make_constants(cfg))
    w_in = inp["w_in"][0]
    wq, wk, wv = w_in[:, 0:AW], w_in[:, AW:2 * AW], w_in[:, 2 * AW:3 * AW]
    wu = w_in[:, 3 * AW:3 * AW + SW]
    wg = w_in[:, 3 * AW + SW:]
    sh["w_qk"] = tile_w(np.concatenate([wq, wk], axis=1))
    sh["w_v"] = tile_w(wv, min(AW, 512))
    sh["w_u"] = tile_w(wu)
    sh["w_g"] = tile_w(wg)
    sh["w_ba"] = tile_w(inp["w_branch_attn"][0])
    sh["w_bs"] = tile_w(inp["w_branch_ssm"][0])
    sh["w_out"] = tile_w(inp["w_out"][0])
    sh["w_glu"] = tile_w(inp["w_glu"][0])
    sh["b_glu"] = pvec(inp["b_glu"][0])
    sh["w_up"] = tile_w(inp["w_up"][0])
    sh["conv_b"] = pvec(inp["conv_b"][0])
    sh["w_down"] = tile_w(inp["w_down"][0])
    sh["w_ple"] = tile_w(inp["w_ple"][0])
    sh["w_pg"] = tile_w(inp["w_ple_gate"][0])
    sh["gains"] = np.ascontiguousarray(np.stack(
        [pvec(inp[k][0]) for k in ("g_mix_pre", "g_mix_post", "g_ffn_pre", "g_ffn_post", "g_ple")], axis=1))
    sh["s_D"] = np.ascontiguousarray(np.tile(inp["ssm_d"][0].reshape(G, 16).T, (8, 1)))
    return sh


def run_cfg(cfg, inp, dbg=False):
    inp = {k: np.asarray(v, np.float32) for k, v in inp.items()}
    nb = cfg["nb"]
    T, D = cfg["T"], cfg["D"]
    TO = T // 2
    shared = prep_shared(cfg, inp)
    in_maps = []
    for b in range(nb):
        for half in range(2):
            in_maps.append(prep_core_inputs(cfg, inp, b, half, shared))
    nc = build_program(cfg, dbg=dbg)
    res = run_bass_kernel_spmd(nc, in_maps, core_ids=list(range(2 * nb)))
    out = np.zeros((nb, T, D), np.float32)
    for b in range(nb):
        for half in range(2):
            idx = np.arange(T) if half == 0 else np.arange(T)[::-1]
            out[b, idx[:TO]] = res.results[b * 2 + half]["outT"].T
    return out, res


def kernel(**inputs):
    out, _ = run_cfg(FULL_CFG, inputs)
    return out
```

```python
import math
from contextlib import ExitStack
import numpy as np
import concourse.bass as bass
import concourse.mybir as mybir
from concourse.bass_utils import run_bass_kernel_spmd

F32 = mybir.dt.float32
BF16 = mybir.dt.bfloat16
I32 = mybir.dt.int32
ALU = mybir.AluOpType
AF = mybir.ActivationFunctionType

NDS = 40
EPS = 1e-6
HALO = 8
TWO_PI = float(2 * np.pi)

FULL_CFG = dict(D=4096, T=2048, H=16, G=128, DFF=12288, PLE=256,
                patterns=((128, 1), (512, 4), (2048, 16)), nb=4)


class Buf:
    __slots__ = ("name", "w", "r", "excl")

    def __init__(self, name="", excl=False):
        self.name = name
        self.w = None
        self.r = []
        self.excl = excl


class Sched:
    ENG = ("pe", "dve", "act", "pool", "sp")

    def __init__(self, nc, stack):
        self.nc = nc
        self.csem = {k: stack.enter_context(nc.semaphore("c_" + k)) for k in self.ENG}
        self.cnt = {k: 0 for k in self.ENG}
        self.dsem = [stack.enter_context(nc.semaphore("d%d" % i)) for i in range(NDS)]
        self.dcnt = [0] * NDS
        self.dnext = 0
        self.waited = {}
        self.ops = {k: [] for k in self.ENG}

    def _deps(self, eng, reads, writes):
        evs = []
        for b in reads:
            if b.w is not None:
                evs.append(b.w)
            if b.excl:
                evs.extend(b.r)
        for b in writes:
            if b.w is not None:
                evs.append(b.w)
            evs.extend(b.r)
        waits = {}
        for (key, sem, val, e) in evs:
            if e == eng and eng == "pe":
                continue
            if self.waited.get((eng, key), 0) >= val:
                continue
            if waits.get(key, (None, 0))[1] < val:
                waits[key] = (sem, val)
        for key, (sem, val) in waits.items():
            self.waited[(eng, key)] = val
        return list(waits.values())

    def _record(self, ev, reads, writes):
        for b in reads:
            b.r.append(ev)
            if len(b.r) > 24:
                b.r = b.r[-24:] if False else b.r
        for b in writes:
            b.w = ev
            b.r = []

    def op(self, eng, fn, reads=(), writes=()):
        waits = self._deps(eng, reads, writes)
        self.cnt[eng] += 1
        ev = ("c_" + eng, self.csem[eng], self.cnt[eng], eng)
        self._record(ev, reads, writes)
        self.ops[eng].append((waits, fn, self.csem[eng], 1))

    def dma(self, eng, out, in_, reads=(), writes=()):
        i = self.dnext
        self.dnext = (self.dnext + 1) % NDS
        waits = self._deps(eng, reads, writes)
        key = "d%d" % i
        if self.dcnt[i] > 0 and self.waited.get((eng, key), 0) < self.dcnt[i]:
            waits.append((self.dsem[i], self.dcnt[i]))
            self.waited[(eng, key)] = self.dcnt[i]
        self.dcnt[i] += 16
        ev = (key, self.dsem[i], self.dcnt[i], "dma")
        self._record(ev, reads, writes)

        def fn(e, out=out, in_=in_):
            return e.dma_start(out=out, in_=in_)

        self.ops[eng].append((waits, fn, self.dsem[i], 16))

    def barrier(self):
        for eng in self.ENG:
            waits = []
            for k in self.ENG:
                if k == eng or self.cnt[k] == 0:
                    continue
                key = "c_" + k
                if self.waited.get((eng, key), 0) < self.cnt[k]:
                    waits.append((self.csem[k], self.cnt[k]))
                    self.waited[(eng, key)] = self.cnt[k]
            for i in range(NDS):
                key = "d%d" % i
                if self.dcnt[i] > 0 and self.waited.get((eng, key), 0) < self.dcnt[i]:
                    waits.append((self.dsem[i], self.dcnt[i]))
                    self.waited[(eng, key)] = self.dcnt[i]
            if waits:
                self.ops[eng].append((waits, None, None, 0))

    def emit(self):
        nc = self.nc
        ops = self.ops
        self.ops = {k: [] for k in self.ENG}

        def run(e, lst):
            for (waits, fn, sem, inc) in lst:
                for (sm, v) in waits:
                    e.wait_ge(sm, v)
                if fn is not None:
                    fn(e).then_inc(sem, inc)

        with nc.Block() as block:
            if ops["pe"]:
                @block.tensor
                def _(e):
                    run(e, ops["pe"])
            if ops["dve"]:
                @block.vector
                def _(e):
                    run(e, ops["dve"])
            if ops["act"]:
                @block.scalar
                def _(e):
                    run(e, ops["act"])
            if ops["pool"]:
                @block.gpsimd
                def _(e):
                    run(e, ops["pool"])
            if ops["sp"]:
                @block.sync
                def _(e):
                    run(e, ops["sp"])


class Prog:
    def __init__(self, nc, s):
        self.nc = nc
        self.s = s
        self.rr = 0

    def mm(self, out, lhsT, rhs, start, stop, r=(), w=()):
        self.s.op("pe", lambda e: e.matmul(out, lhsT=lhsT, rhs=rhs, start=start, stop=stop), r, w)

    def act(self, out, in_, func, r=(), w=(), bias=None, scale=1.0):
        if bias is None:
            self.s.op("act", lambda e: e.activation(out=out, in_=in_, func=func, scale=scale), r, w)
        else:
            self.s.op("act", lambda e: e.activation(out=out, in_=in_, func=func, bias=bias, scale=scale), r, w)

    def tt(self, eng, out, in0, in1, op, r=(), w=()):
        self.s.op(eng, lambda e: e.tensor_tensor(out=out, in0=in0, in1=in1, op=op), r, w)

    def ts(self, eng, out, in0, s1, s2, op0, op1=None, r=(), w=()):
        if op1 is None:
            self.s.op(eng, lambda e: e.tensor_scalar(out=out, in0=in0, scalar1=s1, scalar2=None, op0=op0), r, w)
        else:
            self.s.op(eng, lambda e: e.tensor_scalar(out=out, in0=in0, scalar1=s1, scalar2=s2, op0=op0, op1=op1), r, w)

    def stt(self, eng, out, in0, scalar, in1, op0, op1, r=(), w=()):
        self.s.op(eng, lambda e: e.scalar_tensor_tensor(out=out, in0=in0, scalar=scalar, in1=in1, op0=op0, op1=op1), r, w)

    def copy(self, eng, out, in_, r=(), w=()):
        if eng == "act":
            self.s.op("act", lambda e: e.activation(out=out, in_=in_, func=AF.Copy), r, w)
        else:
            self.s.op(eng, lambda e: e.tensor_copy(out=out, in_=in_), r, w)

    def recip(self, out, in_, r=(), w=()):
        self.s.op("dve", lambda e: e.reciprocal(out=out, in_=in_), r, w)

    def memset(self, eng, ap, val, w=()):
        self.s.op(eng, lambda e: e.memset(ap, val), (), w)

    def dma(self, eng, out, in_, r=(), w=()):
        self.s.dma(eng, out, in_, r, w)

    def phase_end(self):
        self.s.barrier()
        self.s.emit()


def split_blocks(n, bs=512):
    out = []
    t = 0
    while t < n:
        w = min(bs, n - t)
        out.append((t, w))
        t += w
    return out


def bcast_ap(t, offset, dims, rowlen, nparts=128, pbase=0):
    return bass.AP(t, pbase * rowlen + offset, [[rowlen, nparts]] + [[a, b] for (a, b) in dims])


def build_program(cfg, dbg=False):
    D, T, H, G, DFF, PLE = cfg["D"], cfg["T"], cfg["H"], cfg["G"], cfg["DFF"], cfg["PLE"]
    TO = T // 2
    TE = TO + HALO
    DC = D // 128
    AW = H * 128
    SW = G * 16
    SC = SW // 128
    AC = AW // 128
    FC = DFF // 128
    PC = PLE // 128
    NK = T // 8
    NKO = TO // 8 + 1
    GH = min(G, 64)
    blocks_ext = split_blocks(TO) + [(TO, HALO)]
    blocks_own = split_blocks(TO)
    blocks_full = split_blocks(T)
    WMAX = max(w // 2 for (w, d) in cfg["patterns"])
    X0 = T - 128
    MW = T - 128 + TE
    att_scale = 128 ** -0.5

    nc = bass.Bass("TRN2", target_bir_lowering=False)
    uid = [0]

    def SBT(name, shape, dt):
        uid[0] += 1
        return nc.sbuf_tensor("%s_%d" % (name, uid[0]), shape, dt)

    def PST(name, shape, dt):
        uid[0] += 1
        return nc.psum_tensor("%s_%d" % (name, uid[0]), shape, dt)

    def din(name, shape, dt=F32):
        return nc.dram_tensor(name, list(shape), dt, kind="ExternalInput").ap()

    def dscr(name, shape, dt):
        return nc.dram_tensor(name, list(shape), dt, kind=("ExternalOutput" if dbg else "Internal")).ap()

    xT = din("xT", [D, T])
    xTt = din("xTt", [T // 128, 128, DC, 128])
    pT = din("pT", [PLE, TO])
    cosT = din("cosT", [128, T])
    sinT = din("sinT", [128, T])
    gains = din("gains", [128, 5, DC])
    w_qk = din("w_qk", [2 * H, 128, DC, 128])
    w_u = din("w_u", [SC, 128, DC, 128])
    w_g = din("w_g", [2 * DC, 128, DC, 128])
    w_v = din("w_v", [AW // 512 if AW >= 512 else 1, 128, DC, min(AW, 512)])
    w_ba = din("w_ba", [DC, 128, AC, 128])
    w_bs = din("w_bs", [DC, 128, SC, 128])
    w_out = din("w_out", [DC, 128, DC, 128])
    w_glu = din("w_glu", [SC, 128, SC, 128])
    b_glu = din("b_glu", [128, SC])
    w_up = din("w_up", [2 * FC, 128, DC, 128])
    conv_w = din("conv_w", [128, 3, FC])
    conv_b = din("conv_b", [128, FC])
    w_down = din("w_down", [DC, 128, FC, 128])
    w_ple = din("w_ple", [DC, 128, PC, 128])
    w_pg = din("w_pg", [DC, 128, DC, 128])
    s_lam = din("s_lam", [128, 3, G])
    s_B = din("s_B", [128, 2, G, 16])
    s_C = din("s_C", [128, 2, G, 16])
    s_D = din("s_D", [128, G])
    c_ntab = din("c_ntab", [128, 26])
    c_mask = din("c_mask", [128, 2, 128])
    c_ident = din("c_ident", [128, 128])
    c_perm = din("c_perm", [128, 128])
    c_sel = din("c_sel", [128, 64, 128])
    c_selT = din("c_selT", [128, 64, 128])
    c_mtab = din("c_mtab", [128, MW])

    outT = nc.dram_tensor("outT", [D, TO], F32, kind="ExternalOutput").ap()

    qT = dscr("qT", [AW, TE], BF16)
    kT = dscr("kT", [AW, T], BF16)
    vS = dscr("vS", [T, AW], BF16)
    uT = dscr("uT", [SW, T], BF16)
    sgT = dscr("sgT", [2 * D, TE], BF16)
    yaT = dscr("yaT", [AW, TE], BF16)
    zT = dscr("zT", [SW, TE], BF16)
    oT = dscr("oT", [D, TE], F32)
    h1T = dscr("h1T", [D, TE], F32)
    actT = dscr("actT", [DFF, TO], BF16)
    dT = dscr("dT", [D, TO], F32)
    h2T = dscr("h2T", [D, TO], F32)
    hn2T = dscr("hn2T", [D, TE], BF16)
    rs2T = dscr("rs2T", [128, TE], F32)

    with ExitStack() as top:
        s = Sched(nc, top)
        P = Prog(nc, s)

        def rr_eng(engs=("dve", "act", "pool")):
            P.rr += 1
            return engs[P.rr % len(engs)]

        def make_consts(ph):
            ones = ph.enter_context(SBT("ones", [128, 128], BF16))
            Bo = Buf("ones")
            P.memset("pool", ones[:], 1.0, w=[Bo])
            epsT = ph.enter_context(SBT("epsT", [128, 1], F32))
            Be = Buf("eps")
            P.memset("pool", epsT[:], EPS, w=[Be])
            return ones, Bo, epsT, Be

        def psum_pool(ph, n, name="ps"):
            tiles = []
            for i in range(n):
                t = ph.enter_context(PST("%s%d" % (name, i), [128, 512], F32))
                tiles.append((t, Buf("%s%d" % (name, i), True)))
            return tiles

        class Rot:
            def __init__(self, items):
                self.items = items
                self.i = 0

            def next(self):
                it = self.items[self.i % len(self.items)]
                self.i += 1
                return it

        def sb_pool(ph, n, name, shape, dt):
            return Rot([(ph.enter_context(SBT("%s%d" % (name, i), shape, dt)), Buf("%s%d" % (name, i)))
                        for i in range(n)])

        def linear(wd, m_list, KC, mw, rhs_fn, rhs_bufs, blocks_of, epilogue, wpool, pspool, post_tile=None):
            tiles = {}

            def load(m):
                wt, wB = wpool.next()
                P.dma("pool", wt[:], wd[m], w=[wB])
                tiles[m] = (wt, wB)

            m_list = list(m_list)
            load(m_list[0])
            for mi, m in enumerate(m_list):
                if mi + 1 < len(m_list):
                    load(m_list[mi + 1])
                wt, wB = tiles.pop(m)
                for (b0, bn) in blocks_of(m):
                    for sub in range(mw // 128):
                        ps, psB = pspool.next()
                        for c in range(KC):
                            P.mm(ps[:, 0:bn], wt[:, c, sub * 128:(sub + 1) * 128], rhs_fn(c, b0, bn),
                                 c == 0, c == KC - 1, r=[wB] + list(rhs_bufs), w=[psB])
                        epilogue(m, sub, b0, bn, ps, psB)
                if post_tile is not None:
                    post_tile(m)

        def rstd_from(ph, ssq_list, blocks, TN, epsT, Be, name):
            rstd = ph.enter_context(SBT(name, [128, TN], F32))
            B = Buf(name)
            base = blocks[0][0]
            for (sp, spB), (b0, bn) in zip(ssq_list, blocks):
                P.act(rstd[:, b0 - base:b0 - base + bn], sp[:, 0:bn], AF.Sqrt, r=[spB, Be], w=[B], bias=epsT[:, 0:1], scale=1.0 / D)
            P.recip(rstd[:], rstd[:], r=[B], w=[B])
            return rstd, B

        def proj_ssq(ph, tmp, wd, KC, rhs_fn, rhs_bufs, blocks, dst_dram, ones, Bo, tag):
            ssq = [(ph.enter_context(PST("ssq%s%d" % (tag, i), [128, 512], F32)), Buf("ssq", True)) for i in range(len(blocks))]
            wpool = sb_pool(tmp, 3, "wo" + tag, [128, KC, 128], BF16)
            pspool = Rot(psum_pool(tmp, 4, "po" + tag))
            TN = sum(bn for _, bn in blocks)
            base = blocks[0][0]
            ost = sb_pool(tmp, 2, "ost" + tag, [128, TN], F32)
            sqp = sb_pool(tmp, 3, "osq" + tag, [128, 512], BF16)
            pending = []
            state = {}
            n_mt = wd.shape[0]

            def flush():
                while pending:
                    pending.pop(0)()

            def ep(m, sub, b0, bn, ps, psB):
                if b0 == base:
                    state["o"] = ost.next()
                o, Bos = state["o"]
                bi = [b for b, _ in blocks].index(b0)
                P.copy("dve", o[:, b0 - base:b0 - base + bn], ps[:, 0:bn], r=[psB], w=[Bos])
                sq, Bsq = sqp.next()
                P.act(sq[:, 0:bn], ps[:, 0:bn], AF.Square, r=[psB], w=[Bsq])
                flush()
                sp, spB = ssq[bi]
                pending.append(lambda: P.mm(sp[:, 0:bn], ones[:], sq[:, 0:bn], m == 0, m == n_mt - 1, r=[Bo, Bsq], w=[spB]))

            def post(m):
                o, Bos = state["o"]
                P.dma("sp", dst_dram[m * 128:(m + 1) * 128, :], o[:, 0:TN], r=[Bos])

            linear(wd, range(n_mt), KC, 128, rhs_fn, rhs_bufs, lambda m: blocks, ep, wpool, pspool, post)
            flush()
            return ssq

        def ssq_tiles(ph, n, tag):
            return [(ph.enter_context(PST("sq%s%d" % (tag, i), [128, 512], F32)), Buf("sq" + tag, True)) for i in range(n)]

        with ExitStack() as ph:
            ones, Bo, epsT, Be = make_consts(ph)
            hnT = ph.enter_context(SBT("hnT", [128, DC, T], BF16))
            Bhn = Buf("hnT")
            gn = ph.enter_context(SBT("gn", [128, 5, DC], F32))
            Bg = Buf("gn")
            P.dma("sp", gn[:], gains, w=[Bg])
            with ExitStack() as ph0:
                xs_pool = sb_pool(ph0, 2, "xs", [128, DC, 128], F32)
                sq_pool = sb_pool(ph0, 2, "sq", [128, DC, 128], BF16)
                rs_pool = sb_pool(ph0, 2, "rs", [128, 128], F32)
                pp = Rot(psum_pool(ph0, 2, "pn"))
                xTv = xT.rearrange("(c p) t -> p c t", p=128)
                nxt = None
                for tb in range(T // 128):
                    t0 = tb * 128
                    if nxt is None:
                        nxt = xs_pool.next()
                        P.dma("sp", nxt[0][:], xTt[tb], w=[nxt[1]])
                    xs, Bx = nxt
                    if tb + 1 < T // 128:
                        nxt = xs_pool.next()
                        P.dma("sp", nxt[0][:], xTt[tb + 1], w=[nxt[1]])
                    sq, Bs = sq_pool.next()
                    rs, Br = rs_pool.next()
                    ps, psB = pp.next()
                    P.act(sq[:], xs[:], AF.Square, r=[Bx], w=[Bs])
                    for c in range(DC):
                        P.mm(ps[:, 0:128], ones[:], sq[:, c, :], c == 0, c == DC - 1, r=[Bo, Bs], w=[psB])
                    P.act(rs[:], ps[:, 0:128], AF.Sqrt, r=[psB, Be], w=[Br], bias=epsT[:, 0:1], scale=1.0 / D)
                    P.recip(rs[:], rs[:], r=[Br], w=[Br])
                    P.tt("dve", xs[:], xs[:], bcast_ap(rs, 0, [(0, DC), (1, 128)], 128), ALU.mult, r=[Bx, Br], w=[Bx])
                    P.tt("pool", hnT[:, :, t0:t0 + 128], xs[:], bcast_ap(gn, 0, [(1, DC), (0, 128)], 5 * DC),
                         ALU.mult, r=[Bx, Bg], w=[Bhn])
                P.phase_end()

            def hn_rhs(c, b0, bn):
                return hnT[:, c, b0:b0 + bn]

            with ExitStack() as ph1:
                cs = ph1.enter_context(SBT("cs", [128, 2, T], F32))
                Bcs = Buf("cs")
                P.dma("sp", cs[:, 0, :], cosT, w=[Bcs])
                P.dma("sp", cs[:, 1, :], sinT, w=[Bcs])
                permb = ph1.enter_context(SBT("permb", [128, 128], BF16))
                Bperm = Buf("permb")
                P.dma("pool", permb[:], c_perm, w=[Bperm])
                wqpool = sb_pool(ph1, 3, "wqk", [128, DC, 128], BF16)
                pspool = Rot(psum_pool(ph1, 4, "pa"))
                ps2pool = Rot(psum_pool(ph1, 3, "pa2"))
                stg = sb_pool(ph1, 2, "stg", [128, T], BF16)
                qbp = sb_pool(ph1, 3, "qb", [128, 512], BF16)
                t1p = sb_pool(ph1, 3, "t1", [128, 512], F32)
                t2p = sb_pool(ph1, 3, "t2", [128, 512], F32)
                state = {}
                pending = []

                def flush_qk():
                    while pending:
                        pending.pop(0)()

                def ep_qk(m, sub, b0, bn, ps, psB):
                    if b0 == 0:
                        state["stg"] = stg.next()
                    st_, Bst = state["stg"]
                    qb, Bqb = qbp.next()
                    t1, B1 = t1p.next()
                    P.copy("act", qb[:, 0:bn], ps[:, 0:bn], r=[psB], w=[Bqb])
                    P.tt("dve", t1[:, 0:bn], ps[:, 0:bn], cs[:, 0, b0:b0 + bn], ALU.mult, r=[psB, Bcs], w=[B1])
                    flush_qk()

                    def fin(st_=st_, Bst=Bst, qb=qb, Bqb=Bqb, t1=t1, B1=B1, b0=b0, bn=bn):
                        ps2, ps2B = ps2pool.next()
                        P.mm(ps2[:, 0:bn], permb[:], qb[:, 0:bn], True, True, r=[Bperm, Bqb], w=[ps2B])
                        t2, B2 = t2p.next()
                        P.tt("dve", t2[:, 0:bn], ps2[:, 0:bn], cs[:, 1, b0:b0 + bn], ALU.mult, r=[ps2B, Bcs], w=[B2])
                        P.tt("dve", st_[:, b0:b0 + bn], t1[:, 0:bn], t2[:, 0:bn], ALU.add, r=[B1, B2], w=[Bst])
                    pending.append(fin)

                def post_qk(m):
                    flush_qk()
                    st_, Bst = state["stg"]
                    h = m % H
                    if m < H:
                        P.dma("sp", qT[h * 128:(h + 1) * 128, :], st_[:, 0:TE], r=[Bst])
                    else:
                        P.dma("sp", kT[h * 128:(h + 1) * 128, :], st_[:, 0:T], r=[Bst])

                linear(w_qk, range(2 * H), DC, 128, hn_rhs, [Bhn], lambda m: (blocks_ext if m < H else blocks_full),
                       ep_qk, wqpool, pspool, post_qk)
                P.phase_end()

            with ExitStack() as ph1:
                w1pool = sb_pool(ph1, 3, "w1", [128, DC, 128], BF16)
                pspool = Rot(psum_pool(ph1, 6, "pb"))
                stg = sb_pool(ph1, 2, "stgb", [128, T], BF16)
                state = {}

                def ep_u(m, sub, b0, bn, ps, psB):
                    if b0 == 0:
                        state["stg"] = stg.next()
                    st_, Bst = state["stg"]
                    P.copy("act", bcast_ap(st_, b0 // 8, [(1, bn // 8), (NK, 8)], T),
                           bass.AP(ps, 0, [[512, 128], [8, bn // 8], [1, 8]]), r=[psB], w=[Bst])

                def post_u(m):
                    st_, Bst = state["stg"]
                    P.dma("sp", uT[m * 128:(m + 1) * 128, :], st_[:, 0:T], r=[Bst])

                linear(w_u, range(SC), DC, 128, hn_rhs, [Bhn], lambda m: blocks_full, ep_u, w1pool, pspool, post_u)

                def ep_g(m, sub, b0, bn, ps, psB):
                    if b0 == 0:
                        state["stg"] = stg.next()
                    st_, Bst = state["stg"]
                    P.act(st_[:, b0:b0 + bn], ps[:, 0:bn], AF.Sigmoid, r=[psB], w=[Bst])

                def post_g(m):
                    st_, Bst = state["stg"]
                    P.dma("sp", sgT[m * 128:(m + 1) * 128, :], st_[:, 0:TE], r=[Bst])

                linear(w_g, range(2 * DC), DC, 128, hn_rhs, [Bhn], lambda m: blocks_ext, ep_g, w1pool, pspool, post_g)
                P.phase_end()

            with ExitStack() as ph2:
                VW = min(AW, 512)
                wv = sb_pool(ph2, 2 if DC <= 16 else 1, "wv", [128, DC, VW], BF16)
                pspool = Rot(psum_pool(ph2, 4, "pv"))
                vst = sb_pool(ph2, 3, "vst", [128, VW], BF16)
                for cb in range(AW // VW):
                    wt, wB = wv.next()
                    P.dma("pool", wt[:], w_v[cb], w=[wB])
                    for tt_ in range(T // 128):
                        ps, psB = pspool.next()
                        for c in range(DC):
                            P.mm(ps[:, 0:VW], hnT[:, c, tt_ * 128:(tt_ + 1) * 128], wt[:, c, :], c == 0, c == DC - 1,
                                 r=[wB, Bhn], w=[psB])
                        st_, Bst = vst.next()
                        P.copy(rr_eng(("act", "dve")), st_[:], ps[:, 0:VW], r=[psB], w=[Bst])
                        P.dma("sp", vS[tt_ * 128:(tt_ + 1) * 128, cb * VW:(cb + 1) * VW], st_[:], r=[Bst])
                P.phase_end()

        if cfg.get('stop') == 'A':
            return nc
        with ExitStack() as ph:
            ones, Bo, epsT, Be = make_consts(ph)
            mt = ph.enter_context(SBT("mtab", [128, MW], BF16))
            Bm = Buf("mtab")
            P.dma("pool", mt[:], c_mtab, w=[Bm])
            qp = sb_pool(ph, 2, "qh", [128, TE], BF16)
            kp = sb_pool(ph, 2, "kh", [128, T], BF16)
            vp = sb_pool(ph, 2, "vh", [128, T // 128, 128], BF16)
            pS = Rot(psum_pool(ph, 4, "pS"))
            pO = Rot(psum_pool(ph, 2, "pO"))
            pD = Rot(psum_pool(ph, 2, "pD"))
            pe_ = sb_pool(ph, 4, "pe", [128, 512], BF16)
            pm_ = sb_pool(ph, 4, "pm", [128, 512], BF16)
            rd_ = sb_pool(ph, 2, "rd", [128, 512], F32)
            ya_ = sb_pool(ph, 2, "ya", [128, TE], BF16)
            vSv = vS.rearrange("(kt p) f -> p kt f", p=128)

            def load_head(h):
                qh, Bq = qp.next()
                kh, Bk = kp.next()
                vh, Bv = vp.next()
                P.dma("sp", qh[:], qT[h * 128:(h + 1) * 128, :], w=[Bq])
                P.dma("sp", kh[:], kT[h * 128:(h + 1) * 128, :], w=[Bk])
                P.dma("sp", vh[:], vSv[:, :, h * 128:(h + 1) * 128], w=[Bv])
                return (qh, Bq, kh, Bk, vh, Bv)

            nxt = load_head(0)
            for h in range(H):
                qh, Bq, kh, Bk, vh, Bv = nxt
                if h + 1 < H:
                    nxt = load_head(h + 1)
                ya, Bya = ya_.next()
                for (b0, bn) in blocks_ext:
                    kts = []
                    for kt in range(T // 128):
                        dmin = kt * 128 - (b0 + bn - 1)
                        dmax = kt * 128 + 127 - b0
                        if dmin > WMAX or dmax < -WMAX:
                            continue
                        kts.append(kt)
                    po, poB = pO.next()
                    pd, pdB = pD.next()
                    pend = []
                    BD = cfg.get("bdbg", 9)
                    if BD < 1:
                        continue
                    for i, kt in enumerate(kts):
                        ps, psB = pS.next()
                        P.mm(ps[:, 0:bn], kh[:, kt * 128:(kt + 1) * 128], qh[:, b0:b0 + bn], True, True, r=[Bk, Bq], w=[psB])
                        pe, Bpe = pe_.next()
                        pm, Bpm = pm_.next()
                        if BD < 2:
                            continue
                        P.act(pe[:, 0:bn], ps[:, 0:bn], AF.Exp, r=[psB], w=[Bpe], scale=att_scale)
                        if BD < 3:
                            continue
                        xo = b0 - kt * 128 + X0
                        P.tt("dve", pm[:, 0:bn], pe[:, 0:bn], mt[:, xo:xo + bn], ALU.mult, r=[Bpe, Bm], w=[Bpm])
                        if BD < 4:
                            continue
                        while len(pend) > 1:
                            pend.pop(0)()

                        def f(i=i, kt=kt, pm=pm, Bpm=Bpm, po=po, poB=poB, pd=pd, pdB=pdB, bn=bn, nk=len(kts), vh=vh, Bv=Bv):
                            P.mm(po[:, 0:bn], vh[:, kt, :], pm[:, 0:bn], i == 0, i == nk - 1, r=[Bv, Bpm], w=[poB])
                            P.mm(pd[:, 0:bn], ones[:], pm[:, 0:bn], i == 0, i == nk - 1, r=[Bo, Bpm], w=[pdB])
                        pend.append(f)
                    while pend:
                        pend.pop(0)()
                    if BD < 5:
                        continue
                    rd, Brd = rd_.next()
                    P.recip(rd[:, 0:bn], pd[:, 0:bn], r=[pdB], w=[Brd])
                    P.tt("dve", ya[:, b0:b0 + bn], po[:, 0:bn], rd[:, 0:bn], ALU.mult, r=[poB, Brd], w=[Bya])
                P.dma("sp", yaT[h * 128:(h + 1) * 128, :], ya[:], r=[Bya])
            P.phase_end()

        if cfg.get('stop') == 'B':
            return nc
        KB = max(1, min(16, 512 // (2 * GH)))
        NJ = 8
        NW = 24
        GC = min(GH, 16)
        PI_LO = 3.1415925
        for gh in range(G // GH):
            g0 = gh * GH
            with ExitStack() as ph:
                ident = ph.enter_context(SBT("ident", [128, 128], F32))
                identb = ph.enter_context(SBT("identb", [128, 128], BF16))
                Bid = Buf("ident")
                Bidb = Buf("identb")
                P.dma("sp", ident[:], c_ident, w=[Bid])
                P.dma("pool", identb[:], c_ident, w=[Bidb])
                Ws = ph.enter_context(SBT("Ws", [128, GH, 2, 128], BF16))
                BWs = Buf("Ws")
                Tz = ph.enter_context(SBT("Tz", [128, GH, 128], BF16))
                BTz = Buf("Tz")
                Et = ph.enter_context(SBT("Et", [128, GH, 2, 128], BF16))
                BEt = Buf("Et")
                Ac = ph.enter_context(SBT("Ac", [128, 2, GH, 2], F32))
                BAc = Buf("Ac")
                with ExitStack() as pt:
                    lam = pt.enter_context(SBT("lam", [128, 3, GH], F32))
                    Bl = Buf("lam")
                    P.dma("sp", lam[:], s_lam[:, :, g0:g0 + GH], w=[Bl])
                    ntab = pt.enter_context(SBT("ntab", [128, 26], F32))
                    Bnt = Buf("ntab")
                    P.dma("sp", ntab[:], c_ntab, w=[Bnt])
                    Bt = pt.enter_context(SBT("Bt", [128, 2, GH, 16], F32))
                    Ct = pt.enter_context(SBT("Ct", [128, 2, GH, 16], F32))
                    BBt, BCt = Buf("Bt"), Buf("Ct")
                    P.dma("sp", Bt[:], s_B[:, :, g0:g0 + GH, :], w=[BBt])
                    P.dma("sp", Ct[:], s_C[:, :, g0:g0 + GH, :], w=[BCt])
                    msk = pt.enter_context(SBT("msk", [128, 2, 128], F32))
                    Bmk = Buf("msk")
                    P.dma("sp", msk[:], c_mask, w=[Bmk])
                    Dd = pt.enter_context(SBT("Dd", [128, GH], F32))
                    BDd = Buf("Dd")
                    P.dma("sp", Dd[:], s_D[:, g0:g0 + GH], w=[BDd])
                    sm = pt.enter_context(SBT("sm", [128, 8, GH], F32))
                    Bsm = Buf("sm")
                    P.act(sm[:, 0, :], lam[:, 2, :], AF.Exp, r=[Bl], w=[Bsm])
                    P.tt("dve", sm[:, 1, :], lam[:, 0, :], sm[:, 0, :], ALU.mult, r=[Bl, Bsm], w=[Bsm])
                    P.tt("dve", sm[:, 2, :], lam[:, 1, :], sm[:, 0, :], ALU.mult, r=[Bl, Bsm], w=[Bsm])

                    pwm = pt.enter_context(SBT("pwm", [128, 2, GH, NW], F32))
                    pw2 = pt.enter_context(SBT("pw2", [128, 2, GH, 2], F32))
                    pws = pt.enter_context(SBT("pws", [128, 2, GH, NJ], F32))
                    Bpwm, Bpw2, Bpws = Buf("pwm"), Buf("pw2"), Buf("pws")
                    pk = ExitStack()
                    wk = pk.enter_context(SBT("wk", [128, 4, GH, NW], F32))
                    wki = pk.enter_context(SBT("wki", [128, GH, NW], I32))
                    Bwk = Buf("wk")

                    def cpow(col0, nj, dst, rowlen, Bout):
                        def v(i):
                            return bcast_ap(wk, i * GH * NW, [(NW, GH), (1, nj)], 4 * GH * NW)

                        def o(ri):
                            return bcast_ap(dst, ri * GH * rowlen, [(rowlen, GH), (1, nj)], 2 * GH * rowlen)
                        lrdt_b = bcast_ap(sm, 1 * GH, [(1, GH), (0, nj)], 8 * GH)
                        th_b = bcast_ap(sm, 2 * GH, [(1, GH), (0, nj)], 8 * GH)
                        nt_b = bcast_ap(ntab, col0, [(0, GH), (1, nj)], 26)
                        wki_v = bcast_ap(wki, 0, [(NW, GH), (1, nj)], GH * NW)
                        P.tt("dve", o(0), lrdt_b, nt_b, ALU.mult, r=[Bsm, Bnt], w=[Bout])
                        P.act(o(0), o(0), AF.Exp, r=[Bout], w=[Bout])
                        P.tt("dve", v(0), th_b, nt_b, ALU.mult, r=[Bsm, Bnt], w=[Bwk])
                        for (dsti, shift) in ((3, 0.0), (2, float(np.pi / 2))):
                            P.ts("dve", v(1), v(0), shift, None, ALU.add, r=[Bwk], w=[Bwk])
                            P.ts("dve", wki_v, v(1), 1.0 / TWO_PI, None, ALU.mult, r=[Bwk], w=[Bwk])
                            P.copy("dve", v(2), wki_v, r=[Bwk], w=[Bwk])
                            P.stt("dve", v(1), v(2), -TWO_PI, v(1), ALU.mult, ALU.add, r=[Bwk], w=[Bwk])
                            P.ts("dve", v(1), v(1), -PI_LO, PI_LO, ALU.max, ALU.min, r=[Bwk], w=[Bwk])
                            P.act(v(dsti), v(1), AF.Sin, r=[Bwk], w=[Bwk])
                        P.tt("dve", o(1), o(0), v(3), ALU.mult, r=[Bwk, Bout], w=[Bout])
                        P.tt("dve", o(0), o(0), v(2), ALU.mult, r=[Bwk, Bout], w=[Bout])

                    cpow(24, 2, pw2, 2, Bpw2)
                    cpow(0, NW, pwm, NW, Bpwm)
                    ar = bcast_ap(pw2, 0, [(2, GH)], 2 * GH * 2)
                    ai = bcast_ap(pw2, GH * 2, [(2, GH)], 2 * GH * 2)
                    a8r = bcast_ap(pw2, 1, [(2, GH)], 2 * GH * 2)
                    a8i = bcast_ap(pw2, GH * 2 + 1, [(2, GH)], 2 * GH * 2)
                    P.tt("dve", sm[:, 3, :], lam[:, 0, :], lam[:, 0, :], ALU.mult, r=[Bl], w=[Bsm])
                    P.tt("dve", sm[:, 6, :], lam[:, 1, :], lam[:, 1, :], ALU.mult, r=[Bl], w=[Bsm])
                    P.tt("dve", sm[:, 3, :], sm[:, 3, :], sm[:, 6, :], ALU.add, r=[Bsm], w=[Bsm])
                    P.recip(sm[:, 3, :], sm[:, 3, :], r=[Bsm], w=[Bsm])
                    P.ts("dve", sm[:, 6, :], ar, -1.0, None, ALU.add, r=[Bpw2], w=[Bsm])
                    P.tt("dve", sm[:, 4, :], sm[:, 6, :], lam[:, 0, :], ALU.mult, r=[Bsm, Bl], w=[Bsm])
                    P.tt("dve", sm[:, 7, :], ai, lam[:, 1, :], ALU.mult, r=[Bpw2, Bl], w=[Bsm])
                    P.tt("dve", sm[:, 4, :], sm[:, 4, :], sm[:, 7, :], ALU.add, r=[Bsm], w=[Bsm])
                    P.tt("dve", sm[:, 4, :], sm[:, 4, :], sm[:, 3, :], ALU.mult, r=[Bsm], w=[Bsm])
                    P.tt("dve", sm[:, 5, :], ai, lam[:, 0, :], ALU.mult, r=[Bpw2, Bl], w=[Bsm])
                    P.tt("dve", sm[:, 7, :], sm[:, 6, :], lam[:, 1, :], ALU.mult, r=[Bsm, Bl], w=[Bsm])
                    P.tt("dve", sm[:, 5, :], sm[:, 5, :], sm[:, 7, :], ALU.subtract, r=[Bsm], w=[Bsm])
                    P.tt("dve", sm[:, 5, :], sm[:, 5, :], sm[:, 3, :], ALU.mult, r=[Bsm], w=[Bsm])

                    def Acv(k, ri):
                        return bcast_ap(Ac, k * GH * 2 + ri, [(2, GH)], 2 * GH * 2)
                    P.copy("dve", Acv(0, 0), a8r, r=[Bpw2], w=[BAc])
                    P.copy("dve", Acv(0, 1), a8r, r=[Bpw2], w=[BAc])
                    P.copy("dve", Acv(1, 1), a8i, r=[Bpw2], w=[BAc])
                    P.ts("dve", Acv(1, 0), a8i, -1.0, None, ALU.mult, r=[Bpw2], w=[BAc])
                    fr_b = bcast_ap(sm, 4 * GH, [(1, GH), (0, NJ)], 8 * GH)
                    fi_b = bcast_ap(sm, 5 * GH, [(1, GH), (0, NJ)], 8 * GH)

                    def pm_(ri):
                        return bcast_ap(pwm, ri * GH * NW, [(NW, GH), (1, NJ)], 2 * GH * NW)

                    def ps_(ri):
                        return bcast_ap(pws, ri * GH * NJ, [(NJ, GH), (1, NJ)], 2 * GH * NJ)

                    def wv(i):
                        return bcast_ap(wk, i * GH * NW, [(NW, GH), (1, NJ)], 4 * GH * NW)
                    P.tt("dve", wv(0), pm_(0), fr_b, ALU.mult, r=[Bpwm, Bsm], w=[Bwk])
                    P.tt("dve", wv(1), pm_(1), fi_b, ALU.mult, r=[Bpwm, Bsm], w=[Bwk])
                    P.tt("dve", ps_(0), wv(0), wv(1), ALU.subtract, r=[Bwk], w=[Bpws])
                    P.tt("dve", wv(0), pm_(0), fi_b, ALU.mult, r=[Bpwm, Bsm], w=[Bwk])
                    P.tt("dve", wv(1), pm_(1), fr_b, ALU.mult, r=[Bpwm, Bsm], w=[Bwk])
                    P.tt("dve", ps_(1), wv(0), wv(1), ALU.add, r=[Bwk], w=[Bpws])

                    P.phase_end()
                    pk.close()
                    Xp = sb_pool(pt, 2, "Xc", [128, GC, 2, 128], BF16)
                    Yp = sb_pool(pt, 2, "Yc", [128, GC, 2, 128], BF16)
                    tmpx = sb_pool(pt, 1, "ctmpx", [128, 2, GC, 128], F32)
                    tmpy = sb_pool(pt, 1, "ctmpy", [128, 2, GC, 128], F32)

                    def cprod(eng, tmp, ptab, prow, pj0, gc, src, Bsrc, Bp, dst, doff, drow, Bdst, neg_imag):
                        tm, Btm = tmp.next()

                        def pv(ri):
                            return bcast_ap(ptab, ri * GH * prow + gc * GC * prow + pj0, [(prow, GC), (1, 8), (0, 16)], 2 * GH * prow)

                        def sv(ri):
                            return bcast_ap(src, ri * GH * 16 + gc * GC * 16, [(16, GC), (0, 8), (1, 16)], 2 * GH * 16)

                        def tv(i):
                            return bcast_ap(tm, i * GC * 128, [(128, GC), (16, 8), (1, 16)], 2 * GC * 128)

                        def dv(ri):
                            return bcast_ap(dst, doff + ri * 128, [(256, GC), (16, 8), (1, 16)], drow)
                        P.tt(eng, tv(0), pv(0), sv(0), ALU.mult, r=[Bp, Bsrc], w=[Btm])
                        P.tt(eng, tv(1), pv(1), sv(1), ALU.mult, r=[Bp, Bsrc], w=[Btm])
                        P.tt(eng, dv(0), tv(0), tv(1), ALU.subtract, r=[Btm], w=[Bdst])
                        P.tt(eng, tv(0), pv(0), sv(1), ALU.mult, r=[Bp, Bsrc], w=[Btm])
                        P.tt(eng, tv(1), pv(1), sv(0), ALU.mult, r=[Bp, Bsrc], w=[Btm])
                        if neg_imag and eng == "dve":
                            P.stt(eng, dv(1), tv(0), -1.0, tv(1), ALU.mult, ALU.subtract, r=[Btm], w=[Bdst])
                        elif neg_imag:
                            P.ts(eng, tv(0), tv(0), -1.0, None, ALU.mult, r=[Btm], w=[Btm])
                            P.tt(eng, dv(1), tv(0), tv(1), ALU.subtract, r=[Btm], w=[Bdst])
                        else:
                            P.tt(eng, dv(1), tv(0), tv(1), ALU.add, r=[Btm], w=[Bdst])

                    pst = Rot(psum_pool(pt, 4, "pt"))
                    tz1 = sb_pool(pt, 2, "tz1", [128, 4, 128], F32)
                    tz2 = sb_pool(pt, 2, "tz2", [128, 4, 128], F32)
                    for gc in range(GH // GC):
                        Xc, BXc = Xp.next()
                        Yc, BYc = Yp.next()
                        cprod("dve", tmpx, pws, NJ, 0, gc, Bt, BBt, Bpws, Xc, 0, GC * 256, BXc, False)
                        cprod("pool", tmpy, pwm, NW, 8, gc, Ct, BCt, Bpwm, Yc, 0, GC * 256, BYc, True)
                        cprod("dve", tmpx, pwm, NW, 16, gc, Ct, BCt, Bpwm, Et, gc * GC * 256, GH * 256, BEt, True)
                        for q4 in range(GC * 2 // 4):
                            ps, psB = pst.next()
                            for i in range(4):
                                idx = q4 * 4 + i
                                gl_, ri = idx // 2, idx % 2
                                P.mm(ps[:, i * 128:(i + 1) * 128], Xc[:, gl_, ri, :], identb[:], True, True, r=[BXc, Bidb], w=[psB])
                            P.copy(rr_eng(("act", "dve")), bcast_ap(Ws, gc * GC * 256 + q4 * 512, [(1, 512)], GH * 256), ps[:, 0:512],
                                   r=[psB], w=[BWs])
                        for g4 in range(GC // 4):
                            psd = []
                            for d in range(2):
                                ps, psB = pst.next()
                                lo, hi = d * 64, (d + 1) * 64
                                for i in range(4):
                                    gl_ = g4 * 4 + i
                                    P.mm(ps[:, i * 128:(i + 1) * 128], Xc[lo:hi, gl_, 0, :], Yc[lo:hi, gl_, 0, :], True, False, r=[BXc, BYc], w=[psB])
                                    P.mm(ps[:, i * 128:(i + 1) * 128], Xc[lo:hi, gl_, 1, :], Yc[lo:hi, gl_, 1, :], False, True, r=[BXc, BYc], w=[psB])
                                psd.append((ps, psB))
                            a1, B1 = tz1.next()
                            a2, B2 = tz2.next()
                            mk0 = bcast_ap(msk, 0, [(0, 4), (1, 128)], 256)
                            mk1 = bcast_ap(msk, 128, [(0, 4), (1, 128)], 256)
                            pv0 = bass.AP(psd[0][0], 0, [[512, 128], [128, 4], [1, 128]])
                            pv1 = bass.AP(psd[1][0], 0, [[512, 128], [128, 4], [1, 128]])
                            P.tt("dve", a1[:], pv0, mk0, ALU.mult, r=[psd[0][1], Bmk], w=[B1])
                            P.tt("dve", a2[:], pv1, mk1, ALU.mult, r=[psd[1][1], Bmk], w=[B2])
                            P.tt("pool", a1[:], a1[:], a2[:], ALU.add, r=[B1, B2], w=[B1])
                            for i in range(4):
                                g = gc * GC + g4 * 4 + i
                                P.stt("dve", Tz[:, g, :], ident[:], Dd[:, g:g + 1], a1[:, i, :], ALU.mult, ALU.add,
                                      r=[Bid, BDd, B1], w=[BTz])
                    P.phase_end()

                Ut = ph.enter_context(SBT("Ut", [128, GH, NK], BF16))
                BUt = Buf("Ut")
                with ExitStack() as pu:
                    sel = pu.enter_context(SBT("sel", [128, 64, 128], BF16))
                    Bsel = Buf("sel")
                    P.dma("pool", sel[:], c_sel, w=[Bsel])
                    utp = sb_pool(pu, 2, "utile", [128, T], BF16)
                    pst = Rot(psum_pool(pu, 4, "pu"))

                    def load_ut(t8):
                        ut, But = utp.next()
                        ft = (g0 // 8) + t8
                        P.dma("sp", ut[:], uT[ft * 128:(ft + 1) * 128, :], w=[But])
                        return ut, But
                    nxt = load_ut(0)
                    for t8 in range(GH // 8):
                        ut, But = nxt
                        if t8 + 1 < GH // 8:
                            nxt = load_ut(t8 + 1)
                        for gl in range(8):
                            g = t8 * 8 + gl
                            ps, psB = pst.next()
                            for j in range(8):
                                P.mm(ps[:, 0:NK], sel[:, gl * 8 + j, :], ut[:, j * NK:(j + 1) * NK], j == 0, j == 7,
                                     r=[Bsel, But], w=[psB])
                            P.copy(rr_eng(("act", "dve")), Ut[:, g, :], ps[:, 0:NK], r=[psB], w=[BUt])
                    P.phase_end()
                hist = ph.enter_context(SBT("hist", [128, GH, 2, NKO], BF16))
                Bhist = Buf("hist")

                with ExitStack() as pr:
                    KB = 16
                    gpb = 512 // (2 * KB)
                    nbk = GH * 2 * KB // 512
                    pssets = Rot([([pr.enter_context(PST("psS%d_%d" % (a_, b_), [128, 512], F32)) for b_ in range(nbk)], Buf("psS", True))
                                  for a_ in range(2)])
                    Ssb = {d: [pr.enter_context(SBT("Ssb%d_%d" % (d, s_), [128, GH, 2, KB], F32)) for s_ in range(2)] for d in (0, 1)}
                    BSsb = {d: [Buf("Ssb"), Buf("Ssb")] for d in (0, 1)}
                    st = pr.enter_context(SBT("st", [128, 2, GH, 2], F32))
                    w1 = pr.enter_context(SBT("w1r", [128, GH, 2], F32))
                    w2 = pr.enter_context(SBT("w2r", [128, GH, 2], F32))
                    HG = GH // 2
                    Bst = {(d, pp, h_): Buf("st") for d in (0, 1) for pp in (0, 1) for h_ in (0, 1)}
                    Bw1 = {(d, h_): Buf("w1") for d in (0, 1) for h_ in (0, 1)}
                    Bw2 = {(d, h_): Buf("w2") for d in (0, 1) for h_ in (0, 1)}
                    Bhd = {0: Buf("hist0"), 1: Buf("hist1")}
                    P.memset("dve", st[:], 0.0, w=list(Bst.values()))

                    def build_steps(d):
                        lo, hi = d * 64, (d + 1) * 64
                        if d == 1:
                            kblocks = [(kb, min(KB, NK - kb)) for kb in range(0, NK, KB)][::-1]
                        else:
                            kblocks = [(kb, min(KB, NKO - kb)) for kb in range(0, NKO, KB)]

                        def mms(bi):
                            kb, kn = kblocks[bi]
                            pset, pB = pssets.next()
                            for g in range(GH):
                                for ri in range(2):
                                    off = (g % gpb) * 2 * KB + ri * KB
                                    P.mm(pset[g // gpb][:, off:off + kn], Ws[:, g, ri, :], Ut[:, g, kb:kb + kn], True, True,
                                         r=[BWs, BUt], w=[pB])
                            S_, BS_ = Ssb[d][bi % 2], BSsb[d][bi % 2]
                            for j_ in range(nbk):
                                P.copy("act", bcast_ap(S_, j_ * 512, [(1, 512)], GH * 2 * KB, 64, lo), pset[j_][lo:hi, 0:512],
                                       r=[pB], w=[BS_])
                        steps = []
                        cur = 0
                        for bi, (kb, kn) in enumerate(kblocks):
                            S_, BS_ = Ssb[d][bi % 2], BSsb[d][bi % 2]
                            ks = list(range(kb, kb + kn))
                            if d == 1:
                                ks = ks[::-1]
                            for ki, k in enumerate(ks):
                                kk = k - kb
                                fns = []
                                if ki == 0:
                                    if bi == 0:
                                        fns.append(lambda: mms(0))
                                    if bi + 1 < len(kblocks):
                                        fns.append(lambda bi=bi: mms(bi + 1))
                                split = (k >= NKO)
                                parts = [(0, HG, 0), (HG, GH, 1)] if split else [(0, GH, None)]
                                c_, n_ = cur, 1 - cur
                                if k < NKO:
                                    fns.append(lambda k=k, c_=c_: P.copy(
                                        "pool", bcast_ap(hist, k, [(2 * NKO, GH), (NKO, 2)], GH * 2 * NKO, 64, lo),
                                        st[lo:hi, c_, :, :], r=[Bst[(d, c_, 0)], Bst[(d, c_, 1)]], w=[Bhd[d]]))

                                def bl(dct, key, h_):
                                    return [dct[key + (h_,)]] if h_ is not None else [dct[key + (0,)], dct[key + (1,)]]
                                for (gs, ge, h_) in parts:
                                    fns.append(lambda gs=gs, ge=ge, h_=h_, c_=c_: P.tt(
                                        "dve", w1[lo:hi, gs:ge, :], st[lo:hi, c_, gs:ge, :], Ac[lo:hi, 0, gs:ge, :], ALU.mult,
                                        r=bl(Bst, (d, c_), h_) + [BAc], w=bl(Bw1, (d,), h_)))
                                for (gs, ge, h_) in parts:
                                    fns.append(lambda gs=gs, ge=ge, h_=h_, c_=c_: P.tt(
                                        "dve", w2[lo:hi, gs:ge, :],
                                        bcast_ap(st, c_ * GH * 2 + gs * 2 + 1, [(2, ge - gs), (-1, 2)], 2 * GH * 2, 64, lo),
                                        Ac[lo:hi, 1, gs:ge, :], ALU.mult,
                                        r=bl(Bst, (d, c_), h_) + [BAc], w=bl(Bw2, (d,), h_)))
                                for (gs, ge, h_) in parts:
                                    fns.append(lambda gs=gs, ge=ge, h_=h_: P.tt(
                                        "dve", w1[lo:hi, gs:ge, :], w1[lo:hi, gs:ge, :], w2[lo:hi, gs:ge, :], ALU.add,
                                        r=bl(Bw1, (d,), h_) + bl(Bw2, (d,), h_), w=bl(Bw1, (d,), h_)))
                                for (gs, ge, h_) in parts:
                                    fns.append(lambda gs=gs, ge=ge, h_=h_, n_=n_, kk=kk, S_=S_, BS_=BS_: P.tt(
                                        "dve", st[lo:hi, n_, gs:ge, :], w1[lo:hi, gs:ge, :],
                                        bcast_ap(S_, gs * 2 * KB + kk, [(2 * KB, ge - gs), (KB, 2)], GH * 2 * KB, 64, lo), ALU.add,
                                        r=bl(Bw1, (d,), h_) + [BS_], w=bl(Bst, (d, n_), h_)))
                                steps.append((k, fns))
                                cur = n_
                        return steps

                    sa = build_steps(1)
                    sb = build_steps(0)
                    pre = [s_ for s_ in sa if s_[0] >= NKO]
                    pa = [s_ for s_ in sa if s_[0] < NKO]
                    for (_, fns) in pre:
                        for fn in fns:
                            fn()
                    assert len(pa) == len(sb) == NKO
                    for i in range(NKO):
                        fa, fb = pa[i][1], sb[i][1]
                        for j in range(max(len(fa), len(fb))):
                            if j < len(fa):
                                fa[j]()
                            if j < len(fb):
                                fb[j]()
                    P.phase_end()

                with ExitStack() as po_:
                    selT = po_.enter_context(SBT("selT", [128, 64, 128], BF16))
                    BselT = Buf("selT")
                    P.dma("pool", selT[:], c_selT, w=[BselT])
                    Zs = po_.enter_context(SBT("Zs", [128, GH, NKO], BF16))
                    BZs = Buf("Zs")
                    psy = Rot(psum_pool(po_, 3, "py"))
                    for g in range(GH):
                        ps, psB = psy.next()
                        P.mm(ps[:, 0:NKO], Tz[:, g, :], Ut[:, g, 0:NKO], True, False, r=[BTz, BUt], w=[psB])
                        P.mm(ps[:, 0:NKO], Et[:, g, 0, :], hist[:, g, 0, :], False, False, r=[BEt, Bhist], w=[psB])
                        P.mm(ps[:, 0:NKO], Et[:, g, 1, :], hist[:, g, 1, :], False, True, r=[BEt, Bhist], w=[psB])
                        P.act(Zs[:, g, :], ps[:, 0:NKO], AF.Gelu_apprx_tanh, r=[psB], w=[BZs])
                    psz = Rot(psum_pool(po_, 3, "pz"))
                    zst = sb_pool(po_, 2, "zst", [128, TE], BF16)
                    for t8 in range(GH // 8):
                        zs_, Bzs_ = zst.next()
                        for (b0, bn) in blocks_ext:
                            ps, psB = psz.next()
                            k0, kn = b0 // 8, bn // 8
                            for j in range(8):
                                for gl in range(8):
                                    g = t8 * 8 + gl
                                    P.mm(bass.AP(ps, j, [[512, 128], [8, kn]]), selT[:, gl * 8 + j, :], Zs[:, g, k0:k0 + kn],
                                         gl == 0, gl == 7, r=[BselT, BZs], w=[psB])
                            P.copy(rr_eng(("act", "dve")), zs_[:, b0:b0 + bn], ps[:, 0:bn], r=[psB], w=[Bzs_])
                        ft = (g0 // 8) + t8
                        P.dma("sp", zT[ft * 128:(ft + 1) * 128, :], zs_[:], r=[Bzs_])
                    P.phase_end()

        if cfg.get('stop') == 'C':
            return nc
        with ExitStack() as ph:
            ones, Bo, epsT, Be = make_consts(ph)
            gn = ph.enter_context(SBT("gn", [128, 5, DC], F32))
            Bg = Buf("gn")
            P.dma("sp", gn[:], gains, w=[Bg])
            mg = ph.enter_context(SBT("mg", [128, DC, TE], BF16))
            Bmg = Buf("mg")
            with ExitStack() as p0:
                ybT = p0.enter_context(SBT("ybT", [128, SC, TE], BF16))
                Byb = Buf("ybT")
                with ExitStack() as p1:
                    zTs = p1.enter_context(SBT("zTs", [128, SC, TE], BF16))
                    Bz = Buf("zTs")
                    P.dma("sp", zTs[:], zT.rearrange("(c p) t -> p c t", p=128), w=[Bz])
                    bg = p1.enter_context(SBT("bg", [128, SC], F32))
                    Bbg = Buf("bg")
                    P.dma("sp", bg[:], b_glu, w=[Bbg])
                    wpool = sb_pool(p1, 3, "wgl", [128, SC, 128], BF16)
                    pspool = Rot(psum_pool(p1, 4, "pg"))
                    sgp = sb_pool(p1, 2, "sgl", [128, 512], F32)

                    def ep_glu(m, sub, b0, bn, ps, psB):
                        sg, Bsg = sgp.next()
                        P.act(sg[:, 0:bn], ps[:, 0:bn], AF.Sigmoid, r=[psB, Bbg], w=[Bsg], bias=bg[:, m:m + 1])
                        P.tt("dve", ybT[:, m, b0:b0 + bn], sg[:, 0:bn], zTs[:, m, b0:b0 + bn], ALU.mult, r=[Bsg, Bz], w=[Byb])

                    linear(w_glu, range(SC), SC, 128, lambda c, b0, bn: zTs[:, c, b0:b0 + bn], [Bz], lambda m: blocks_ext,
                           ep_glu, wpool, pspool)
                    P.phase_end()

                with ExitStack() as p2:
                    yaS = p2.enter_context(SBT("yaS", [128, AC, TE], BF16))
                    Bya = Buf("yaS")
                    P.dma("sp", yaS[:], yaT.rearrange("(c p) t -> p c t", p=128), w=[Bya])
                    wa = sb_pool(p2, 2, "wa", [128, AC, 128], BF16)
                    wb = sb_pool(p2, 2, "wb", [128, SC, 128], BF16)
                    sgp = sb_pool(p2, 2, "sgab", [128, 2, TE], BF16)
                    pspool = Rot(psum_pool(p2, 6, "pm"))
                    t1p = sb_pool(p2, 2, "m1", [128, 512], F32)
                    t2p = sb_pool(p2, 2, "m2", [128, 512], F32)

                    def load_m(m):
                        wat, BwA = wa.next()
                        wbt, BwB = wb.next()
                        sg, Bsg = sgp.next()
                        P.dma("pool", wat[:], w_ba[m], w=[BwA])
                        P.dma("pool", wbt[:], w_bs[m], w=[BwB])
                        P.dma("sp", sg[:, 0, :], sgT[m * 128:(m + 1) * 128, :], w=[Bsg])
                        P.dma("sp", sg[:, 1, :], sgT[D + m * 128:D + (m + 1) * 128, :], w=[Bsg])
                        return (wat, BwA, wbt, BwB, sg, Bsg)

                    nxt = load_m(0)
                    for m in range(DC):
                        wat, BwA, wbt, BwB, sg, Bsg = nxt
                        if m + 1 < DC:
                            nxt = load_m(m + 1)
                        for (b0, bn) in blocks_ext:
                            psA, BA = pspool.next()
                            psB_, BB = pspool.next()
                            for c in range(AC):
                                P.mm(psA[:, 0:bn], wat[:, c, :], yaS[:, c, b0:b0 + bn], c == 0, c == AC - 1, r=[BwA, Bya], w=[BA])
                            for c in range(SC):
                                P.mm(psB_[:, 0:bn], wbt[:, c, :], ybT[:, c, b0:b0 + bn], c == 0, c == SC - 1, r=[BwB, Byb], w=[BB])
                            t1, B1 = t1p.next()
                            t2, B2 = t2p.next()
                            P.tt("dve", t1[:, 0:bn], psA[:, 0:bn], sg[:, 0, b0:b0 + bn], ALU.mult, r=[BA, Bsg], w=[B1])
                            P.tt("dve", t2[:, 0:bn], psB_[:, 0:bn], sg[:, 1, b0:b0 + bn], ALU.mult, r=[BB, Bsg], w=[B2])
                            P.tt("pool", mg[:, m, b0:b0 + bn], t1[:, 0:bn], t2[:, 0:bn], ALU.add, r=[B1, B2], w=[Bmg])
                    P.phase_end()

            with ExitStack() as ptmp:
                ssq = proj_ssq(ph, ptmp, w_out, DC, lambda c, b0, bn: mg[:, c, b0:b0 + bn], [Bmg], blocks_ext, oT, ones, Bo, "o")
                P.phase_end()
            rstd1, Br1 = rstd_from(ph, ssq, blocks_ext, TE, epsT, Be, "rstd1")
            ssq2 = ssq_tiles(ph, len(blocks_ext), "h")
            with ExitStack() as p3:
                op_ = sb_pool(p3, 2, "o_in", [128, TE], F32)
                xp_ = sb_pool(p3, 2, "x_in", [128, TE], F32)
                sqp = sb_pool(p3, 3, "hsq", [128, 512], BF16)
                hb_ = sb_pool(p3, 2, "h_bf", [128, TE], BF16)
                pending = []

                def load_r(m):
                    o, Bo_ = op_.next()
                    x, Bx_ = xp_.next()
                    P.dma("sp", o[:], oT[m * 128:(m + 1) * 128, :], w=[Bo_])
                    P.dma("sp", x[:], xT[m * 128:(m + 1) * 128, 0:TE], w=[Bx_])
                    return (o, Bo_, x, Bx_)

                nxt = load_r(0)
                for m in range(DC):
                    o, Bo_, x, Bx_ = nxt
                    if m + 1 < DC:
                        nxt = load_r(m + 1)
                    P.stt("dve", o[:], o[:], gn[:, 1, m:m + 1], rstd1[:], ALU.mult, ALU.mult, r=[Bo_, Bg, Br1], w=[Bo_])
                    P.tt("pool", o[:], o[:], x[:], ALU.add, r=[Bo_, Bx_], w=[Bo_])
                    P.dma("sp", h1T[m * 128:(m + 1) * 128, :], o[:], r=[Bo_])
                    hb, Bhb = hb_.next()
                    P.act(hb[:], o[:], AF.Copy, r=[Bo_, Bg], w=[Bhb], scale=gn[:, 2, m:m + 1])
                    P.dma("sp", hn2T[m * 128:(m + 1) * 128, :], hb[:], r=[Bhb])
                    for bi, (b0, bn) in enumerate(blocks_ext):
                        sq, Bsq = sqp.next()
                        P.act(sq[:, 0:bn], o[:, b0:b0 + bn], AF.Square, r=[Bo_], w=[Bsq])
                        while len(pending) > 2:
                            pending.pop(0)()
                        sp_, spB = ssq2[bi]
                        pending.append(lambda sp_=sp_, spB=spB, sq=sq, Bsq=Bsq, bn=bn, m=m:
                                       P.mm(sp_[:, 0:bn], ones[:], sq[:, 0:bn], m == 0, m == DC - 1, r=[Bo, Bsq], w=[spB]))
                while pending:
                    pending.pop(0)()
                P.phase_end()
            rstd2, Br2 = rstd_from(ph, ssq2, blocks_ext, TE, epsT, Be, "rstd2")
            P.dma("sp", rs2T, rstd2[:], r=[Br2])
            P.phase_end()

        if cfg.get('stop') == 'D':
            return nc
        with ExitStack() as ph:
            hn2 = ph.enter_context(SBT("hn2", [128, DC, TE], BF16))
            Bh2 = Buf("hn2")
            P.dma("sp", hn2[:], hn2T.rearrange("(c p) t -> p c t", p=128), w=[Bh2])
            cw = ph.enter_context(SBT("cw", [128, 3, FC], F32))
            cb = ph.enter_context(SBT("cb", [128, FC], F32))
            Bcw = Buf("cw")
            P.dma("sp", cw[:], conv_w, w=[Bcw])
            P.dma("sp", cb[:], conv_b, w=[Bcw])
            rs2 = ph.enter_context(SBT("rs2", [128, TE], F32))
            Brs2 = Buf("rs2")
            P.dma("sp", rs2[:], rs2T, w=[Brs2])
            wpool = sb_pool(ph, 4, "wup", [128, DC, 128], BF16)
            pspool = Rot(psum_pool(ph, 8, "pu"))
            Ap = sb_pool(ph, 2, "Asb", [128, TE + 2], F32)
            for (a_, Ba_) in Ap.items:
                P.memset("pool", a_[:], 0.0, w=[Ba_])
            cp_ = sb_pool(ph, 2, "cv", [128, TO], F32)
            gp_ = sb_pool(ph, 2, "gl", [128, TO], F32)
            op_ = sb_pool(ph, 2, "actst", [128, TO], BF16)

            def load_w(j):
                wa_, BwA = wpool.next()
                wb_, BwB = wpool.next()
                P.dma("pool", wa_[:], w_up[j], w=[BwA])
                P.dma("pool", wb_[:], w_up[FC + j], w=[BwB])
                return (wa_, BwA, wb_, BwB)

            nxt = load_w(0)
            for j in range(FC):
                wa_, BwA, wb_, BwB = nxt
                if j + 1 < FC:
                    nxt = load_w(j + 1)
                A, BA = Ap.next()
                for (b0, bn) in blocks_ext:
                    ps, psB = pspool.next()
                    for c in range(DC):
                        P.mm(ps[:, 0:bn], wa_[:, c, :], hn2[:, c, b0:b0 + bn], c == 0, c == DC - 1, r=[BwA, Bh2], w=[psB])
                    P.tt("dve", A[:, 1 + b0:1 + b0 + bn], ps[:, 0:bn], rs2[:, b0:b0 + bn], ALU.mult, r=[psB, Brs2], w=[BA])
                bps = []
                for (b0, bn) in blocks_own:
                    ps, psB = pspool.next()
                    for c in range(DC):
                        P.mm(ps[:, 0:bn], wb_[:, c, :], hn2[:, c, b0:b0 + bn], c == 0, c == DC - 1, r=[BwB, Bh2], w=[psB])
                    bps.append((ps, psB, b0, bn))
                cv, Bcv = cp_.next()
                gl, Bgl = gp_.next()
                ot, Bot = op_.next()
                P.ts("dve", cv[:], A[:, 0:TO], cw[:, 0, j:j + 1], None, ALU.mult, r=[BA, Bcw], w=[Bcv])
                P.stt("dve", cv[:], A[:, 1:TO + 1], cw[:, 1, j:j + 1], cv[:], ALU.mult, ALU.add, r=[BA, Bcw, Bcv], w=[Bcv])
                P.stt("dve", cv[:], A[:, 2:TO + 2], cw[:, 2, j:j + 1], cv[:], ALU.mult, ALU.add, r=[BA, Bcw, Bcv], w=[Bcv])
                P.act(gl[:], cv[:], AF.Gelu_apprx_tanh, r=[Bcv, Bcw], w=[Bgl], bias=cb[:, j:j + 1])
                P.tt("pool", gl[:], gl[:], rs2[:, 0:TO], ALU.mult, r=[Bgl, Brs2], w=[Bgl])
                for (ps, psB, b0, bn) in bps:
                    P.tt("dve", ot[:, b0:b0 + bn], ps[:, 0:bn], gl[:, b0:b0 + bn], ALU.mult, r=[psB, Bgl], w=[Bot])
                P.dma("sp", actT[j * 128:(j + 1) * 128, :], ot[:], r=[Bot])
            P.phase_end()

        if cfg.get('stop') == 'E':
            return nc
        with ExitStack() as ph:
            ones, Bo, epsT, Be = make_consts(ph)
            gn = ph.enter_context(SBT("gn", [128, 5, DC], F32))
            Bg = Buf("gn")
            P.dma("sp", gn[:], gains, w=[Bg])
            NBK = len(blocks_own)
            BW = max(bn for _, bn in blocks_own)
            ssqF = [(ph.enter_context(PST("ssqF%d" % i, [128, 512], F32)), Buf("ssqF", True)) for i in range(NBK)]
            rsF = [(ph.enter_context(SBT("rsF%d" % i, [128, BW], F32)), Buf("rsF")) for i in range(NBK)]
            aS = ph.enter_context(SBT("aS", [128, FC, BW], BF16))
            BaS = Buf("aS")
            wpool = sb_pool(ph, 3, "wdn", [128, FC, 128], BF16)
            pspool = Rot(psum_pool(ph, 4, "pf"))
            ost = sb_pool(ph, 2, "ostF", [128, BW], F32)
            sqp = sb_pool(ph, 3, "osqF", [128, 512], BF16)
            dp_ = sb_pool(ph, 3, "d_in", [128, BW], F32)
            hp_ = sb_pool(ph, 3, "h_in", [128, BW], F32)
            BdT = {}
            actTv = actT.rearrange("(c p) t -> p c t", p=128)

            def f2_load(bi, m):
                b0, bn = blocks_own[bi]
                d_, Bd_ = dp_.next()
                h_, Bh_ = hp_.next()
                P.dma("sp", d_[:, 0:bn], dT[m * 128:(m + 1) * 128, b0:b0 + bn], r=[BdT[(bi, m)]], w=[Bd_])
                P.dma("sp", h_[:, 0:bn], h1T[m * 128:(m + 1) * 128, b0:b0 + bn], w=[Bh_])
                return (d_, Bd_, h_, Bh_)

            def f2_fin(bi, m, ld):
                b0, bn = blocks_own[bi]
                d_, Bd_, h_, Bh_ = ld
                rs, Brs = rsF[bi]
                P.stt("dve", d_[:, 0:bn], d_[:, 0:bn], gn[:, 3, m:m + 1], rs[:, 0:bn], ALU.mult, ALU.mult, r=[Bd_, Bg, Brs], w=[Bd_])
                P.tt("pool", d_[:, 0:bn], d_[:, 0:bn], h_[:, 0:bn], ALU.add, r=[Bd_, Bh_], w=[Bd_])
                P.dma("sp", h2T[m * 128:(m + 1) * 128, b0:b0 + bn], d_[:, 0:bn], r=[Bd_])

            for bi, (b0, bn) in enumerate(blocks_own):
                P.dma("sp", aS[:, :, 0:bn], actTv[:, :, b0:b0 + bn], w=[BaS])
                pending = []
                state = {}
                sp, spB = ssqF[bi]

                def ep(m, sub, b0_, bn_, ps, psB, sp=sp, spB=spB):
                    o, Bos = ost.next()
                    state["o"] = (o, Bos)
                    P.copy("dve", o[:, 0:bn_], ps[:, 0:bn_], r=[psB], w=[Bos])
                    sq, Bsq = sqp.next()
                    P.act(sq[:, 0:bn_], ps[:, 0:bn_], AF.Square, r=[psB], w=[Bsq])
                    while pending:
                        pending.pop(0)()
                    pending.append(lambda: P.mm(sp[:, 0:bn_], ones[:], sq[:, 0:bn_], m == 0, m == DC - 1, r=[Bo, Bsq], w=[spB]))

                def post(m, bi=bi, b0=b0, bn=bn):
                    o, Bos = state["o"]
                    BdT[(bi, m)] = Buf("dT")
                    P.dma("sp", dT[m * 128:(m + 1) * 128, b0:b0 + bn], o[:, 0:bn], r=[Bos], w=[BdT[(bi, m)]])
                    if bi > 0:
                        if "ld" in state:
                            f2_fin(bi - 1, m - 1, state.pop("ld"))
                        state["ld"] = f2_load(bi - 1, m)

                linear(w_down, range(DC), FC, 128, lambda c, b0_, bn_: aS[:, c, 0:bn_], [BaS], lambda m, b0=b0, bn=bn: [(b0, bn)],
                       ep, wpool, pspool, post)
                while pending:
                    pending.pop(0)()
                if "ld" in state:
                    f2_fin(bi - 1, DC - 1, state.pop("ld"))
                rs, Brs = rsF[bi]
                P.act(rs[:, 0:bn], sp[:, 0:bn], AF.Sqrt, r=[spB, Be], w=[Brs], bias=epsT[:, 0:1], scale=1.0 / D)
                P.recip(rs[:, 0:bn], rs[:, 0:bn], r=[Brs], w=[Brs])
            last = NBK - 1
            ld = f2_load(last, 0)
            for m in range(DC):
                nx = f2_load(last, m + 1) if m + 1 < DC else None
                f2_fin(last, m, ld)
                ld = nx
            P.phase_end()

        if cfg.get('stop') == 'F':
            return nc
        with ExitStack() as ph:
            ones, Bo, epsT, Be = make_consts(ph)
            gn = ph.enter_context(SBT("gn", [128, 5, DC], F32))
            Bg = Buf("gn")
            P.dma("sp", gn[:], gains, w=[Bg])
            h2b = ph.enter_context(SBT("h2b", [128, DC, TO], BF16))
            Bh2b = Buf("h2b")
            P.dma("pool", h2b[:], h2T.rearrange("(c p) t -> p c t", p=128), w=[Bh2b])
            pTb = ph.enter_context(SBT("pTb", [128, PC, TO], BF16))
            BpT = Buf("pTb")
            P.dma("pool", pTb[:], pT.rearrange("(c p) t -> p c t", p=128), w=[BpT])
            wpl = ph.enter_context(SBT("wpl", [128, DC, PC, 128], BF16))
            Bwpl = Buf("wpl")
            P.dma("pool", wpl[:], w_ple.rearrange("m p c j -> p m c j"), w=[Bwpl])
            ssq = ssq_tiles(ph, len(blocks_own), "p")
            pspool = Rot(psum_pool(ph, 5, "pp"))
            sqp = sb_pool(ph, 3, "psq", [128, 512], BF16)
            pending = []
            for m in range(DC):
                for bi, (b0, bn) in enumerate(blocks_own):
                    ps, psB = pspool.next()
                    for c in range(PC):
                        P.mm(ps[:, 0:bn], wpl[:, m, c, :], pTb[:, c, b0:b0 + bn], c == 0, c == PC - 1, r=[Bwpl, BpT], w=[psB])
                    sq, Bsq = sqp.next()
                    P.act(sq[:, 0:bn], ps[:, 0:bn], AF.Square, r=[psB], w=[Bsq])
                    while len(pending) > 1:
                        pending.pop(0)()
                    sp_, spB = ssq[bi]
                    pending.append(lambda sp_=sp_, spB=spB, sq=sq, Bsq=Bsq, bn=bn, m=m:
                                   P.mm(sp_[:, 0:bn], ones[:], sq[:, 0:bn], m == 0, m == DC - 1, r=[Bo, Bsq], w=[spB]))
            while pending:
                pending.pop(0)()
            rstdp, Brp = rstd_from(ph, ssq, blocks_own, TO, epsT, Be, "rstdp")
            wpool = sb_pool(ph, 3, "wpg", [128, DC, 128], BF16)
            hp_ = sb_pool(ph, 2, "h2in", [128, TO], F32)
            sgp = sb_pool(ph, 2, "sgp", [128, 512], F32)
            tp_ = sb_pool(ph, 2, "tpl", [128, 512], F32)
            state = {}

            def ep_pg(m, sub, b0, bn, ps, psB):
                if b0 == 0:
                    h_, Bh_ = hp_.next()
                    P.dma("sp", h_[:], h2T[m * 128:(m + 1) * 128, :], w=[Bh_])
                    state["h"] = (h_, Bh_)
                h_, Bh_ = state["h"]
                ps2, ps2B = pspool.next()
                for c in range(PC):
                    P.mm(ps2[:, 0:bn], wpl[:, m, c, :], pTb[:, c, b0:b0 + bn], c == 0, c == PC - 1, r=[Bwpl, BpT], w=[ps2B])
                sg, Bsg = sgp.next()
                tp, Btp = tp_.next()
                P.act(sg[:, 0:bn], ps[:, 0:bn], AF.Sigmoid, r=[psB], w=[Bsg])
                P.tt("dve", tp[:, 0:bn], ps2[:, 0:bn], rstdp[:, b0:b0 + bn], ALU.mult, r=[ps2B, Brp], w=[Btp])
                P.tt("pool", tp[:, 0:bn], tp[:, 0:bn], sg[:, 0:bn], ALU.mult, r=[Btp, Bsg], w=[Btp])
                P.stt("dve", h_[:, b0:b0 + bn], tp[:, 0:bn], gn[:, 4, m:m + 1], h_[:, b0:b0 + bn], ALU.mult, ALU.add,
                      r=[Btp, Bg, Bh_], w=[Bh_])

            def post_pg(m):
                h_, Bh_ = state["h"]
                P.dma("sp", outT[m * 128:(m + 1) * 128, :], h_[:], r=[Bh_])

            linear(w_pg, range(DC), DC, 128, lambda c, b0, bn: h2b[:, c, b0:b0 + bn], [Bh2b], lambda m: blocks_own,
                   ep_pg, wpool, pspool, post_pg)
            P.phase_end()
    return nc


def tile_w(W, mw=128):
    K, N = W.shape
    return np.ascontiguousarray(W.reshape(K // 128, 128, N // mw, mw).transpose(2, 1, 0, 3))


def pvec(v):
    return np.ascontiguousarray(np.asarray(v, np.float32).reshape(-1, 128).T)


def make_constants(cfg):
    T = cfg["T"]
    TO = T // 2
    TE = TO + HALO
    c = {}
    j = np.arange(8, dtype=np.float32)
    n1 = np.zeros((128, 8), np.float32)
    n1[:64] = 7 - j
    n1[64:] = j
    nt = np.zeros((128, 26), np.float32)
    nt[:, 0:8] = n1
    nt[:, 8:16] = -n1
    nt[:, 16:24] = 8 - n1
    nt[:, 24] = 1
    nt[:, 25] = 8
    c["c_ntab"] = nt
    jj = np.arange(128) // 16
    mk = np.zeros((128, 2, 128), np.float32)
    mk[:, 0, :] = (jj[:, None] <= jj[None, :])
    mk[:, 1, :] = (jj[:, None] >= jj[None, :])
    c["c_mask"] = mk
    c["c_ident"] = np.eye(128, dtype=np.float32)
    pm = np.zeros((128, 128), np.float32)
    pm[(np.arange(128) + 64) % 128, np.arange(128)] = 1
    c["c_perm"] = pm
    sel = np.zeros((128, 64, 128), np.float32)
    selT = np.zeros((128, 64, 128), np.float32)
    for gl in range(8):
        for jx in range(8):
            for cc in range(16):
                sel[16 * gl + cc, gl * 8 + jx, 16 * jx + cc] = 1
                selT[16 * jx + cc, gl * 8 + jx, 16 * gl + cc] = 1
    c["c_sel"] = sel
    c["c_selT"] = selT
    X0 = T - 128
    MW = T - 128 + TE
    delta = np.arange(128)[:, None] - (np.arange(MW)[None, :] - X0)
    w = np.zeros_like(delta, dtype=np.float32)
    for (win, dil) in cfg["patterns"]:
        ns = win // (2 * dil)
        w += ((delta % dil == 0) & (np.abs(delta) <= ns * dil)).astype(np.float32)
    c["c_mtab"] = w
    return c


def prep_core_inputs(cfg, inp, b, half, shared):
    D, T, H, G = cfg["D"], cfg["T"], cfg["H"], cfg["G"]
    TO = T // 2
    idx = np.arange(T) if half == 0 else np.arange(T)[::-1]
    m = dict(shared)
    m["xT"] = np.ascontiguousarray(inp["x"][b][idx].T)
    m["xTt"] = np.ascontiguousarray(m["xT"].reshape(D // 128, 128, T // 128, 128).transpose(2, 1, 0, 3))
    m["pT"] = np.ascontiguousarray(inp["p"][0, b][idx[:TO]].T)
    inv_freq = (10000.0 ** (-np.arange(0, 128, 2, dtype=np.float32) / 128)).astype(np.float32)
    ang = idx.astype(np.float32)[:, None] * inv_freq[None, :]
    ang = np.concatenate([ang, ang], axis=-1)
    sgn = np.concatenate([-np.ones(64, np.float32), np.ones(64, np.float32)])
    m["cosT"] = np.ascontiguousarray(np.cos(ang).T.astype(np.float32))
    m["sinT"] = np.ascontiguousarray((np.sin(ang) * sgn[None, :]).T.astype(np.float32))
    dd = [0, 1] if half == 0 else [1, 0]
    lam = np.zeros((128, 3, G), np.float32)
    sB = np.zeros((128, 2, G, 16), np.float32)
    sC = np.zeros((128, 2, G, 16), np.float32)
    for d in range(2):
        s_ = dd[d]
        lam[d * 64:(d + 1) * 64, 0] = inp["ssm_lambda_re"][0, s_].T
        lam[d * 64:(d + 1) * 64, 1] = inp["ssm_lambda_im"][0, s_].T
        lam[d * 64:(d + 1) * 64, 2] = inp["ssm_log_dt"][0, s_][None, :]
        sB[d * 64:(d + 1) * 64, 0] = inp["ssm_b_re"][0, s_].transpose(1, 0, 2)
        sB[d * 64:(d + 1) * 64, 1] = inp["ssm_b_im"][0, s_].transpose(1, 0, 2)
        sC[d * 64:(d + 1) * 64, 0] = inp["ssm_c_re"][0, s_].transpose(2, 0, 1)
        sC[d * 64:(d + 1) * 64, 1] = inp["ssm_c_im"][0, s_].transpose(2, 0, 1)
    m["s_lam"], m["s_B"], m["s_C"] = lam, sB, sC
    cwv = inp["conv_w"][0]
    if half == 1:
        cwv = cwv[::-1]
    m["conv_w"] = np.ascontiguousarray(np.stack([pvec(cwv[t]) for t in range(3)], axis=1))
    return m


def prep_shared(cfg, inp):
    D, T, H, G, DFF = cfg["D"], cfg["T"], cfg["H"], cfg["G"], cfg["DFF"]
    AW, SW = H * 128, G * 16
    sh = dict(make_constants(cfg))
    w_in = inp["w_in"][0]
    wq, wk, wv = w_in[:, 0:AW], w_in[:, AW:2 * AW], w_in[:, 2 * AW:3 * AW]
    wu = w_in[:, 3 * AW:3 * AW + SW]
    wg = w_in[:, 3 * AW + SW:]
    sh["w_qk"] = tile_w(np.concatenate([wq, wk], axis=1))
    sh["w_v"] = tile_w(wv, min(AW, 512))
    sh["w_u"] = tile_w(wu)
    sh["w_g"] = tile_w(wg)
    sh["w_ba"] = tile_w(inp["w_branch_attn"][0])
    sh["w_bs"] = tile_w(inp["w_branch_ssm"][0])
    sh["w_out"] = tile_w(inp["w_out"][0])
    sh["w_glu"] = tile_w(inp["w_glu"][0])
    sh["b_glu"] = pvec(inp["b_glu"][0])
    sh["w_up"] = tile_w(inp["w_up"][0])
    sh["conv_b"] = pvec(inp["conv_b"][0])
    sh["w_down"] = tile_w(inp["w_down"][0])
    sh["w_ple"] = tile_w(inp["w_ple"][0])
    sh["w_pg"] = tile_w(inp["w_ple_gate"][0])
    sh["gains"] = np.ascontiguousarray(np.stack(
        [pvec(inp[k][0]) for k in ("g_mix_pre", "g_mix_post", "g_ffn_pre", "g_ffn_post", "g_ple")], axis=1))
    sh["s_D"] = np.ascontiguousarray(np.tile(inp["ssm_d"][0].reshape(G, 16).T, (8, 1)))
    return sh


def run_cfg(cfg, inp, dbg=False):
    inp = {k: np.asarray(v, np.float32) for k, v in inp.items()}
    nb = cfg["nb"]
    T, D = cfg["T"], cfg["D"]
    TO = T // 2
    shared = prep_shared(cfg, inp)
    in_maps = []
    for b in range(nb):
        for half in range(2):
            in_maps.append(prep_core_inputs(cfg, inp, b, half, shared))
    nc = build_program(cfg, dbg=dbg)
    res = run_bass_kernel_spmd(nc, in_maps, core_ids=list(range(2 * nb)))
    out = np.zeros((nb, T, D), np.float32)
    for b in range(nb):
        for half in range(2):
            idx = np.arange(T) if half == 0 else np.arange(T)[::-1]
            out[b, idx[:TO]] = res.results[b * 2 + half]["outT"].T
    return out, res


def kernel(**inputs):
    out, _ = run_cfg(FULL_CFG, inputs)
    return out
```

```python
import math
from contextlib import ExitStack
import numpy as np
import concourse.bass as bass
import concourse.mybir as mybir
from concourse.bass_utils import run_bass_kernel_spmd

F32 = mybir.dt.float32
BF16 = mybir.dt.bfloat16
I32 = mybir.dt.int32
ALU = mybir.AluOpType
AF = mybir.ActivationFunctionType

NDS = 40
EPS = 1e-6
HALO = 8
TWO_PI = float(2 * np.pi)

FULL_CFG = dict(D=4096, T=2048, H=16, G=128, DFF=12288, PLE=256,
                patterns=((128, 1), (512, 4), (2048, 16)), nb=4)


class Buf:
    __slots__ = ("name", "w", "r", "excl")

    def __init__(self, name="", excl=False):
        self.name = name
        self.w = None
        self.r = []
        self.excl = excl


class Sched:
    ENG = ("pe", "dve", "act", "pool", "sp")

    def __init__(self, nc, stack):
        self.nc = nc
        self.csem = {k: stack.enter_context(nc.semaphore("c_" + k)) for k in self.ENG}
        self.cnt = {k: 0 for k in self.ENG}
        self.dsem = [stack.enter_context(nc.semaphore("d%d" % i)) for i in range(NDS)]
        self.dcnt = [0] * NDS
        self.dnext = 0
        self.waited = {}
        self.ops = {k: [] for k in self.ENG}

    def _deps(self, eng, reads, writes):
        evs = []
        for b in reads:
            if b.w is not None:
                evs.append(b.w)
            if b.excl:
                evs.extend(b.r)
        for b in writes:
            if b.w is not None:
                evs.append(b.w)
            evs.extend(b.r)
        waits = {}
        for (key, sem, val, e) in evs:
            if e == eng and eng == "pe":
                continue
            if self.waited.get((eng, key), 0) >= val:
                continue
            if waits.get(key, (None, 0))[1] < val:
                waits[key] = (sem, val)
        for key, (sem, val) in waits.items():
            self.waited[(eng, key)] = val
        return list(waits.values())

    def _record(self, ev, reads, writes):
        for b in reads:
            b.r.append(ev)
            if len(b.r) > 24:
                b.r = b.r[-24:] if False else b.r
        for b in writes:
            b.w = ev
            b.r = []

    def op(self, eng, fn, reads=(), writes=()):
        waits = self._deps(eng, reads, writes)
        self.cnt[eng] += 1
        ev = ("c_" + eng, self.csem[eng], self.cnt[eng], eng)
        self._record(ev, reads, writes)
        self.ops[eng].append((waits, fn, self.csem[eng], 1))

    def dma(self, eng, out, in_, reads=(), writes=()):
        i = self.dnext
        self.dnext = (self.dnext + 1) % NDS
        waits = self._deps(eng, reads, writes)
        key = "d%d" % i
        if self.dcnt[i] > 0 and self.waited.get((eng, key), 0) < self.dcnt[i]:
            waits.append((self.dsem[i], self.dcnt[i]))
            self.waited[(eng, key)] = self.dcnt[i]
        self.dcnt[i] += 16
        ev = (key, self.dsem[i], self.dcnt[i], "dma")
        self._record(ev, reads, writes)

        def fn(e, out=out, in_=in_):
            return e.dma_start(out=out, in_=in_)

        self.ops[eng].append((waits, fn, self.dsem[i], 16))

    def barrier(self):
        for eng in self.ENG:
            waits = []
            for k in self.ENG:
                if k == eng or self.cnt[k] == 0:
                    continue
                key = "c_" + k
                if self.waited.get((eng, key), 0) < self.cnt[k]:
                    waits.append((self.csem[k], self.cnt[k]))
                    self.waited[(eng, key)] = self.cnt[k]
            for i in range(NDS):
                key = "d%d" % i
                if self.dcnt[i] > 0 and self.waited.get((eng, key), 0) < self.dcnt[i]:
                    waits.append((self.dsem[i], self.dcnt[i]))
                    self.waited[(eng, key)] = self.dcnt[i]
            if waits:
                self.ops[eng].append((waits, None, None, 0))

    def emit(self):
        nc = self.nc
        ops = self.ops
        self.ops = {k: [] for k in self.ENG}

        def run(e, lst):
            for (waits, fn, sem, inc) in lst:
                for (sm, v) in waits:
                    e.wait_ge(sm, v)
                if fn is not None:
                    fn(e).then_inc(sem, inc)

        with nc.Block() as block:
            if ops["pe"]:
                @block.tensor
                def _(e):
                    run(e, ops["pe"])
            if ops["dve"]:
                @block.vector
                def _(e):
                    run(e, ops["dve"])
            if ops["act"]:
                @block.scalar
                def _(e):
                    run(e, ops["act"])
            if ops["pool"]:
                @block.gpsimd
                def _(e):
                    run(e, ops["pool"])
            if ops["sp"]:
                @block.sync
                def _(e):
                    run(e, ops["sp"])


class Prog:
    def __init__(self, nc, s):
        self.nc = nc
        self.s = s
        self.rr = 0

    def mm(self, out, lhsT, rhs, start, stop, r=(), w=()):
        self.s.op("pe", lambda e: e.matmul(out, lhsT=lhsT, rhs=rhs, start=start, stop=stop), r, w)

    def act(self, out, in_, func, r=(), w=(), bias=None, scale=1.0):
        if bias is None:
            self.s.op("act", lambda e: e.activation(out=out, in_=in_, func=func, scale=scale), r, w)
        else:
            self.s.op("act", lambda e: e.activation(out=out, in_=in_, func=func, bias=bias, scale=scale), r, w)

    def tt(self, eng, out, in0, in1, op, r=(), w=()):
        self.s.op(eng, lambda e: e.tensor_tensor(out=out, in0=in0, in1=in1, op=op), r, w)

    def ts(self, eng, out, in0, s1, s2, op0, op1=None, r=(), w=()):
        if op1 is None:
            self.s.op(eng, lambda e: e.tensor_scalar(out=out, in0=in0, scalar1=s1, scalar2=None, op0=op0), r, w)
        else:
            self.s.op(eng, lambda e: e.tensor_scalar(out=out, in0=in0, scalar1=s1, scalar2=s2, op0=op0, op1=op1), r, w)

    def stt(self, eng, out, in0, scalar, in1, op0, op1, r=(), w=()):
        self.s.op(eng, lambda e: e.scalar_tensor_tensor(out=out, in0=in0, scalar=scalar, in1=in1, op0=op0, op1=op1), r, w)

    def copy(self, eng, out, in_, r=(), w=()):
        if eng == "act":
            self.s.op("act", lambda e: e.activation(out=out, in_=in_, func=AF.Copy), r, w)
        else:
            self.s.op(eng, lambda e: e.tensor_copy(out=out, in_=in_), r, w)

    def recip(self, out, in_, r=(), w=()):
        self.s.op("dve", lambda e: e.reciprocal(out=out, in_=in_), r, w)

    def memset(self, eng, ap, val, w=()):
        self.s.op(eng, lambda e: e.memset(ap, val), (), w)

    def dma(self, eng, out, in_, r=(), w=()):
        self.s.dma(eng, out, in_, r, w)

    def phase_end(self):
        self.s.barrier()
        self.s.emit()


def split_blocks(n, bs=512):
    out = []
    t = 0
    while t < n:
        w = min(bs, n - t)
        out.append((t, w))
        t += w
    return out


def bcast_ap(t, offset, dims, rowlen, nparts=128, pbase=0):
    return bass.AP(t, pbase * rowlen + offset, [[rowlen, nparts]] + [[a, b] for (a, b) in dims])


def build_program(cfg, dbg=False):
    D, T, H, G, DFF, PLE = cfg["D"], cfg["T"], cfg["H"], cfg["G"], cfg["DFF"], cfg["PLE"]
    TO = T // 2
    TE = TO + HALO
    DC = D // 128
    AW = H * 128
    SW = G * 16
    SC = SW // 128
    AC = AW // 128
    FC = DFF // 128
    PC = PLE // 128
    NK = T // 8
    NKO = TO // 8 + 1
    GH = min(G, 64)
    blocks_ext = split_blocks(TO) + [(TO, HALO)]
    blocks_own = split_blocks(TO)
    blocks_full = split_blocks(T)
    WMAX = max(w // 2 for (w, d) in cfg["patterns"])
    X0 = T - 128
    MW = T - 128 + TE
    att_scale = 128 ** -0.5

    nc = bass.Bass("TRN2", target_bir_lowering=False)
    uid = [0]

    def SBT(name, shape, dt):
        uid[0] += 1
        return nc.sbuf_tensor("%s_%d" % (name, uid[0]), shape, dt)

    def PST(name, shape, dt):
        uid[0] += 1
        return nc.psum_tensor("%s_%d" % (name, uid[0]), shape, dt)

    def din(name, shape, dt=F32):
        return nc.dram_tensor(name, list(shape), dt, kind="ExternalInput").ap()

    def dscr(name, shape, dt):
        return nc.dram_tensor(name, list(shape), dt, kind=("ExternalOutput" if dbg else "Internal")).ap()

    xT = din("xT", [D, T])
    xTt = din("xTt", [T // 128, 128, DC, 128])
    pT = din("pT", [PLE, TO])
    cosT = din("cosT", [128, T])
    sinT = din("sinT", [128, T])
    gains = din("gains", [128, 5, DC])
    w_qk = din("w_qk", [2 * H, 128, DC, 128])
    w_u = din("w_u", [SC, 128, DC, 128])
    w_g = din("w_g", [2 * DC, 128, DC, 128])
    w_v = din("w_v", [AW // 512 if AW >= 512 else 1, 128, DC, min(AW, 512)])
    w_ba = din("w_ba", [DC, 128, AC, 128])
    w_bs = din("w_bs", [DC, 128, SC, 128])
    w_out = din("w_out", [DC, 128, DC, 128])
    w_glu = din("w_glu", [SC, 128, SC, 128])
    b_glu = din("b_glu", [128, SC])
    w_up = din("w_up", [2 * FC, 128, DC, 128])
    conv_w = din("conv_w", [128, 3, FC])
    conv_b = din("conv_b", [128, FC])
    w_down = din("w_down", [DC, 128, FC, 128])
    w_ple = din("w_ple", [DC, 128, PC, 128])
    w_pg = din("w_pg", [DC, 128, DC, 128])
    s_lam = din("s_lam", [128, 3, G])
    s_B = din("s_B", [128, 2, G, 16])
    s_C = din("s_C", [128, 2, G, 16])
    s_D = din("s_D", [128, G])
    c_ntab = din("c_ntab", [128, 26])
    c_mask = din("c_mask", [128, 2, 128])
    c_ident = din("c_ident", [128, 128])
    c_perm = din("c_perm", [128, 128])
    c_sel = din("c_sel", [128, 64, 128])
    c_selT = din("c_selT", [128, 64, 128])
    c_mtab = din("c_mtab", [128, MW])

    outT = nc.dram_tensor("outT", [D, TO], F32, kind="ExternalOutput").ap()

    qT = dscr("qT", [AW, TE], BF16)
    kT = dscr("kT", [AW, T], BF16)
    vS = dscr("vS", [T, AW], BF16)
    uT = dscr("uT", [SW, T], BF16)
    sgT = dscr("sgT", [2 * D, TE], BF16)
    yaT = dscr("yaT", [AW, TE], BF16)
    zT = dscr("zT", [SW, TE], BF16)
    oT = dscr("oT", [D, TE], F32)
    h1T = dscr("h1T", [D, TE], F32)
    actT = dscr("actT", [DFF, TO], BF16)
    dT = dscr("dT", [D, TO], F32)
    h2T = dscr("h2T", [D, TO], F32)
    hn2T = dscr("hn2T", [D, TE], BF16)
    rs2T = dscr("rs2T", [128, TE], F32)

    with ExitStack() as top:
        s = Sched(nc, top)
        P = Prog(nc, s)

        def rr_eng(engs=("dve", "act", "pool")):
            P.rr += 1
            return engs[P.rr % len(engs)]

        def make_consts(ph):
            ones = ph.enter_context(SBT("ones", [128, 128], BF16))
            Bo = Buf("ones")
            P.memset("pool", ones[:], 1.0, w=[Bo])
            epsT = ph.enter_context(SBT("epsT", [128, 1], F32))
            Be = Buf("eps")
            P.memset("pool", epsT[:], EPS, w=[Be])
            return ones, Bo, epsT, Be

        def psum_pool(ph, n, name="ps"):
            tiles = []
            for i in range(n):
                t = ph.enter_context(PST("%s%d" % (name, i), [128, 512], F32))
                tiles.append((t, Buf("%s%d" % (name, i), True)))
            return tiles

        class Rot:
            def __init__(self, items):
                self.items = items
                self.i = 0

            def next(self):
                it = self.items[self.i % len(self.items)]
                self.i += 1
                return it

        def sb_pool(ph, n, name, shape, dt):
            return Rot([(ph.enter_context(SBT("%s%d" % (name, i), shape, dt)), Buf("%s%d" % (name, i)))
                        for i in range(n)])

        def linear(wd, m_list, KC, mw, rhs_fn, rhs_bufs, blocks_of, epilogue, wpool, pspool, post_tile=None):
            tiles = {}

            def load(m):
                wt, wB = wpool.next()
                P.dma("pool", wt[:], wd[m], w=[wB])
                tiles[m] = (wt, wB)

            m_list = list(m_list)
            load(m_list[0])
            for mi, m in enumerate(m_list):
                if mi + 1 < len(m_list):
                    load(m_list[mi + 1])
                wt, wB = tiles.pop(m)
                for (b0, bn) in blocks_of(m):
                    for sub in range(mw // 128):
                        ps, psB = pspool.next()
                        for c in range(KC):
                            P.mm(ps[:, 0:bn], wt[:, c, sub * 128:(sub + 1) * 128], rhs_fn(c, b0, bn),
                                 c == 0, c == KC - 1, r=[wB] + list(rhs_bufs), w=[psB])
                        epilogue(m, sub, b0, bn, ps, psB)
                if post_tile is not None:
                    post_tile(m)

        def rstd_from(ph, ssq_list, blocks, TN, epsT, Be, name):
            rstd = ph.enter_context(SBT(name, [128, TN], F32))
            B = Buf(name)
            base = blocks[0][0]
            for (sp, spB), (b0, bn) in zip(ssq_list, blocks):
                P.act(rstd[:, b0 - base:b0 - base + bn], sp[:, 0:bn], AF.Sqrt, r=[spB, Be], w=[B], bias=epsT[:, 0:1], scale=1.0 / D)
            P.recip(rstd[:], rstd[:], r=[B], w=[B])
            return rstd, B

        def proj_ssq(ph, tmp, wd, KC, rhs_fn, rhs_bufs, blocks, dst_dram, ones, Bo, tag):
            ssq = [(ph.enter_context(PST("ssq%s%d" % (tag, i), [128, 512], F32)), Buf("ssq", True)) for i in range(len(blocks))]
            wpool = sb_pool(tmp, 3, "wo" + tag, [128, KC, 128], BF16)
            pspool = Rot(psum_pool(tmp, 4, "po" + tag))
            TN = sum(bn for _, bn in blocks)
            base = blocks[0][0]
            ost = sb_pool(tmp, 2, "ost" + tag, [128, TN], F32)
            sqp = sb_pool(tmp, 3, "osq" + tag, [128, 512], BF16)
            pending = []
            state = {}
            n_mt = wd.shape[0]

            def flush():
                while pending:
                    pending.pop(0)()

            def ep(m, sub, b0, bn, ps, psB):
                if b0 == base:
                    state["o"] = ost.next()
                o, Bos = state["o"]
                bi = [b for b, _ in blocks].index(b0)
                P.copy("dve", o[:, b0 - base:b0 - base + bn], ps[:, 0:bn], r=[psB], w=[Bos])
                sq, Bsq = sqp.next()
                P.act(sq[:, 0:bn], ps[:, 0:bn], AF.Square, r=[psB], w=[Bsq])
                flush()
                sp, spB = ssq[bi]
                pending.append(lambda: P.mm(sp[:, 0:bn], ones[:], sq[:, 0:bn], m == 0, m == n_mt - 1, r=[Bo, Bsq], w=[spB]))

            def post(m):
                o, Bos = state["o"]
                P.dma("sp", dst_dram[m * 128:(m + 1) * 128, :], o[:, 0:TN], r=[Bos])

            linear(wd, range(n_mt), KC, 128, rhs_fn, rhs_bufs, lambda m: blocks, ep, wpool, pspool, post)
            flush()
            return ssq

        def ssq_tiles(ph, n, tag):
            return [(ph.enter_context(PST("sq%s%d" % (tag, i), [128, 512], F32)), Buf("sq" + tag, True)) for i in range(n)]

        with ExitStack() as ph:
            ones, Bo, epsT, Be = make_consts(ph)
            hnT = ph.enter_context(SBT("hnT", [128, DC, T], BF16))
            Bhn = Buf("hnT")
            gn = ph.enter_context(SBT("gn", [128, 5, DC], F32))
            Bg = Buf("gn")
            P.dma("sp", gn[:], gains, w=[Bg])
            with ExitStack() as ph0:
                xs_pool = sb_pool(ph0, 2, "xs", [128, DC, 128], F32)
                sq_pool = sb_pool(ph0, 2, "sq", [128, DC, 128], BF16)
                rs_pool = sb_pool(ph0, 2, "rs", [128, 128], F32)
                pp = Rot(psum_pool(ph0, 2, "pn"))
                xTv = xT.rearrange("(c p) t -> p c t", p=128)
                nxt = None
                for tb in range(T // 128):
                    t0 = tb * 128
                    if nxt is None:
                        nxt = xs_pool.next()
                        P.dma("sp", nxt[0][:], xTt[tb], w=[nxt[1]])
                    xs, Bx = nxt
                    if tb + 1 < T // 128:
                        nxt = xs_pool.next()
                        P.dma("sp", nxt[0][:], xTt[tb + 1], w=[nxt[1]])
                    sq, Bs = sq_pool.next()
                    rs, Br = rs_pool.next()
                    ps, psB = pp.next()
                    P.act(sq[:], xs[:], AF.Square, r=[Bx], w=[Bs])
                    for c in range(DC):
                        P.mm(ps[:, 0:128], ones[:], sq[:, c, :], c == 0, c == DC - 1, r=[Bo, Bs], w=[psB])
                    P.act(rs[:], ps[:, 0:128], AF.Sqrt, r=[psB, Be], w=[Br], bias=epsT[:, 0:1], scale=1.0 / D)
                    P.recip(rs[:], rs[:], r=[Br], w=[Br])
                    P.tt("dve", xs[:], xs[:], bcast_ap(rs, 0, [(0, DC), (1, 128)], 128), ALU.mult, r=[Bx, Br], w=[Bx])
                    P.tt("pool", hnT[:, :, t0:t0 + 128], xs[:], bcast_ap(gn, 0, [(1, DC), (0, 128)], 5 * DC),
                         ALU.mult, r=[Bx, Bg], w=[Bhn])
                P.phase_end()

            def hn_rhs(c, b0, bn):
                return hnT[:, c, b0:b0 + bn]

            with ExitStack() as ph1:
                cs = ph1.enter_context(SBT("cs", [128, 2, T], F32))
                Bcs = Buf("cs")
                P.dma("sp", cs[:, 0, :], cosT, w=[Bcs])
                P.dma("sp", cs[:, 1, :], sinT, w=[Bcs])
                permb = ph1.enter_context(SBT("permb", [128, 128], BF16))
                Bperm = Buf("permb")
                P.dma("pool", permb[:], c_perm, w=[Bperm])
                wqpool = sb_pool(ph1, 3, "wqk", [128, DC, 128], BF16)
                pspool = Rot(psum_pool(ph1, 4, "pa"))
                ps2pool = Rot(psum_pool(ph1, 3, "pa2"))
                stg = sb_pool(ph1, 2, "stg", [128, T], BF16)
                qbp = sb_pool(ph1, 3, "qb", [128, 512], BF16)
                t1p = sb_pool(ph1, 3, "t1", [128, 512], F32)
                t2p = sb_pool(ph1, 3, "t2", [128, 512], F32)
                state = {}
                pending = []

                def flush_qk():
                    while pending:
                        pending.pop(0)()

                def ep_qk(m, sub, b0, bn, ps, psB):
                    if b0 == 0:
                        state["stg"] = stg.next()
                    st_, Bst = state["stg"]
                    qb, Bqb = qbp.next()
                    t1, B1 = t1p.next()
                    P.copy("act", qb[:, 0:bn], ps[:, 0:bn], r=[psB], w=[Bqb])
                    P.tt("dve", t1[:, 0:bn], ps[:, 0:bn], cs[:, 0, b0:b0 + bn], ALU.mult, r=[psB, Bcs], w=[B1])
                    flush_qk()

                    def fin(st_=st_, Bst=Bst, qb=qb, Bqb=Bqb, t1=t1, B1=B1, b0=b0, bn=bn):
                        ps2, ps2B = ps2pool.next()
                        P.mm(ps2[:, 0:bn], permb[:], qb[:, 0:bn], True, True, r=[Bperm, Bqb], w=[ps2B])
                        t2, B2 = t2p.next()
                        P.tt("dve", t2[:, 0:bn], ps2[:, 0:bn], cs[:, 1, b0:b0 + bn], ALU.mult, r=[ps2B, Bcs], w=[B2])
                        P.tt("dve", st_[:, b0:b0 + bn], t1[:, 0:bn], t2[:, 0:bn], ALU.add, r=[B1, B2], w=[Bst])
                    pending.append(fin)

                def post_qk(m):
                    flush_qk()
                    st_, Bst = state["stg"]
                    h = m % H
                    if m < H:
                        P.dma("sp", qT[h * 128:(h + 1) * 128, :], st_[:, 0:TE], r=[Bst])
                    else:
                        P.dma("sp", kT[h * 128:(h + 1) * 128, :], st_[:, 0:T], r=[Bst])

                linear(w_qk, range(2 * H), DC, 128, hn_rhs, [Bhn], lambda m: (blocks_ext if m < H else blocks_full),
                       ep_qk, wqpool, pspool, post_qk)
                P.phase_end()

            with ExitStack() as ph1:
                w1pool = sb_pool(ph1, 3, "w1", [128, DC, 128], BF16)
                pspool = Rot(psum_pool(ph1, 6, "pb"))
                stg = sb_pool(ph1, 2, "stgb", [128, T], BF16)
                state = {}

                def ep_u(m, sub, b0, bn, ps, psB):
                    if b0 == 0:
                        state["stg"] = stg.next()
                    st_, Bst = state["stg"]
                    P.copy("act", st_[:, b0:b0 + bn], ps[:, 0:bn], r=[psB], w=[Bst])

                def post_u(m):
                    st_, Bst = state["stg"]
                    P.dma("sp", uT[m * 128:(m + 1) * 128, :], st_[:, 0:T], r=[Bst])

                linear(w_u, range(SC), DC, 128, hn_rhs, [Bhn], lambda m: blocks_full, ep_u, w1pool, pspool, post_u)

                def ep_g(m, sub, b0, bn, ps, psB):
                    if b0 == 0:
                        state["stg"] = stg.next()
                    st_, Bst = state["stg"]
                    P.act(st_[:, b0:b0 + bn], ps[:, 0:bn], AF.Sigmoid, r=[psB], w=[Bst])

                def post_g(m):
                    st_, Bst = state["stg"]
                    P.dma("sp", sgT[m * 128:(m + 1) * 128, :], st_[:, 0:TE], r=[Bst])

                linear(w_g, range(2 * DC), DC, 128, hn_rhs, [Bhn], lambda m: blocks_ext, ep_g, w1pool, pspool, post_g)
                P.phase_end()

            with ExitStack() as ph2:
                VW = min(AW, 512)
                wv = sb_pool(ph2, 2 if DC <= 16 else 1, "wv", [128, DC, VW], BF16)
                pspool = Rot(psum_pool(ph2, 4, "pv"))
                vst = sb_pool(ph2, 3, "vst", [128, VW], BF16)
                for cb in range(AW // VW):
                    wt, wB = wv.next()
                    P.dma("pool", wt[:], w_v[cb], w=[wB])
                    for tt_ in range(T // 128):
                        ps, psB = pspool.next()
                        for c in range(DC):
                            P.mm(ps[:, 0:VW], hnT[:, c, tt_ * 128:(tt_ + 1) * 128], wt[:, c, :], c == 0, c == DC - 1,
                                 r=[wB, Bhn], w=[psB])
                        st_, Bst = vst.next()
                        P.copy(rr_eng(("act", "dve")), st_[:], ps[:, 0:VW], r=[psB], w=[Bst])
                        P.dma("sp", vS[tt_ * 128:(tt_ + 1) * 128, cb * VW:(cb + 1) * VW], st_[:], r=[Bst])
                P.phase_end()

        if cfg.get('stop') == 'A':
            return nc
        with ExitStack() as ph:
            ones, Bo, epsT, Be = make_consts(ph)
            mt = ph.enter_context(SBT("mtab", [128, MW], BF16))
            Bm = Buf("mtab")
            P.dma("pool", mt[:], c_mtab, w=[Bm])
            qp = sb_pool(ph, 2, "qh", [128, TE], BF16)
            kp = sb_pool(ph, 2, "kh", [128, T], BF16)
            vp = sb_pool(ph, 2, "vh", [128, T // 128, 128], BF16)
            pS = Rot(psum_pool(ph, 4, "pS"))
            pO = Rot(psum_pool(ph, 2, "pO"))
            pD = Rot(psum_pool(ph, 2, "pD"))
            pe_ = sb_pool(ph, 4, "pe", [128, 512], BF16)
            pm_ = sb_pool(ph, 4, "pm", [128, 512], BF16)
            rd_ = sb_pool(ph, 2, "rd", [128, 512], F32)
            ya_ = sb_pool(ph, 2, "ya", [128, TE], BF16)
            vSv = vS.rearrange("(kt p) f -> p kt f", p=128)

            def load_head(h):
                qh, Bq = qp.next()
                kh, Bk = kp.next()
                vh, Bv = vp.next()
                P.dma("sp", qh[:], qT[h * 128:(h + 1) * 128, :], w=[Bq])
                P.dma("sp", kh[:], kT[h * 128:(h + 1) * 128, :], w=[Bk])
                P.dma("sp", vh[:], vSv[:, :, h * 128:(h + 1) * 128], w=[Bv])
                return (qh, Bq, kh, Bk, vh, Bv)

            nxt = load_head(0)
            for h in range(H):
                qh, Bq, kh, Bk, vh, Bv = nxt
                if h + 1 < H:
                    nxt = load_head(h + 1)
                ya, Bya = ya_.next()
                for (b0, bn) in blocks_ext:
                    kts = []
                    for kt in range(T // 128):
                        dmin = kt * 128 - (b0 + bn - 1)
                        dmax = kt * 128 + 127 - b0
                        if dmin > WMAX or dmax < -WMAX:
                            continue
                        kts.append(kt)
                    po, poB = pO.next()
                    pd, pdB = pD.next()
                    pend = []
                    BD = cfg.get("bdbg", 9)
                    if BD < 1:
                        continue
                    for i, kt in enumerate(kts):
                        ps, psB = pS.next()
                        P.mm(ps[:, 0:bn], kh[:, kt * 128:(kt + 1) * 128], qh[:, b0:b0 + bn], True, True, r=[Bk, Bq], w=[psB])
                        pe, Bpe = pe_.next()
                        pm, Bpm = pm_.next()
                        if BD < 2:
                            continue
                        P.act(pe[:, 0:bn], ps[:, 0:bn], AF.Exp, r=[psB], w=[Bpe], scale=att_scale)
                        if BD < 3:
                            continue
                        xo = b0 - kt * 128 + X0
                        P.tt("dve", pm[:, 0:bn], pe[:, 0:bn], mt[:, xo:xo + bn], ALU.mult, r=[Bpe, Bm], w=[Bpm])
                        if BD < 4:
                            continue
                        while len(pend) > 1:
                            pend.pop(0)()

                        def f(i=i, kt=kt, pm=pm, Bpm=Bpm, po=po, poB=poB, pd=pd, pdB=pdB, bn=bn, nk=len(kts), vh=vh, Bv=Bv):
                            P.mm(po[:, 0:bn], vh[:, kt, :], pm[:, 0:bn], i == 0, i == nk - 1, r=[Bv, Bpm], w=[poB])
                            P.mm(pd[:, 0:bn], ones[:], pm[:, 0:bn], i == 0, i == nk - 1, r=[Bo, Bpm], w=[pdB])
                        pend.append(f)
                    while pend:
                        pend.pop(0)()
                    if BD < 5:
                        continue
                    rd, Brd = rd_.next()
                    P.recip(rd[:, 0:bn], pd[:, 0:bn], r=[pdB], w=[Brd])
                    P.tt("dve", ya[:, b0:b0 + bn], po[:, 0:bn], rd[:, 0:bn], ALU.mult, r=[poB, Brd], w=[Bya])
                P.dma("sp", yaT[h * 128:(h + 1) * 128, :], ya[:], r=[Bya])
            P.phase_end()

        if cfg.get('stop') == 'B':
            return nc
        KB = max(1, min(16, 512 // (2 * GH)))
        NJ = 8
        NW = 24
        GC = min(GH, 16)
        PI_LO = 3.1415925
        for gh in range(G // GH):
            g0 = gh * GH
            with ExitStack() as ph:
                ident = ph.enter_context(SBT("ident", [128, 128], F32))
                identb = ph.enter_context(SBT("identb", [128, 128], BF16))
                Bid = Buf("ident")
                Bidb = Buf("identb")
                P.dma("sp", ident[:], c_ident, w=[Bid])
                P.dma("pool", identb[:], c_ident, w=[Bidb])
                Ws = ph.enter_context(SBT("Ws", [128, GH, 2, 128], BF16))
                BWs = Buf("Ws")
                Tz = ph.enter_context(SBT("Tz", [128, GH, 128], BF16))
                BTz = Buf("Tz")
                Et = ph.enter_context(SBT("Et", [128, GH, 2, 128], BF16))
                BEt = Buf("Et")
                Ac = ph.enter_context(SBT("Ac", [128, 2, GH, 2], F32))
                BAc = Buf("Ac")
                with ExitStack() as pt:
                    lam = pt.enter_context(SBT("lam", [128, 3, GH], F32))
                    Bl = Buf("lam")
                    P.dma("sp", lam[:], s_lam[:, :, g0:g0 + GH], w=[Bl])
                    ntab = pt.enter_context(SBT("ntab", [128, 26], F32))
                    Bnt = Buf("ntab")
                    P.dma("sp", ntab[:], c_ntab, w=[Bnt])
                    Bt = pt.enter_context(SBT("Bt", [128, 2, GH, 16], F32))
                    Ct = pt.enter_context(SBT("Ct", [128, 2, GH, 16], F32))
                    BBt, BCt = Buf("Bt"), Buf("Ct")
                    P.dma("sp", Bt[:], s_B[:, :, g0:g0 + GH, :], w=[BBt])
                    P.dma("sp", Ct[:], s_C[:, :, g0:g0 + GH, :], w=[BCt])
                    msk = pt.enter_context(SBT("msk", [128, 2, 128], F32))
                    Bmk = Buf("msk")
                    P.dma("sp", msk[:], c_mask, w=[Bmk])
                    Dd = pt.enter_context(SBT("Dd", [128, GH], F32))
                    BDd = Buf("Dd")
                    P.dma("sp", Dd[:], s_D[:, g0:g0 + GH], w=[BDd])
                    sm = pt.enter_context(SBT("sm", [128, 8, GH], F32))
                    Bsm = Buf("sm")
                    P.act(sm[:, 0, :], lam[:, 2, :], AF.Exp, r=[Bl], w=[Bsm])
                    P.tt("dve", sm[:, 1, :], lam[:, 0, :], sm[:, 0, :], ALU.mult, r=[Bl, Bsm], w=[Bsm])
                    P.tt("dve", sm[:, 2, :], lam[:, 1, :], sm[:, 0, :], ALU.mult, r=[Bl, Bsm], w=[Bsm])

                    pwm = pt.enter_context(SBT("pwm", [128, 2, GH, NW], F32))
                    pw2 = pt.enter_context(SBT("pw2", [128, 2, GH, 2], F32))
                    pws = pt.enter_context(SBT("pws", [128, 2, GH, NJ], F32))
                    Bpwm, Bpw2, Bpws = Buf("pwm"), Buf("pw2"), Buf("pws")
                    pk = ExitStack()
                    wk = pk.enter_context(SBT("wk", [128, 4, GH, NW], F32))
                    wki = pk.enter_context(SBT("wki", [128, GH, NW], I32))
                    Bwk = Buf("wk")

                    def cpow(col0, nj, dst, rowlen, Bout):
                        def v(i):
                            return bcast_ap(wk, i * GH * NW, [(NW, GH), (1, nj)], 4 * GH * NW)

                        def o(ri):
                            return bcast_ap(dst, ri * GH * rowlen, [(rowlen, GH), (1, nj)], 2 * GH * rowlen)
                        lrdt_b = bcast_ap(sm, 1 * GH, [(1, GH), (0, nj)], 8 * GH)
                        th_b = bcast_ap(sm, 2 * GH, [(1, GH), (0, nj)], 8 * GH)
                        nt_b = bcast_ap(ntab, col0, [(0, GH), (1, nj)], 26)
                        wki_v = bcast_ap(wki, 0, [(NW, GH), (1, nj)], GH * NW)
                        P.tt("dve", o(0), lrdt_b, nt_b, ALU.mult, r=[Bsm, Bnt], w=[Bout])
                        P.act(o(0), o(0), AF.Exp, r=[Bout], w=[Bout])
                        P.tt("dve", v(0), th_b, nt_b, ALU.mult, r=[Bsm, Bnt], w=[Bwk])
                        for (dsti, shift) in ((3, 0.0), (2, float(np.pi / 2))):
                            P.ts("dve", v(1), v(0), shift, None, ALU.add, r=[Bwk], w=[Bwk])
                            P.ts("dve", wki_v, v(1), 1.0 / TWO_PI, None, ALU.mult, r=[Bwk], w=[Bwk])
                            P.copy("dve", v(2), wki_v, r=[Bwk], w=[Bwk])
                            P.stt("dve", v(1), v(2), -TWO_PI, v(1), ALU.mult, ALU.add, r=[Bwk], w=[Bwk])
                            P.ts("dve", v(1), v(1), -PI_LO, PI_LO, ALU.max, ALU.min, r=[Bwk], w=[Bwk])
                            P.act(v(dsti), v(1), AF.Sin, r=[Bwk], w=[Bwk])
                        P.tt("dve", o(1), o(0), v(3), ALU.mult, r=[Bwk, Bout], w=[Bout])
                        P.tt("dve", o(0), o(0), v(2), ALU.mult, r=[Bwk, Bout], w=[Bout])

                    cpow(24, 2, pw2, 2, Bpw2)
                    cpow(0, NW, pwm, NW, Bpwm)
                    ar = bcast_ap(pw2, 0, [(2, GH)], 2 * GH * 2)
                    ai = bcast_ap(pw2, GH * 2, [(2, GH)], 2 * GH * 2)
                    a8r = bcast_ap(pw2, 1, [(2, GH)], 2 * GH * 2)
                    a8i = bcast_ap(pw2, GH * 2 + 1, [(2, GH)], 2 * GH * 2)
                    P.tt("dve", sm[:, 3, :], lam[:, 0, :], lam[:, 0, :], ALU.mult, r=[Bl], w=[Bsm])
                    P.tt("dve", sm[:, 6, :], lam[:, 1, :], lam[:, 1, :], ALU.mult, r=[Bl], w=[Bsm])
                    P.tt("dve", sm[:, 3, :], sm[:, 3, :], sm[:, 6, :], ALU.add, r=[Bsm], w=[Bsm])
                    P.recip(sm[:, 3, :], sm[:, 3, :], r=[Bsm], w=[Bsm])
                    P.ts("dve", sm[:, 6, :], ar, -1.0, None, ALU.add, r=[Bpw2], w=[Bsm])
                    P.tt("dve", sm[:, 4, :], sm[:, 6, :], lam[:, 0, :], ALU.mult, r=[Bsm, Bl], w=[Bsm])
                    P.tt("dve", sm[:, 7, :], ai, lam[:, 1, :], ALU.mult, r=[Bpw2, Bl], w=[Bsm])
                    P.tt("dve", sm[:, 4, :], sm[:, 4, :], sm[:, 7, :], ALU.add, r=[Bsm], w=[Bsm])
                    P.tt("dve", sm[:, 4, :], sm[:, 4, :], sm[:, 3, :], ALU.mult, r=[Bsm], w=[Bsm])
                    P.tt("dve", sm[:, 5, :], ai, lam[:, 0, :], ALU.mult, r=[Bpw2, Bl], w=[Bsm])
                    P.tt("dve", sm[:, 7, :], sm[:, 6, :], lam[:, 1, :], ALU.mult, r=[Bsm, Bl], w=[Bsm])
                    P.tt("dve", sm[:, 5, :], sm[:, 5, :], sm[:, 7, :], ALU.subtract, r=[Bsm], w=[Bsm])
                    P.tt("dve", sm[:, 5, :], sm[:, 5, :], sm[:, 3, :], ALU.mult, r=[Bsm], w=[Bsm])

                    def Acv(k, ri):
                        return bcast_ap(Ac, k * GH * 2 + ri, [(2, GH)], 2 * GH * 2)
                    P.copy("dve", Acv(0, 0), a8r, r=[Bpw2], w=[BAc])
                    P.copy("dve", Acv(0, 1), a8r, r=[Bpw2], w=[BAc])
                    P.copy("dve", Acv(1, 1), a8i, r=[Bpw2], w=[BAc])
                    P.ts("dve", Acv(1, 0), a8i, -1.0, None, ALU.mult, r=[Bpw2], w=[BAc])
                    fr_b = bcast_ap(sm, 4 * GH, [(1, GH), (0, NJ)], 8 * GH)
                    fi_b = bcast_ap(sm, 5 * GH, [(1, GH), (0, NJ)], 8 * GH)

                    def pm_(ri):
                        return bcast_ap(pwm, ri * GH * NW, [(NW, GH), (1, NJ)], 2 * GH * NW)

                    def ps_(ri):
                        return bcast_ap(pws, ri * GH * NJ, [(NJ, GH), (1, NJ)], 2 * GH * NJ)

                    def wv(i):
                        return bcast_ap(wk, i * GH * NW, [(NW, GH), (1, NJ)], 4 * GH * NW)
                    P.tt("dve", wv(0), pm_(0), fr_b, ALU.mult, r=[Bpwm, Bsm], w=[Bwk])
                    P.tt("dve", wv(1), pm_(1), fi_b, ALU.mult, r=[Bpwm, Bsm], w=[Bwk])
                    P.tt("dve", ps_(0), wv(0), wv(1), ALU.subtract, r=[Bwk], w=[Bpws])
                    P.tt("dve", wv(0), pm_(0), fi_b, ALU.mult, r=[Bpwm, Bsm], w=[Bwk])
                    P.tt("dve", wv(1), pm_(1), fr_b, ALU.mult, r=[Bpwm, Bsm], w=[Bwk])
                    P.tt("dve", ps_(1), wv(0), wv(1), ALU.add, r=[Bwk], w=[Bpws])

                    P.phase_end()
                    pk.close()
                    Xp = sb_pool(pt, 2, "Xc", [128, GC, 2, 128], BF16)
                    Yp = sb_pool(pt, 2, "Yc", [128, GC, 2, 128], BF16)
                    tmpx = sb_pool(pt, 1, "ctmpx", [128, 2, GC, 128], F32)
                    tmpy = sb_pool(pt, 1, "ctmpy", [128, 2, GC, 128], F32)

                    def cprod(eng, tmp, ptab, prow, pj0, gc, src, Bsrc, Bp, dst, doff, drow, Bdst, neg_imag):
                        tm, Btm = tmp.next()

                        def pv(ri):
                            return bcast_ap(ptab, ri * GH * prow + gc * GC * prow + pj0, [(prow, GC), (1, 8), (0, 16)], 2 * GH * prow)

                        def sv(ri):
                            return bcast_ap(src, ri * GH * 16 + gc * GC * 16, [(16, GC), (0, 8), (1, 16)], 2 * GH * 16)

                        def tv(i):
                            return bcast_ap(tm, i * GC * 128, [(128, GC), (16, 8), (1, 16)], 2 * GC * 128)

                        def dv(ri):
                            return bcast_ap(dst, doff + ri * 128, [(256, GC), (16, 8), (1, 16)], drow)
                        P.tt(eng, tv(0), pv(0), sv(0), ALU.mult, r=[Bp, Bsrc], w=[Btm])
                        P.tt(eng, tv(1), pv(1), sv(1), ALU.mult, r=[Bp, Bsrc], w=[Btm])
                        P.tt(eng, dv(0), tv(0), tv(1), ALU.subtract, r=[Btm], w=[Bdst])
                        P.tt(eng, tv(0), pv(0), sv(1), ALU.mult, r=[Bp, Bsrc], w=[Btm])
                        P.tt(eng, tv(1), pv(1), sv(0), ALU.mult, r=[Bp, Bsrc], w=[Btm])
                        if neg_imag and eng == "dve":
                            P.stt(eng, dv(1), tv(0), -1.0, tv(1), ALU.mult, ALU.subtract, r=[Btm], w=[Bdst])
                        elif neg_imag:
                            P.ts(eng, tv(0), tv(0), -1.0, None, ALU.mult, r=[Btm], w=[Btm])
                            P.tt(eng, dv(1), tv(0), tv(1), ALU.subtract, r=[Btm], w=[Bdst])
                        else:
                            P.tt(eng, dv(1), tv(0), tv(1), ALU.add, r=[Btm], w=[Bdst])

                    pst = Rot(psum_pool(pt, 4, "pt"))
                    tz1 = sb_pool(pt, 2, "tz1", [128, 4, 128], F32)
                    tz2 = sb_pool(pt, 2, "tz2", [128, 4, 128], F32)
                    for gc in range(GH // GC):
                        Xc, BXc = Xp.next()
                        Yc, BYc = Yp.next()
                        cprod("dve", tmpx, pws, NJ, 0, gc, Bt, BBt, Bpws, Xc, 0, GC * 256, BXc, False)
                        cprod("pool", tmpy, pwm, NW, 8, gc, Ct, BCt, Bpwm, Yc, 0, GC * 256, BYc, True)
                        cprod("dve", tmpx, pwm, NW, 16, gc, Ct, BCt, Bpwm, Et, gc * GC * 256, GH * 256, BEt, True)
                        for q4 in range(GC * 2 // 4):
                            ps, psB = pst.next()
                            for i in range(4):
                                idx = q4 * 4 + i
                                gl_, ri = idx // 2, idx % 2
                                P.mm(ps[:, i * 128:(i + 1) * 128], Xc[:, gl_, ri, :], identb[:], True, True, r=[BXc, Bidb], w=[psB])
                            P.copy(rr_eng(("act", "dve")), bcast_ap(Ws, gc * GC * 256 + q4 * 512, [(1, 512)], GH * 256), ps[:, 0:512],
                                   r=[psB], w=[BWs])
                        for g4 in range(GC // 4):
                            psd = []
                            for d in range(2):
                                ps, psB = pst.next()
                                lo, hi = d * 64, (d + 1) * 64
                                for i in range(4):
                                    gl_ = g4 * 4 + i
                                    P.mm(ps[:, i * 128:(i + 1) * 128], Xc[lo:hi, gl_, 0, :], Yc[lo:hi, gl_, 0, :], True, False, r=[BXc, BYc], w=[psB])
                                    P.mm(ps[:, i * 128:(i + 1) * 128], Xc[lo:hi, gl_, 1, :], Yc[lo:hi, gl_, 1, :], False, True, r=[BXc, BYc], w=[psB])
                                psd.append((ps, psB))
                            a1, B1 = tz1.next()
                            a2, B2 = tz2.next()
                            mk0 = bcast_ap(msk, 0, [(0, 4), (1, 128)], 256)
                            mk1 = bcast_ap(msk, 128, [(0, 4), (1, 128)], 256)
                            pv0 = bass.AP(psd[0][0], 0, [[512, 128], [128, 4], [1, 128]])
                            pv1 = bass.AP(psd[1][0], 0, [[512, 128], [128, 4], [1, 128]])
                            P.tt("dve", a1[:], pv0, mk0, ALU.mult, r=[psd[0][1], Bmk], w=[B1])
                            P.tt("dve", a2[:], pv1, mk1, ALU.mult, r=[psd[1][1], Bmk], w=[B2])
                            P.tt("pool", a1[:], a1[:], a2[:], ALU.add, r=[B1, B2], w=[B1])
                            for i in range(4):
                                g = gc * GC + g4 * 4 + i
                                P.stt("dve", Tz[:, g, :], ident[:], Dd[:, g:g + 1], a1[:, i, :], ALU.mult, ALU.add,
                                      r=[Bid, BDd, B1], w=[BTz])
                    P.phase_end()

                Ut = ph.enter_context(SBT("Ut", [128, GH, NK], BF16))
                BUt = Buf("Ut")
                with ExitStack() as pu:
                    sel = pu.enter_context(SBT("sel", [128, 64, 128], BF16))
                    Bsel = Buf("sel")
                    P.dma("pool", sel[:], c_sel, w=[Bsel])
                    utp = sb_pool(pu, 2, "utile", [128, T], BF16)
                    pst = Rot(psum_pool(pu, 4, "pu"))

                    def load_ut(t8):
                        ut, But = utp.next()
                        ft = (g0 // 8) + t8
                        P.dma("sp", ut[:], uT[ft * 128:(ft + 1) * 128, :], w=[But])
                        return ut, But
                    nxt = load_ut(0)
                    for t8 in range(GH // 8):
                        ut, But = nxt
                        if t8 + 1 < GH // 8:
                            nxt = load_ut(t8 + 1)
                        for gl in range(8):
                            g = t8 * 8 + gl
                            ps, psB = pst.next()
                            for j in range(8):
                                P.mm(ps[:, 0:NK], sel[:, gl * 8 + j, :], bcast_ap(ut, j, [(8, NK)], T), j == 0, j == 7,
                                     r=[Bsel, But], w=[psB])
                            P.copy(rr_eng(("act", "dve")), Ut[:, g, :], ps[:, 0:NK], r=[psB], w=[BUt])
                    P.phase_end()
                hist = ph.enter_context(SBT("hist", [128, GH, 2, NKO], BF16))
                Bhist = Buf("hist")

                with ExitStack() as pr:
                    KB = 16
                    gpb = 512 // (2 * KB)
                    nbk = GH * 2 * KB // 512
                    pssets = Rot([([pr.enter_context(PST("psS%d_%d" % (a_, b_), [128, 512], F32)) for b_ in range(nbk)], Buf("psS", True))
                                  for a_ in range(2)])
                    Ssb = {d: [pr.enter_context(SBT("Ssb%d_%d" % (d, s_), [128, GH, 2, KB], F32)) for s_ in range(2)] for d in (0, 1)}
                    BSsb = {d: [Buf("Ssb"), Buf("Ssb")] for d in (0, 1)}
                    st = pr.enter_context(SBT("st", [128, 2, GH, 2], F32))
                    w1 = pr.enter_context(SBT("w1r", [128, GH, 2], F32))
                    w2 = pr.enter_context(SBT("w2r", [128, GH, 2], F32))
                    HG = GH // 2
                    Bst = {(d, pp, h_): Buf("st") for d in (0, 1) for pp in (0, 1) for h_ in (0, 1)}
                    Bw1 = {(d, h_): Buf("w1") for d in (0, 1) for h_ in (0, 1)}
                    Bw2 = {(d, h_): Buf("w2") for d in (0, 1) for h_ in (0, 1)}
                    Bhd = {0: Buf("hist0"), 1: Buf("hist1")}
                    P.memset("dve", st[:], 0.0, w=list(Bst.values()))

                    def build_steps(d):
                        lo, hi = d * 64, (d + 1) * 64
                        if d == 1:
                            kblocks = [(kb, min(KB, NK - kb)) for kb in range(0, NK, KB)][::-1]
                        else:
                            kblocks = [(kb, min(KB, NKO - kb)) for kb in range(0, NKO, KB)]

                        def mms(bi):
                            kb, kn = kblocks[bi]
                            pset, pB = pssets.next()
                            for g in range(GH):
                                for ri in range(2):
                                    off = (g % gpb) * 2 * KB + ri * KB
                                    P.mm(pset[g // gpb][:, off:off + kn], Ws[:, g, ri, :], Ut[:, g, kb:kb + kn], True, True,
                                         r=[BWs, BUt], w=[pB])
                            S_, BS_ = Ssb[d][bi % 2], BSsb[d][bi % 2]
                            for j_ in range(nbk):
                                P.copy("act", bcast_ap(S_, j_ * 512, [(1, 512)], GH * 2 * KB, 64, lo), pset[j_][lo:hi, 0:512],
                                       r=[pB], w=[BS_])
                        steps = []
                        cur = 0
                        for bi, (kb, kn) in enumerate(kblocks):
                            S_, BS_ = Ssb[d][bi % 2], BSsb[d][bi % 2]
                            ks = list(range(kb, kb + kn))
                            if d == 1:
                                ks = ks[::-1]
                            for ki, k in enumerate(ks):
                                kk = k - kb
                                fns = []
                                if ki == 0:
                                    if bi == 0:
                                        fns.append(lambda: mms(0))
                                    if bi + 1 < len(kblocks):
                                        fns.append(lambda bi=bi: mms(bi + 1))
                                split = (k >= NKO)
                                parts = [(0, HG, 0), (HG, GH, 1)] if split else [(0, GH, None)]
                                c_, n_ = cur, 1 - cur
                                if k < NKO:
                                    fns.append(lambda k=k, c_=c_: P.copy(
                                        "pool", bcast_ap(hist, k, [(2 * NKO, GH), (NKO, 2)], GH * 2 * NKO, 64, lo),
                                        st[lo:hi, c_, :, :], r=[Bst[(d, c_, 0)], Bst[(d, c_, 1)]], w=[Bhd[d]]))

                                def bl(dct, key, h_):
                                    return [dct[key + (h_,)]] if h_ is not None else [dct[key + (0,)], dct[key + (1,)]]
                                for (gs, ge, h_) in parts:
                                    fns.append(lambda gs=gs, ge=ge, h_=h_, c_=c_: P.tt(
                                        "dve", w1[lo:hi, gs:ge, :], st[lo:hi, c_, gs:ge, :], Ac[lo:hi, 0, gs:ge, :], ALU.mult,
                                        r=bl(Bst, (d, c_), h_) + [BAc], w=bl(Bw1, (d,), h_)))
                                for (gs, ge, h_) in parts:
                                    fns.append(lambda gs=gs, ge=ge, h_=h_, c_=c_: P.tt(
                                        "dve", w2[lo:hi, gs:ge, :],
                                        bcast_ap(st, c_ * GH * 2 + gs * 2 + 1, [(2, ge - gs), (-1, 2)], 2 * GH * 2, 64, lo),
                                        Ac[lo:hi, 1, gs:ge, :], ALU.mult,
                                        r=bl(Bst, (d, c_), h_) + [BAc], w=bl(Bw2, (d,), h_)))
                                for (gs, ge, h_) in parts:
                                    fns.append(lambda gs=gs, ge=ge, h_=h_: P.tt(
                                        "dve", w1[lo:hi, gs:ge, :], w1[lo:hi, gs:ge, :], w2[lo:hi, gs:ge, :], ALU.add,
                                        r=bl(Bw1, (d,), h_) + bl(Bw2, (d,), h_), w=bl(Bw1, (d,), h_)))
                                for (gs, ge, h_) in parts:
                                    fns.append(lambda gs=gs, ge=ge, h_=h_, n_=n_, kk=kk, S_=S_, BS_=BS_: P.tt(
                                        "dve", st[lo:hi, n_, gs:ge, :], w1[lo:hi, gs:ge, :],
                                        bcast_ap(S_, gs * 2 * KB + kk, [(2 * KB, ge - gs), (KB, 2)], GH * 2 * KB, 64, lo), ALU.add,
                                        r=bl(Bw1, (d,), h_) + [BS_], w=bl(Bst, (d, n_), h_)))
                                steps.append((k, fns))
                                cur = n_
                        return steps

                    sa = build_steps(1)
                    sb = build_steps(0)
                    pre = [s_ for s_ in sa if s_[0] >= NKO]
                    pa = [s_ for s_ in sa if s_[0] < NKO]
                    for (_, fns) in pre:
                        for fn in fns:
                            fn()
                    assert len(pa) == len(sb) == NKO
                    for i in range(NKO):
                        fa, fb = pa[i][1], sb[i][1]
                        for j in range(max(len(fa), len(fb))):
                            if j < len(fa):
                                fa[j]()
                            if j < len(fb):
                                fb[j]()
                    P.phase_end()

                with ExitStack() as po_:
                    selT = po_.enter_context(SBT("selT", [128, 64, 128], BF16))
                    BselT = Buf("selT")
                    P.dma("pool", selT[:], c_selT, w=[BselT])
                    Zs = po_.enter_context(SBT("Zs", [128, GH, NKO], BF16))
                    BZs = Buf("Zs")
                    psy = Rot(psum_pool(po_, 3, "py"))
                    for g in range(GH):
                        ps, psB = psy.next()
                        P.mm(ps[:, 0:NKO], Tz[:, g, :], Ut[:, g, 0:NKO], True, False, r=[BTz, BUt], w=[psB])
                        P.mm(ps[:, 0:NKO], Et[:, g, 0, :], hist[:, g, 0, :], False, False, r=[BEt, Bhist], w=[psB])
                        P.mm(ps[:, 0:NKO], Et[:, g, 1, :], hist[:, g, 1, :], False, True, r=[BEt, Bhist], w=[psB])
                        P.act(Zs[:, g, :], ps[:, 0:NKO], AF.Gelu_apprx_tanh, r=[psB], w=[BZs])
                    psz = Rot(psum_pool(po_, 3, "pz"))
                    zst = sb_pool(po_, 2, "zst", [128, TE], BF16)
                    for t8 in range(GH // 8):
                        zs_, Bzs_ = zst.next()
                        for (b0, bn) in blocks_ext:
                            ps, psB = psz.next()
                            k0, kn = b0 // 8, bn // 8
                            for j in range(8):
                                for gl in range(8):
                                    g = t8 * 8 + gl
                                    P.mm(bass.AP(ps, j, [[512, 128], [8, kn]]), selT[:, gl * 8 + j, :], Zs[:, g, k0:k0 + kn],
                                         gl == 0, gl == 7, r=[BselT, BZs], w=[psB])
                            P.copy(rr_eng(("act", "dve")), zs_[:, b0:b0 + bn], ps[:, 0:bn], r=[psB], w=[Bzs_])
                        ft = (g0 // 8) + t8
                        P.dma("sp", zT[ft * 128:(ft + 1) * 128, :], zs_[:], r=[Bzs_])
                    P.phase_end()

        if cfg.get('stop') == 'C':
            return nc
        with ExitStack() as ph:
            ones, Bo, epsT, Be = make_consts(ph)
            gn = ph.enter_context(SBT("gn", [128, 5, DC], F32))
            Bg = Buf("gn")
            P.dma("sp", gn[:], gains, w=[Bg])
            mg = ph.enter_context(SBT("mg", [128, DC, TE], BF16))
            Bmg = Buf("mg")
            with ExitStack() as p0:
                ybT = p0.enter_context(SBT("ybT", [128, SC, TE], BF16))
                Byb = Buf("ybT")
                with ExitStack() as p1:
                    zTs = p1.enter_context(SBT("zTs", [128, SC, TE], BF16))
                    Bz = Buf("zTs")
                    P.dma("sp", zTs[:], zT.rearrange("(c p) t -> p c t", p=128), w=[Bz])
                    bg = p1.enter_context(SBT("bg", [128, SC], F32))
                    Bbg = Buf("bg")
                    P.dma("sp", bg[:], b_glu, w=[Bbg])
                    wpool = sb_pool(p1, 3, "wgl", [128, SC, 128], BF16)
                    pspool = Rot(psum_pool(p1, 4, "pg"))
                    sgp = sb_pool(p1, 2, "sgl", [128, 512], F32)

                    def ep_glu(m, sub, b0, bn, ps, psB):
                        sg, Bsg = sgp.next()
                        P.act(sg[:, 0:bn], ps[:, 0:bn], AF.Sigmoid, r=[psB, Bbg], w=[Bsg], bias=bg[:, m:m + 1])
                        P.tt("dve", ybT[:, m, b0:b0 + bn], sg[:, 0:bn], zTs[:, m, b0:b0 + bn], ALU.mult, r=[Bsg, Bz], w=[Byb])

                    linear(w_glu, range(SC), SC, 128, lambda c, b0, bn: zTs[:, c, b0:b0 + bn], [Bz], lambda m: blocks_ext,
                           ep_glu, wpool, pspool)
                    P.phase_end()

                with ExitStack() as p2:
                    yaS = p2.enter_context(SBT("yaS", [128, AC, TE], BF16))
                    Bya = Buf("yaS")
                    P.dma("sp", yaS[:], yaT.rearrange("(c p) t -> p c t", p=128), w=[Bya])
                    wa = sb_pool(p2, 2, "wa", [128, AC, 128], BF16)
                    wb = sb_pool(p2, 2, "wb", [128, SC, 128], BF16)
                    sgp = sb_pool(p2, 2, "sgab", [128, 2, TE], BF16)
                    pspool = Rot(psum_pool(p2, 6, "pm"))
                    t1p = sb_pool(p2, 2, "m1", [128, 512], F32)
                    t2p = sb_pool(p2, 2, "m2", [128, 512], F32)

                    def load_m(m):
                        wat, BwA = wa.next()
                        wbt, BwB = wb.next()
                        sg, Bsg = sgp.next()
                        P.dma("pool", wat[:], w_ba[m], w=[BwA])
                        P.dma("pool", wbt[:], w_bs[m], w=[BwB])
                        P.dma("sp", sg[:, 0, :], sgT[m * 128:(m + 1) * 128, :], w=[Bsg])
                        P.dma("sp", sg[:, 1, :], sgT[D + m * 128:D + (m + 1) * 128, :], w=[Bsg])
                        return (wat, BwA, wbt, BwB, sg, Bsg)

                    nxt = load_m(0)
                    for m in range(DC):
                        wat, BwA, wbt, BwB, sg, Bsg = nxt
                        if m + 1 < DC:
                            nxt = load_m(m + 1)
                        for (b0, bn) in blocks_ext:
                            psA, BA = pspool.next()
                            psB_, BB = pspool.next()
                            for c in range(AC):
                                P.mm(psA[:, 0:bn], wat[:, c, :], yaS[:, c, b0:b0 + bn], c == 0, c == AC - 1, r=[BwA, Bya], w=[BA])
                            for c in range(SC):
                                P.mm(psB_[:, 0:bn], wbt[:, c, :], ybT[:, c, b0:b0 + bn], c == 0, c == SC - 1, r=[BwB, Byb], w=[BB])
                            t1, B1 = t1p.next()
                            t2, B2 = t2p.next()
                            P.tt("dve", t1[:, 0:bn], psA[:, 0:bn], sg[:, 0, b0:b0 + bn], ALU.mult, r=[BA, Bsg], w=[B1])
                            P.tt("dve", t2[:, 0:bn], psB_[:, 0:bn], sg[:, 1, b0:b0 + bn], ALU.mult, r=[BB, Bsg], w=[B2])
                            P.tt("pool", mg[:, m, b0:b0 + bn], t1[:, 0:bn], t2[:, 0:bn], ALU.add, r=[B1, B2], w=[Bmg])
                    P.phase_end()

            with ExitStack() as ptmp:
                ssq = proj_ssq(ph, ptmp, w_out, DC, lambda c, b0, bn: mg[:, c, b0:b0 + bn], [Bmg], blocks_ext, oT, ones, Bo, "o")
                P.phase_end()
            rstd1, Br1 = rstd_from(ph, ssq, blocks_ext, TE, epsT, Be, "rstd1")
            ssq2 = ssq_tiles(ph, len(blocks_ext), "h")
            with ExitStack() as p3:
                op_ = sb_pool(p3, 2, "o_in", [128, TE], F32)
                xp_ = sb_pool(p3, 2, "x_in", [128, TE], F32)
                sqp = sb_pool(p3, 3, "hsq", [128, 512], BF16)
                hb_ = sb_pool(p3, 2, "h_bf", [128, TE], BF16)
                pending = []

                def load_r(m):
                    o, Bo_ = op_.next()
                    x, Bx_ = xp_.next()
                    P.dma("sp", o[:], oT[m * 128:(m + 1) * 128, :], w=[Bo_])
                    P.dma("sp", x[:], xT[m * 128:(m + 1) * 128, 0:TE], w=[Bx_])
                    return (o, Bo_, x, Bx_)

                nxt = load_r(0)
                for m in range(DC):
                    o, Bo_, x, Bx_ = nxt
                    if m + 1 < DC:
                        nxt = load_r(m + 1)
                    P.stt("dve", o[:], o[:], gn[:, 1, m:m + 1], rstd1[:], ALU.mult, ALU.mult, r=[Bo_, Bg, Br1], w=[Bo_])
                    P.tt("pool", o[:], o[:], x[:], ALU.add, r=[Bo_, Bx_], w=[Bo_])
                    P.dma("sp", h1T[m * 128:(m + 1) * 128, :], o[:], r=[Bo_])
                    hb, Bhb = hb_.next()
                    P.act(hb[:], o[:], AF.Copy, r=[Bo_, Bg], w=[Bhb], scale=gn[:, 2, m:m + 1])
                    P.dma("sp", hn2T[m * 128:(m + 1) * 128, :], hb[:], r=[Bhb])
                    for bi, (b0, bn) in enumerate(blocks_ext):
                        sq, Bsq = sqp.next()
                        P.act(sq[:, 0:bn], o[:, b0:b0 + bn], AF.Square, r=[Bo_], w=[Bsq])
                        while len(pending) > 2:
                            pending.pop(0)()
                        sp_, spB = ssq2[bi]
                        pending.append(lambda sp_=sp_, spB=spB, sq=sq, Bsq=Bsq, bn=bn, m=m:
                                       P.mm(sp_[:, 0:bn], ones[:], sq[:, 0:bn], m == 0, m == DC - 1, r=[Bo, Bsq], w=[spB]))
                while pending:
                    pending.pop(0)()
                P.phase_end()
            rstd2, Br2 = rstd_from(ph, ssq2, blocks_ext, TE, epsT, Be, "rstd2")
            P.dma("sp", rs2T, rstd2[:], r=[Br2])
            P.phase_end()

        if cfg.get('stop') == 'D':
            return nc
        with ExitStack() as ph:
            hn2 = ph.enter_context(SBT("hn2", [128, DC, TE], BF16))
            Bh2 = Buf("hn2")
            P.dma("sp", hn2[:], hn2T.rearrange("(c p) t -> p c t", p=128), w=[Bh2])
            cw = ph.enter_context(SBT("cw", [128, 3, FC], F32))
            cb = ph.enter_context(SBT("cb", [128, FC], F32))
            Bcw = Buf("cw")
            P.dma("sp", cw[:], conv_w, w=[Bcw])
            P.dma("sp", cb[:], conv_b, w=[Bcw])
            rs2 = ph.enter_context(SBT("rs2", [128, TE], F32))
            Brs2 = Buf("rs2")
            P.dma("sp", rs2[:], rs2T, w=[Brs2])
            wpool = sb_pool(ph, 4, "wup", [128, DC, 128], BF16)
            pspool = Rot(psum_pool(ph, 8, "pu"))
            Ap = sb_pool(ph, 2, "Asb", [128, TE + 2], F32)
            for (a_, Ba_) in Ap.items:
                P.memset("pool", a_[:], 0.0, w=[Ba_])
            cp_ = sb_pool(ph, 2, "cv", [128, TO], F32)
            gp_ = sb_pool(ph, 2, "gl", [128, TO], F32)
            op_ = sb_pool(ph, 2, "actst", [128, TO], BF16)

            def load_w(j):
                wa_, BwA = wpool.next()
                wb_, BwB = wpool.next()
                P.dma("pool", wa_[:], w_up[j], w=[BwA])
                P.dma("pool", wb_[:], w_up[FC + j], w=[BwB])
                return (wa_, BwA, wb_, BwB)

            nxt = load_w(0)
            for j in range(FC):
                wa_, BwA, wb_, BwB = nxt
                if j + 1 < FC:
                    nxt = load_w(j + 1)
                A, BA = Ap.next()
                for (b0, bn) in blocks_ext:
                    ps, psB = pspool.next()
                    for c in range(DC):
                        P.mm(ps[:, 0:bn], wa_[:, c, :], hn2[:, c, b0:b0 + bn], c == 0, c == DC - 1, r=[BwA, Bh2], w=[psB])
                    P.tt("dve", A[:, 1 + b0:1 + b0 + bn], ps[:, 0:bn], rs2[:, b0:b0 + bn], ALU.mult, r=[psB, Brs2], w=[BA])
                bps = []
                for (b0, bn) in blocks_own:
                    ps, psB = pspool.next()
                    for c in range(DC):
                        P.mm(ps[:, 0:bn], wb_[:, c, :], hn2[:, c, b0:b0 + bn], c == 0, c == DC - 1, r=[BwB, Bh2], w=[psB])
                    bps.append((ps, psB, b0, bn))
                cv, Bcv = cp_.next()
                gl, Bgl = gp_.next()
                ot, Bot = op_.next()
                P.ts("dve", cv[:], A[:, 0:TO], cw[:, 0, j:j + 1], None, ALU.mult, r=[BA, Bcw], w=[Bcv])
                P.stt("dve", cv[:], A[:, 1:TO + 1], cw[:, 1, j:j + 1], cv[:], ALU.mult, ALU.add, r=[BA, Bcw, Bcv], w=[Bcv])
                P.stt("dve", cv[:], A[:, 2:TO + 2], cw[:, 2, j:j + 1], cv[:], ALU.mult, ALU.add, r=[BA, Bcw, Bcv], w=[Bcv])
                P.act(gl[:], cv[:], AF.Gelu_apprx_tanh, r=[Bcv, Bcw], w=[Bgl], bias=cb[:, j:j + 1])
                P.tt("pool", gl[:], gl[:], rs2[:, 0:TO], ALU.mult, r=[Bgl, Brs2], w=[Bgl])
                for (ps, psB, b0, bn) in bps:
                    P.tt("dve", ot[:, b0:b0 + bn], ps[:, 0:bn], gl[:, b0:b0 + bn], ALU.mult, r=[psB, Bgl], w=[Bot])
                P.dma("sp", actT[j * 128:(j + 1) * 128, :], ot[:], r=[Bot])
            P.phase_end()

        if cfg.get('stop') == 'E':
            return nc
        with ExitStack() as ph:
            ones, Bo, epsT, Be = make_consts(ph)
            gn = ph.enter_context(SBT("gn", [128, 5, DC], F32))
            Bg = Buf("gn")
            P.dma("sp", gn[:], gains, w=[Bg])
            NBK = len(blocks_own)
            BW = max(bn for _, bn in blocks_own)
            ssqF = [(ph.enter_context(PST("ssqF%d" % i, [128, 512], F32)), Buf("ssqF", True)) for i in range(NBK)]
            rsF = [(ph.enter_context(SBT("rsF%d" % i, [128, BW], F32)), Buf("rsF")) for i in range(NBK)]
            aS = ph.enter_context(SBT("aS", [128, FC, BW], BF16))
            BaS = Buf("aS")
            wpool = sb_pool(ph, 3, "wdn", [128, FC, 128], BF16)
            pspool = Rot(psum_pool(ph, 4, "pf"))
            ost = sb_pool(ph, 2, "ostF", [128, BW], F32)
            sqp = sb_pool(ph, 3, "osqF", [128, 512], BF16)
            dp_ = sb_pool(ph, 3, "d_in", [128, BW], F32)
            hp_ = sb_pool(ph, 3, "h_in", [128, BW], F32)
            BdT = {}
            actTv = actT.rearrange("(c p) t -> p c t", p=128)

            def f2_load(bi, m):
                b0, bn = blocks_own[bi]
                d_, Bd_ = dp_.next()
                h_, Bh_ = hp_.next()
                P.dma("sp", d_[:, 0:bn], dT[m * 128:(m + 1) * 128, b0:b0 + bn], r=[BdT[(bi, m)]], w=[Bd_])
                P.dma("sp", h_[:, 0:bn], h1T[m * 128:(m + 1) * 128, b0:b0 + bn], w=[Bh_])
                return (d_, Bd_, h_, Bh_)

            def f2_fin(bi, m, ld):
                b0, bn = blocks_own[bi]
                d_, Bd_, h_, Bh_ = ld
                rs, Brs = rsF[bi]
                P.stt("dve", d_[:, 0:bn], d_[:, 0:bn], gn[:, 3, m:m + 1], rs[:, 0:bn], ALU.mult, ALU.mult, r=[Bd_, Bg, Brs], w=[Bd_])
                P.tt("pool", d_[:, 0:bn], d_[:, 0:bn], h_[:, 0:bn], ALU.add, r=[Bd_, Bh_], w=[Bd_])
                P.dma("sp", h2T[m * 128:(m + 1) * 128, b0:b0 + bn], d_[:, 0:bn], r=[Bd_])

            for bi, (b0, bn) in enumerate(blocks_own):
                P.dma("sp", aS[:, :, 0:bn], actTv[:, :, b0:b0 + bn], w=[BaS])
                pending = []
                state = {}
                sp, spB = ssqF[bi]

                def ep(m, sub, b0_, bn_, ps, psB, sp=sp, spB=spB):
                    o, Bos = ost.next()
                    state["o"] = (o, Bos)
                    P.copy("dve", o[:, 0:bn_], ps[:, 0:bn_], r=[psB], w=[Bos])
                    sq, Bsq = sqp.next()
                    P.act(sq[:, 0:bn_], ps[:, 0:bn_], AF.Square, r=[psB], w=[Bsq])
                    while pending:
                        pending.pop(0)()
                    pending.append(lambda: P.mm(sp[:, 0:bn_], ones[:], sq[:, 0:bn_], m == 0, m == DC - 1, r=[Bo, Bsq], w=[spB]))

                def post(m, bi=bi, b0=b0, bn=bn):
                    o, Bos = state["o"]
                    BdT[(bi, m)] = Buf("dT")
                    P.dma("sp", dT[m * 128:(m + 1) * 128, b0:b0 + bn], o[:, 0:bn], r=[Bos], w=[BdT[(bi, m)]])
                    if bi > 0:
                        if "ld" in state:
                            f2_fin(bi - 1, m - 1, state.pop("ld"))
                        state["ld"] = f2_load(bi - 1, m)

                linear(w_down, range(DC), FC, 128, lambda c, b0_, bn_: aS[:, c, 0:bn_], [BaS], lambda m, b0=b0, bn=bn: [(b0, bn)],
                       ep, wpool, pspool, post)
                while pending:
                    pending.pop(0)()
                if "ld" in state:
                    f2_fin(bi - 1, DC - 1, state.pop("ld"))
                rs, Brs = rsF[bi]
                P.act(rs[:, 0:bn], sp[:, 0:bn], AF.Sqrt, r=[spB, Be], w=[Brs], bias=epsT[:, 0:1], scale=1.0 / D)
                P.recip(rs[:, 0:bn], rs[:, 0:bn], r=[Brs], w=[Brs])
            last = NBK - 1
            ld = f2_load(last, 0)
            for m in range(DC):
                nx = f2_load(last, m + 1) if m + 1 < DC else None
                f2_fin(last, m, ld)
                ld = nx
            P.phase_end()

        if cfg.get('stop') == 'F':
            return nc
        with ExitStack() as ph:
            ones, Bo, epsT, Be = make_consts(ph)
            gn = ph.enter_context(SBT("gn", [128, 5, DC], F32))
            Bg = Buf("gn")
            P.dma("sp", gn[:], gains, w=[Bg])
            h2b = ph.enter_context(SBT("h2b", [128, DC, TO], BF16))
            Bh2b = Buf("h2b")
            P.dma("pool", h2b[:], h2T.rearrange("(c p) t -> p c t", p=128), w=[Bh2b])
            pTb = ph.enter_context(SBT("pTb", [128, PC, TO], BF16))
            BpT = Buf("pTb")
            P.dma("pool", pTb[:], pT.rearrange("(c p) t -> p c t", p=128), w=[BpT])
            wpl = ph.enter_context(SBT("wpl", [128, DC, PC, 128], BF16))
            Bwpl = Buf("wpl")
            P.dma("pool", wpl[:], w_ple.rearrange("m p c j -> p m c j"), w=[Bwpl])
            ssq = ssq_tiles(ph, len(blocks_own), "p")
            pspool = Rot(psum_pool(ph, 5, "pp"))
            sqp = sb_pool(ph, 3, "psq", [128, 512], BF16)
            pending = []
            for m in range(DC):
                for bi, (b0, bn) in enumerate(blocks_own):
                    ps, psB = pspool.next()
                    for c in range(PC):
                        P.mm(ps[:, 0:bn], wpl[:, m, c, :], pTb[:, c, b0:b0 + bn], c == 0, c == PC - 1, r=[Bwpl, BpT], w=[psB])
                    sq, Bsq = sqp.next()
                    P.act(sq[:, 0:bn], ps[:, 0:bn], AF.Square, r=[psB], w=[Bsq])
                    while len(pending) > 1:
                        pending.pop(0)()
                    sp_, spB = ssq[bi]
                    pending.append(lambda sp_=sp_, spB=spB, sq=sq, Bsq=Bsq, bn=bn, m=m:
                                   P.mm(sp_[:, 0:bn], ones[:], sq[:, 0:bn], m == 0, m == DC - 1, r=[Bo, Bsq], w=[spB]))
            while pending:
                pending.pop(0)()
            rstdp, Brp = rstd_from(ph, ssq, blocks_own, TO, epsT, Be, "rstdp")
            wpool = sb_pool(ph, 3, "wpg", [128, DC, 128], BF16)
            hp_ = sb_pool(ph, 2, "h2in", [128, TO], F32)
            sgp = sb_pool(ph, 2, "sgp", [128, 512], F32)
            tp_ = sb_pool(ph, 2, "tpl", [128, 512], F32)
            state = {}

            def ep_pg(m, sub, b0, bn, ps, psB):
                if b0 == 0:
                    h_, Bh_ = hp_.next()
                    P.dma("sp", h_[:], h2T[m * 128:(m + 1) * 128, :], w=[Bh_])
                    state["h"] = (h_, Bh_)
                h_, Bh_ = state["h"]
                ps2, ps2B = pspool.next()
                for c in range(PC):
                    P.mm(ps2[:, 0:bn], wpl[:, m, c, :], pTb[:, c, b0:b0 + bn], c == 0, c == PC - 1, r=[Bwpl, BpT], w=[ps2B])
                sg, Bsg = sgp.next()
                tp, Btp = tp_.next()
                P.act(sg[:, 0:bn], ps[:, 0:bn], AF.Sigmoid, r=[psB], w=[Bsg])
                P.tt("dve", tp[:, 0:bn], ps2[:, 0:bn], rstdp[:, b0:b0 + bn], ALU.mult, r=[ps2B, Brp], w=[Btp])
                P.tt("pool", tp[:, 0:bn], tp[:, 0:bn], sg[:, 0:bn], ALU.mult, r=[Btp, Bsg], w=[Btp])
                P.stt("dve", h_[:, b0:b0 + bn], tp[:, 0:bn], gn[:, 4, m:m + 1], h_[:, b0:b0 + bn], ALU.mult, ALU.add,
                      r=[Btp, Bg, Bh_], w=[Bh_])

            def post_pg(m):
                h_, Bh_ = state["h"]
                P.dma("sp", outT[m * 128:(m + 1) * 128, :], h_[:], r=[Bh_])

            linear(w_pg, range(DC), DC, 128, lambda c, b0, bn: h2b[:, c, b0:b0 + bn], [Bh2b], lambda m: blocks_own,
                   ep_pg, wpool, pspool, post_pg)
            P.phase_end()
    return nc


def tile_w(W, mw=128):
    K, N = W.shape
    return np.ascontiguousarray(W.reshape(K // 128, 128, N // mw, mw).transpose(2, 1, 0, 3))


def pvec(v):
    return np.ascontiguousarray(np.asarray(v, np.float32).reshape(-1, 128).T)


def make_constants(cfg):
    T = cfg["T"]
    TO = T // 2
    TE = TO + HALO
    c = {}
    j = np.arange(8, dtype=np.float32)
    n1 = np.zeros((128, 8), np.float32)
    n1[:64] = 7 - j
    n1[64:] = j
    nt = np.zeros((128, 26), np.float32)
    nt[:, 0:8] = n1
    nt[:, 8:16] = -n1
    nt[:, 16:24] = 8 - n1
    nt[:, 24] = 1
    nt[:, 25] = 8
    c["c_ntab"] = nt
    jj = np.arange(128) // 16
    mk = np.zeros((128, 2, 128), np.float32)
    mk[:, 0, :] = (jj[:, None] <= jj[None, :])
    mk[:, 1, :] = (jj[:, None] >= jj[None, :])
    c["c_mask"] = mk
    c["c_ident"] = np.eye(128, dtype=np.float32)
    pm = np.zeros((128, 128), np.float32)
    pm[(np.arange(128) + 64) % 128, np.arange(128)] = 1
    c["c_perm"] = pm
    sel = np.zeros((128, 64, 128), np.float32)
    selT = np.zeros((128, 64, 128), np.float32)
    for gl in range(8):
        for jx in range(8):
            for cc in range(16):
                sel[16 * gl + cc, gl * 8 + jx, 16 * jx + cc] = 1
                selT[16 * jx + cc, gl * 8 + jx, 16 * gl + cc] = 1
    c["c_sel"] = sel
    c["c_selT"] = selT
    X0 = T - 128
    MW = T - 128 + TE
    delta = np.arange(128)[:, None] - (np.arange(MW)[None, :] - X0)
    w = np.zeros_like(delta, dtype=np.float32)
    for (win, dil) in cfg["patterns"]:
        ns = win // (2 * dil)
        w += ((delta % dil == 0) & (np.abs(delta) <= ns * dil)).astype(np.float32)
    c["c_mtab"] = w
    return c


def prep_core_inputs(cfg, inp, b, half, shared):
    D, T, H, G = cfg["D"], cfg["T"], cfg["H"], cfg["G"]
    TO = T // 2
    idx = np.arange(T) if half == 0 else np.arange(T)[::-1]
    m = dict(shared)
    m["xT"] = np.ascontiguousarray(inp["x"][b][idx].T)
    m["xTt"] = np.ascontiguousarray(m["xT"].reshape(D // 128, 128, T // 128, 128).transpose(2, 1, 0, 3))
    m["pT"] = np.ascontiguousarray(inp["p"][0, b][idx[:TO]].T)
    inv_freq = (10000.0 ** (-np.arange(0, 128, 2, dtype=np.float32) / 128)).astype(np.float32)
    ang = idx.astype(np.float32)[:, None] * inv_freq[None, :]
    ang = np.concatenate([ang, ang], axis=-1)
    sgn = np.concatenate([-np.ones(64, np.float32), np.ones(64, np.float32)])
    m["cosT"] = np.ascontiguousarray(np.cos(ang).T.astype(np.float32))
    m["sinT"] = np.ascontiguousarray((np.sin(ang) * sgn[None, :]).T.astype(np.float32))
    dd = [0, 1] if half == 0 else [1, 0]
    lam = np.zeros((128, 3, G), np.float32)
    sB = np.zeros((128, 2, G, 16), np.float32)
    sC = np.zeros((128, 2, G, 16), np.float32)
    for d in range(2):
        s_ = dd[d]
        lam[d * 64:(d + 1) * 64, 0] = inp["ssm_lambda_re"][0, s_].T
        lam[d * 64:(d + 1) * 64, 1] = inp["ssm_lambda_im"][0, s_].T
        lam[d * 64:(d + 1) * 64, 2] = inp["ssm_log_dt"][0, s_][None, :]
        sB[d * 64:(d + 1) * 64, 0] = inp["ssm_b_re"][0, s_].transpose(1, 0, 2)
        sB[d * 64:(d + 1) * 64, 1] = inp["ssm_b_im"][0, s_].transpose(1, 0, 2)
        sC[d * 64:(d + 1) * 64, 0] = inp["ssm_c_re"][0, s_].transpose(2, 0, 1)
        sC[d * 64:(d + 1) * 64, 1] = inp["ssm_c_im"][0, s_].transpose(2, 0, 1)
    m["s_lam"], m["s_B"], m["s_C"] = lam, sB, sC
    cwv = inp["conv_w"][0]
    if half == 1:
        cwv = cwv[::-1]
    m["conv_w"] = np.ascontiguousarray(np.stack([pvec(cwv[t]) for t in range(3)], axis=1))
    return m


def prep_shared(cfg, inp):
    D, T, H, G, DFF = cfg["D"], cfg["T"], cfg["H"], cfg["G"], cfg["DFF"]
    AW, SW = H * 128, G * 16
    sh = dict(make_constants(cfg))
    w_in = inp["w_in"][0]
    wq, wk, wv = w_in[:, 0:AW], w_in[:, AW:2 * AW], w_in[:, 2 * AW:3 * AW]
    wu = w_in[:, 3 * AW:3 * AW + SW]
    wg = w_in[:, 3 * AW + SW:]
    sh["w_qk"] = tile_w(np.concatenate([wq, wk], axis=1))
    sh["w_v"] = tile_w(wv, min(AW, 512))
    sh["w_u"] = tile_w(wu)
    sh["w_g"] = tile_w(wg)
    sh["w_ba"] = tile_w(inp["w_branch_attn"][0])
    sh["w_bs"] = tile_w(inp["w_branch_ssm"][0])
    sh["w_out"] = tile_w(inp["w_out"][0])
    sh["w_glu"] = tile_w(inp["w_glu"][0])
    sh["b_glu"] = pvec(inp["b_glu"][0])
    sh["w_up"] = tile_w(inp["w_up"][0])
    sh["conv_b"] = pvec(inp["conv_b"][0])
    sh["w_down"] = tile_w(inp["w_down"][0])
    sh["w_ple"] = tile_w(inp["w_ple"][0])
    sh["w_pg"] = tile_w(inp["w_ple_gate"][0])
    sh["gains"] = np.ascontiguousarray(np.stack(
        [pvec(inp[k][0]) for k in ("g_mix_pre", "g_mix_post", "g_ffn_pre", "g_ffn_post", "g_ple")], axis=1))
    sh["s_D"] = np.ascontiguousarray(np.tile(inp["ssm_d"][0].reshape(G, 16).T, (8, 1)))
    return sh


def run_cfg(cfg, inp, dbg=False):
    inp = {k: np.asarray(v, np.float32) for k, v in inp.items()}
    nb = cfg["nb"]
    T, D = cfg["T"], cfg["D"]
    TO = T // 2
    shared = prep_shared(cfg, inp)
    in_maps = []
    for b in range(nb):
        for half in range(2):
            in_maps.append(prep_core_inputs(cfg, inp, b, half, shared))
    nc = build_program(cfg, dbg=dbg)
    res = run_bass_kernel_spmd(nc, in_maps, core_ids=list(range(2 * nb)))
    out = np.zeros((nb, T, D), np.float32)
    for b in range(nb):
        for half in range(2):
            idx = np.arange(T) if half == 0 else np.arange(T)[::-1]
            out[b, idx[:TO]] = res.results[b * 2 + half]["outT"].T
    return out, res


def kernel(**inputs):
    out, _ = run_cfg(FULL_CFG, inputs)
    return out
```

```python
import math
from contextlib import ExitStack
import numpy as np
import concourse.bass as bass
import concourse.mybir as mybir
from concourse.bass_utils import run_bass_kernel_spmd

F32 = mybir.dt.float32
BF16 = mybir.dt.bfloat16
I32 = mybir.dt.int32
ALU = mybir.AluOpType
AF = mybir.ActivationFunctionType

NDS = 40
EPS = 1e-6
HALO = 8
TWO_PI = float(2 * np.pi)

FULL_CFG = dict(D=4096, T=2048, H=16, G=128, DFF=12288, PLE=256,
                patterns=((128, 1), (512, 4), (2048, 16)), nb=4)


class Buf:
    __slots__ = ("name", "w", "r", "excl")

    def __init__(self, name="", excl=False):
        self.name = name
        self.w = None
        self.r = []
        self.excl = excl


class Sched:
    ENG = ("pe", "dve", "act", "pool", "sp")

    def __init__(self, nc, stack):
        self.nc = nc
        self.csem = {k: stack.enter_context(nc.semaphore("c_" + k)) for k in self.ENG}
        self.cnt = {k: 0 for k in self.ENG}
        self.dsem = [stack.enter_context(nc.semaphore("d%d" % i)) for i in range(NDS)]
        self.dcnt = [0] * NDS
        self.dnext = 0
        self.waited = {}
        self.ops = {k: [] for k in self.ENG}

    def _deps(self, eng, reads, writes):
        evs = []
        for b in reads:
            if b.w is not None:
                evs.append(b.w)
            if b.excl:
                evs.extend(b.r)
        for b in writes:
            if b.w is not None:
                evs.append(b.w)
            evs.extend(b.r)
        waits = {}
        for (key, sem, val, e) in evs:
            if e == eng and eng == "pe":
                continue
            if self.waited.get((eng, key), 0) >= val:
                continue
            if waits.get(key, (None, 0))[1] < val:
                waits[key] = (sem, val)
        for key, (sem, val) in waits.items():
            self.waited[(eng, key)] = val
        return list(waits.values())

    def _record(self, ev, reads, writes):
        for b in reads:
            b.r.append(ev)
            if len(b.r) > 24:
                b.r = b.r[-24:] if False else b.r
        for b in writes:
            b.w = ev
            b.r = []

    def op(self, eng, fn, reads=(), writes=()):
        waits = self._deps(eng, reads, writes)
        self.cnt[eng] += 1
        ev = ("c_" + eng, self.csem[eng], self.cnt[eng], eng)
        self._record(ev, reads, writes)
        self.ops[eng].append((waits, fn, self.csem[eng], 1))

    def dma(self, eng, out, in_, reads=(), writes=()):
        i = self.dnext
        self.dnext = (self.dnext + 1) % NDS
        waits = self._deps(eng, reads, writes)
        key = "d%d" % i
        if self.dcnt[i] > 0 and self.waited.get((eng, key), 0) < self.dcnt[i]:
            waits.append((self.dsem[i], self.dcnt[i]))
            self.waited[(eng, key)] = self.dcnt[i]
        self.dcnt[i] += 16
        ev = (key, self.dsem[i], self.dcnt[i], "dma")
        self._record(ev, reads, writes)

        def fn(e, out=out, in_=in_):
            return e.dma_start(out=out, in_=in_)

        self.ops[eng].append((waits, fn, self.dsem[i], 16))

    def barrier(self):
        for eng in self.ENG:
            waits = []
            for k in self.ENG:
                if k == eng or self.cnt[k] == 0:
                    continue
                key = "c_" + k
                if self.waited.get((eng, key), 0) < self.cnt[k]:
                    waits.append((self.csem[k], self.cnt[k]))
                    self.waited[(eng, key)] = self.cnt[k]
            for i in range(NDS):
                key = "d%d" % i
                if self.dcnt[i] > 0 and self.waited.get((eng, key), 0) < self.dcnt[i]:
                    waits.append((self.dsem[i], self.dcnt[i]))
                    self.waited[(eng, key)] = self.dcnt[i]
            if waits:
                self.ops[eng].append((waits, None, None, 0))

    def emit(self):
        nc = self.nc
        ops = self.ops
        self.ops = {k: [] for k in self.ENG}

        def run(e, lst):
            for (waits, fn, sem, inc) in lst:
                for (sm, v) in waits:
                    e.wait_ge(sm, v)
                if fn is not None:
                    fn(e).then_inc(sem, inc)

        with nc.Block() as block:
            if ops["pe"]:
                @block.tensor
                def _(e):
                    run(e, ops["pe"])
            if ops["dve"]:
                @block.vector
                def _(e):
                    run(e, ops["dve"])
            if ops["act"]:
                @block.scalar
                def _(e):
                    run(e, ops["act"])
            if ops["pool"]:
                @block.gpsimd
                def _(e):
                    run(e, ops["pool"])
            if ops["sp"]:
                @block.sync
                def _(e):
                    run(e, ops["sp"])


class Prog:
    def __init__(self, nc, s):
        self.nc = nc
        self.s = s
        self.rr = 0

    def mm(self, out, lhsT, rhs, start, stop, r=(), w=()):
        self.s.op("pe", lambda e: e.matmul(out, lhsT=lhsT, rhs=rhs, start=start, stop=stop), r, w)

    def act(self, out, in_, func, r=(), w=(), bias=None, scale=1.0):
        if bias is None:
            self.s.op("act", lambda e: e.activation(out=out, in_=in_, func=func, scale=scale), r, w)
        else:
            self.s.op("act", lambda e: e.activation(out=out, in_=in_, func=func, bias=bias, scale=scale), r, w)

    def tt(self, eng, out, in0, in1, op, r=(), w=()):
        self.s.op(eng, lambda e: e.tensor_tensor(out=out, in0=in0, in1=in1, op=op), r, w)

    def ts(self, eng, out, in0, s1, s2, op0, op1=None, r=(), w=()):
        if op1 is None:
            self.s.op(eng, lambda e: e.tensor_scalar(out=out, in0=in0, scalar1=s1, scalar2=None, op0=op0), r, w)
        else:
            self.s.op(eng, lambda e: e.tensor_scalar(out=out, in0=in0, scalar1=s1, scalar2=s2, op0=op0, op1=op1), r, w)

    def stt(self, eng, out, in0, scalar, in1, op0, op1, r=(), w=()):
        self.s.op(eng, lambda e: e.scalar_tensor_tensor(out=out, in0=in0, scalar=scalar, in1=in1, op0=op0, op1=op1), r, w)

    def copy(self, eng, out, in_, r=(), w=()):
        if eng == "act":
            self.s.op("act", lambda e: e.activation(out=out, in_=in_, func=AF.Copy), r, w)
        else:
            self.s.op(eng, lambda e: e.tensor_copy(out=out, in_=in_), r, w)

    def recip(self, out, in_, r=(), w=()):
        self.s.op("dve", lambda e: e.reciprocal(out=out, in_=in_), r, w)

    def memset(self, eng, ap, val, w=()):
        self.s.op(eng, lambda e: e.memset(ap, val), (), w)

    def dma(self, eng, out, in_, r=(), w=()):
        self.s.dma(eng, out, in_, r, w)

    def phase_end(self):
        self.s.barrier()
        self.s.emit()


def split_blocks(n, bs=512):
    out = []
    t = 0
    while t < n:
        w = min(bs, n - t)
        out.append((t, w))
        t += w
    return out


def bcast_ap(t, offset, dims, rowlen, nparts=128, pbase=0):
    return bass.AP(t, pbase * rowlen + offset, [[rowlen, nparts]] + [[a, b] for (a, b) in dims])


def build_program(cfg, dbg=False):
    D, T, H, G, DFF, PLE = cfg["D"], cfg["T"], cfg["H"], cfg["G"], cfg["DFF"], cfg["PLE"]
    TO = T // 2
    TE = TO + HALO
    DC = D // 128
    AW = H * 128
    SW = G * 16
    SC = SW // 128
    AC = AW // 128
    FC = DFF // 128
    PC = PLE // 128
    NK = T // 8
    NKO = TO // 8 + 1
    GH = min(G, 64)
    blocks_ext = split_blocks(TO) + [(TO, HALO)]
    blocks_own = split_blocks(TO)
    blocks_full = split_blocks(T)
    WMAX = max(w // 2 for (w, d) in cfg["patterns"])
    X0 = T - 128
    MW = T - 128 + TE
    att_scale = 128 ** -0.5

    nc = bass.Bass("TRN2", target_bir_lowering=False)
    uid = [0]

    def SBT(name, shape, dt):
        uid[0] += 1
        return nc.sbuf_tensor("%s_%d" % (name, uid[0]), shape, dt)

    def PST(name, shape, dt):
        uid[0] += 1
        return nc.psum_tensor("%s_%d" % (name, uid[0]), shape, dt)

    def din(name, shape, dt=F32):
        return nc.dram_tensor(name, list(shape), dt, kind="ExternalInput").ap()

    def dscr(name, shape, dt):
        return nc.dram_tensor(name, list(shape), dt, kind=("ExternalOutput" if dbg else "Internal")).ap()

    xT = din("xT", [D, T])
    xTt = din("xTt", [T // 128, 128, DC, 128])
    pT = din("pT", [PLE, TO])
    cosT = din("cosT", [128, T])
    sinT = din("sinT", [128, T])
    gains = din("gains", [128, 5, DC])
    w_qk = din("w_qk", [2 * H, 128, DC, 128])
    w_u = din("w_u", [SC, 128, DC, 128])
    w_g = din("w_g", [2 * DC, 128, DC, 128])
    w_v = din("w_v", [AW // 512 if AW >= 512 else 1, 128, DC, min(AW, 512)])
    w_ba = din("w_ba", [DC, 128, AC, 128])
    w_bs = din("w_bs", [DC, 128, SC, 128])
    w_out = din("w_out", [DC, 128, DC, 128])
    w_glu = din("w_glu", [SC, 128, SC, 128])
    b_glu = din("b_glu", [128, SC])
    w_up = din("w_up", [2 * FC, 128, DC, 128])
    conv_w = din("conv_w", [128, 3, FC])
    conv_b = din("conv_b", [128, FC])
    w_down = din("w_down", [DC, 128, FC, 128])
    w_ple = din("w_ple", [DC, 128, PC, 128])
    w_pg = din("w_pg", [DC, 128, DC, 128])
    s_lam = din("s_lam", [128, 3, G])
    s_B = din("s_B", [128, 2, G, 16])
    s_C = din("s_C", [128, 2, G, 16])
    s_D = din("s_D", [128, G])
    c_ntab = din("c_ntab", [128, 26])
    c_mask = din("c_mask", [128, 2, 128])
    c_ident = din("c_ident", [128, 128])
    c_perm = din("c_perm", [128, 128])
    c_sel = din("c_sel", [128, 64, 128])
    c_selT = din("c_selT", [128, 64, 128])
    c_mtab = din("c_mtab", [128, MW])

    outT = nc.dram_tensor("outT", [D, TO], F32, kind="ExternalOutput").ap()

    qT = dscr("qT", [AW, TE], BF16)
    kT = dscr("kT", [AW, T], BF16)
    vS = dscr("vS", [T, AW], BF16)
    uT = dscr("uT", [SW, T], BF16)
    sgT = dscr("sgT", [2 * D, TE], BF16)
    yaT = dscr("yaT", [AW, TE], BF16)
    zT = dscr("zT", [SW, TE], BF16)
    oT = dscr("oT", [D, TE], F32)
    h1T = dscr("h1T", [D, TE], F32)
    actT = dscr("actT", [DFF, TO], BF16)
    dT = dscr("dT", [D, TO], F32)
    h2T = dscr("h2T", [D, TO], F32)
    hn2T = dscr("hn2T", [D, TE], BF16)
    rs2T = dscr("rs2T", [128, TE], F32)

    with ExitStack() as top:
        s = Sched(nc, top)
        P = Prog(nc, s)

        def rr_eng(engs=("dve", "act", "pool")):
            P.rr += 1
            return engs[P.rr % len(engs)]

        def make_consts(ph):
            ones = ph.enter_context(SBT("ones", [128, 128], BF16))
            Bo = Buf("ones")
            P.memset("pool", ones[:], 1.0, w=[Bo])
            epsT = ph.enter_context(SBT("epsT", [128, 1], F32))
            Be = Buf("eps")
            P.memset("pool", epsT[:], EPS, w=[Be])
            return ones, Bo, epsT, Be

        def psum_pool(ph, n, name="ps"):
            tiles = []
            for i in range(n):
                t = ph.enter_context(PST("%s%d" % (name, i), [128, 512], F32))
                tiles.append((t, Buf("%s%d" % (name, i), True)))
            return tiles

        class Rot:
            def __init__(self, items):
                self.items = items
                self.i = 0

            def next(self):
                it = self.items[self.i % len(self.items)]
                self.i += 1
                return it

        def sb_pool(ph, n, name, shape, dt):
            return Rot([(ph.enter_context(SBT("%s%d" % (name, i), shape, dt)), Buf("%s%d" % (name, i)))
                        for i in range(n)])

        def linear(wd, m_list, KC, mw, rhs_fn, rhs_bufs, blocks_of, epilogue, wpool, pspool, post_tile=None):
            tiles = {}

            def load(m):
                wt, wB = wpool.next()
                P.dma("pool", wt[:], wd[m], w=[wB])
                tiles[m] = (wt, wB)

            m_list = list(m_list)
            load(m_list[0])
            for mi, m in enumerate(m_list):
                if mi + 1 < len(m_list):
                    load(m_list[mi + 1])
                wt, wB = tiles.pop(m)
                for (b0, bn) in blocks_of(m):
                    for sub in range(mw // 128):
                        ps, psB = pspool.next()
                        for c in range(KC):
                            P.mm(ps[:, 0:bn], wt[:, c, sub * 128:(sub + 1) * 128], rhs_fn(c, b0, bn),
                                 c == 0, c == KC - 1, r=[wB] + list(rhs_bufs), w=[psB])
                        epilogue(m, sub, b0, bn, ps, psB)
                if post_tile is not None:
                    post_tile(m)

        def rstd_from(ph, ssq_list, blocks, TN, epsT, Be, name):
            rstd = ph.enter_context(SBT(name, [128, TN], F32))
            B = Buf(name)
            base = blocks[0][0]
            for (sp, spB), (b0, bn) in zip(ssq_list, blocks):
                P.act(rstd[:, b0 - base:b0 - base + bn], sp[:, 0:bn], AF.Sqrt, r=[spB, Be], w=[B], bias=epsT[:, 0:1], scale=1.0 / D)
            P.recip(rstd[:], rstd[:], r=[B], w=[B])
            return rstd, B

        def proj_ssq(ph, tmp, wd, KC, rhs_fn, rhs_bufs, blocks, dst_dram, ones, Bo, tag):
            ssq = [(ph.enter_context(PST("ssq%s%d" % (tag, i), [128, 512], F32)), Buf("ssq", True)) for i in range(len(blocks))]
            wpool = sb_pool(tmp, 3, "wo" + tag, [128, KC, 128], BF16)
            pspool = Rot(psum_pool(tmp, 4, "po" + tag))
            TN = sum(bn for _, bn in blocks)
            base = blocks[0][0]
            ost = sb_pool(tmp, 2, "ost" + tag, [128, TN], F32)
            sqp = sb_pool(tmp, 3, "osq" + tag, [128, 512], BF16)
            pending = []
            state = {}
            n_mt = wd.shape[0]

            def flush():
                while pending:
                    pending.pop(0)()

            def ep(m, sub, b0, bn, ps, psB):
                if b0 == base:
                    state["o"] = ost.next()
                o, Bos = state["o"]
                bi = [b for b, _ in blocks].index(b0)
                P.copy("dve", o[:, b0 - base:b0 - base + bn], ps[:, 0:bn], r=[psB], w=[Bos])
                sq, Bsq = sqp.next()
                P.act(sq[:, 0:bn], ps[:, 0:bn], AF.Square, r=[psB], w=[Bsq])
                flush()
                sp, spB = ssq[bi]
                pending.append(lambda: P.mm(sp[:, 0:bn], ones[:], sq[:, 0:bn], m == 0, m == n_mt - 1, r=[Bo, Bsq], w=[spB]))

            def post(m):
                o, Bos = state["o"]
                P.dma("sp", dst_dram[m * 128:(m + 1) * 128, :], o[:, 0:TN], r=[Bos])

            linear(wd, range(n_mt), KC, 128, rhs_fn, rhs_bufs, lambda m: blocks, ep, wpool, pspool, post)
            flush()
            return ssq

        def ssq_tiles(ph, n, tag):
            return [(ph.enter_context(PST("sq%s%d" % (tag, i), [128, 512], F32)), Buf("sq" + tag, True)) for i in range(n)]

        with ExitStack() as ph:
            ones, Bo, epsT, Be = make_consts(ph)
            hnT = ph.enter_context(SBT("hnT", [128, DC, T], BF16))
            Bhn = Buf("hnT")
            gn = ph.enter_context(SBT("gn", [128, 5, DC], F32))
            Bg = Buf("gn")
            P.dma("sp", gn[:], gains, w=[Bg])
            with ExitStack() as ph0:
                xs_pool = sb_pool(ph0, 2, "xs", [128, DC, 128], F32)
                sq_pool = sb_pool(ph0, 2, "sq", [128, DC, 128], BF16)
                rs_pool = sb_pool(ph0, 2, "rs", [128, 128], F32)
                pp = Rot(psum_pool(ph0, 2, "pn"))
                xTv = xT.rearrange("(c p) t -> p c t", p=128)
                nxt = None
                for tb in range(T // 128):
                    t0 = tb * 128
                    if nxt is None:
                        nxt = xs_pool.next()
                        P.dma("sp", nxt[0][:], xTt[tb], w=[nxt[1]])
                    xs, Bx = nxt
                    if tb + 1 < T // 128:
                        nxt = xs_pool.next()
                        P.dma("sp", nxt[0][:], xTt[tb + 1], w=[nxt[1]])
                    sq, Bs = sq_pool.next()
                    rs, Br = rs_pool.next()
                    ps, psB = pp.next()
                    P.act(sq[:], xs[:], AF.Square, r=[Bx], w=[Bs])
                    for c in range(DC):
                        P.mm(ps[:, 0:128], ones[:], sq[:, c, :], c == 0, c == DC - 1, r=[Bo, Bs], w=[psB])
                    P.act(rs[:], ps[:, 0:128], AF.Sqrt, r=[psB, Be], w=[Br], bias=epsT[:, 0:1], scale=1.0 / D)
                    P.recip(rs[:], rs[:], r=[Br], w=[Br])
                    P.tt("dve", xs[:], xs[:], bcast_ap(rs, 0, [(0, DC), (1, 128)], 128), ALU.mult, r=[Bx, Br], w=[Bx])
                    P.tt("pool", hnT[:, :, t0:t0 + 128], xs[:], bcast_ap(gn, 0, [(1, DC), (0, 128)], 5 * DC),
                         ALU.mult, r=[Bx, Bg], w=[Bhn])
                P.phase_end()

            def hn_rhs(c, b0, bn):
                return hnT[:, c, b0:b0 + bn]

            with ExitStack() as ph1:
                cs = ph1.enter_context(SBT("cs", [128, 2, T], F32))
                Bcs = Buf("cs")
                P.dma("sp", cs[:, 0, :], cosT, w=[Bcs])
                P.dma("sp", cs[:, 1, :], sinT, w=[Bcs])
                permb = ph1.enter_context(SBT("permb", [128, 128], BF16))
                Bperm = Buf("permb")
                P.dma("pool", permb[:], c_perm, w=[Bperm])
                wqpool = sb_pool(ph1, 3, "wqk", [128, DC, 128], BF16)
                pspool = Rot(psum_pool(ph1, 4, "pa"))
                ps2pool = Rot(psum_pool(ph1, 3, "pa2"))
                stg = sb_pool(ph1, 2, "stg", [128, T], BF16)
                qbp = sb_pool(ph1, 3, "qb", [128, 512], BF16)
                t1p = sb_pool(ph1, 3, "t1", [128, 512], F32)
                t2p = sb_pool(ph1, 3, "t2", [128, 512], F32)
                state = {}
                pending = []

                def flush_qk():
                    while pending:
                        pending.pop(0)()

                def ep_qk(m, sub, b0, bn, ps, psB):
                    if b0 == 0:
                        state["stg"] = stg.next()
                    st_, Bst = state["stg"]
                    qb, Bqb = qbp.next()
                    t1, B1 = t1p.next()
                    P.copy("act", qb[:, 0:bn], ps[:, 0:bn], r=[psB], w=[Bqb])
                    P.tt("dve", t1[:, 0:bn], ps[:, 0:bn], cs[:, 0, b0:b0 + bn], ALU.mult, r=[psB, Bcs], w=[B1])
                    flush_qk()

                    def fin(st_=st_, Bst=Bst, qb=qb, Bqb=Bqb, t1=t1, B1=B1, b0=b0, bn=bn):
                        ps2, ps2B = ps2pool.next()
                        P.mm(ps2[:, 0:bn], permb[:], qb[:, 0:bn], True, True, r=[Bperm, Bqb], w=[ps2B])
                        t2, B2 = t2p.next()
                        P.tt("dve", t2[:, 0:bn], ps2[:, 0:bn], cs[:, 1, b0:b0 + bn], ALU.mult, r=[ps2B, Bcs], w=[B2])
                        P.tt("dve", st_[:, b0:b0 + bn], t1[:, 0:bn], t2[:, 0:bn], ALU.add, r=[B1, B2], w=[Bst])
                    pending.append(fin)

                def post_qk(m):
                    flush_qk()
                    st_, Bst = state["stg"]
                    h = m % H
                    if m < H:
                        P.dma("sp", qT[h * 128:(h + 1) * 128, :], st_[:, 0:TE], r=[Bst])
                    else:
                        P.dma("sp", kT[h * 128:(h + 1) * 128, :], st_[:, 0:T], r=[Bst])

                linear(w_qk, range(2 * H), DC, 128, hn_rhs, [Bhn], lambda m: (blocks_ext if m < H else blocks_full),
                       ep_qk, wqpool, pspool, post_qk)
                P.phase_end()

            with ExitStack() as ph1:
                w1pool = sb_pool(ph1, 3, "w1", [128, DC, 128], BF16)
                pspool = Rot(psum_pool(ph1, 6, "pb"))
                stg = sb_pool(ph1, 2, "stgb", [128, T], BF16)
                state = {}

                def ep_u(m, sub, b0, bn, ps, psB):
                    if b0 == 0:
                        state["stg"] = stg.next()
                    st_, Bst = state["stg"]
                    P.copy("act", st_[:, b0:b0 + bn], ps[:, 0:bn], r=[psB], w=[Bst])

                def post_u(m):
                    st_, Bst = state["stg"]
                    P.dma("sp", uT[m * 128:(m + 1) * 128, :], st_[:, 0:T], r=[Bst])

                linear(w_u, range(SC), DC, 128, hn_rhs, [Bhn], lambda m: blocks_full, ep_u, w1pool, pspool, post_u)

                def ep_g(m, sub, b0, bn, ps, psB):
                    if b0 == 0:
                        state["stg"] = stg.next()
                    st_, Bst = state["stg"]
                    P.act(st_[:, b0:b0 + bn], ps[:, 0:bn], AF.Sigmoid, r=[psB], w=[Bst])

                def post_g(m):
                    st_, Bst = state["stg"]
                    P.dma("sp", sgT[m * 128:(m + 1) * 128, :], st_[:, 0:TE], r=[Bst])

                linear(w_g, range(2 * DC), DC, 128, hn_rhs, [Bhn], lambda m: blocks_ext, ep_g, w1pool, pspool, post_g)
                P.phase_end()

            with ExitStack() as ph2:
                VW = min(AW, 512)
                wv = sb_pool(ph2, 2 if DC <= 16 else 1, "wv", [128, DC, VW], BF16)
                pspool = Rot(psum_pool(ph2, 4, "pv"))
                vst = sb_pool(ph2, 3, "vst", [128, VW], BF16)
                for cb in range(AW // VW):
                    wt, wB = wv.next()
                    P.dma("pool", wt[:], w_v[cb], w=[wB])
                    for tt_ in range(T // 128):
                        ps, psB = pspool.next()
                        for c in range(DC):
                            P.mm(ps[:, 0:VW], hnT[:, c, tt_ * 128:(tt_ + 1) * 128], wt[:, c, :], c == 0, c == DC - 1,
                                 r=[wB, Bhn], w=[psB])
                        st_, Bst = vst.next()
                        P.copy(rr_eng(("act", "dve")), st_[:], ps[:, 0:VW], r=[psB], w=[Bst])
                        P.dma("sp", vS[tt_ * 128:(tt_ + 1) * 128, cb * VW:(cb + 1) * VW], st_[:], r=[Bst])
                P.phase_end()

        if cfg.get('stop') == 'A':
            return nc
        with ExitStack() as ph:
            ones, Bo, epsT, Be = make_consts(ph)
            mt = ph.enter_context(SBT("mtab", [128, MW], BF16))
            Bm = Buf("mtab")
            P.dma("pool", mt[:], c_mtab, w=[Bm])
            qp = sb_pool(ph, 2, "qh", [128, TE], BF16)
            kp = sb_pool(ph, 2, "kh", [128, T], BF16)
            vp = sb_pool(ph, 2, "vh", [128, T // 128, 128], BF16)
            pS = Rot(psum_pool(ph, 4, "pS"))
            pO = Rot(psum_pool(ph, 2, "pO"))
            pD = Rot(psum_pool(ph, 2, "pD"))
            pe_ = sb_pool(ph, 4, "pe", [128, 512], BF16)
            pm_ = sb_pool(ph, 4, "pm", [128, 512], BF16)
            rd_ = sb_pool(ph, 2, "rd", [128, 512], F32)
            ya_ = sb_pool(ph, 2, "ya", [128, TE], BF16)
            vSv = vS.rearrange("(kt p) f -> p kt f", p=128)

            def load_head(h):
                qh, Bq = qp.next()
                kh, Bk = kp.next()
                vh, Bv = vp.next()
                P.dma("sp", qh[:], qT[h * 128:(h + 1) * 128, :], w=[Bq])
                P.dma("sp", kh[:], kT[h * 128:(h + 1) * 128, :], w=[Bk])
                P.dma("sp", vh[:], vSv[:, :, h * 128:(h + 1) * 128], w=[Bv])
                return (qh, Bq, kh, Bk, vh, Bv)

            nxt = load_head(0)
            for h in range(H):
                qh, Bq, kh, Bk, vh, Bv = nxt
                if h + 1 < H:
                    nxt = load_head(h + 1)
                ya, Bya = ya_.next()
                for (b0, bn) in blocks_ext:
                    kts = []
                    for kt in range(T // 128):
                        dmin = kt * 128 - (b0 + bn - 1)
                        dmax = kt * 128 + 127 - b0
                        if dmin > WMAX or dmax < -WMAX:
                            continue
                        kts.append(kt)
                    po, poB = pO.next()
                    pd, pdB = pD.next()
                    pend = []
                    BD = cfg.get("bdbg", 9)
                    if BD < 1:
                        continue
                    for i, kt in enumerate(kts):
                        ps, psB = pS.next()
                        P.mm(ps[:, 0:bn], kh[:, kt * 128:(kt + 1) * 128], qh[:, b0:b0 + bn], True, True, r=[Bk, Bq], w=[psB])
                        pe, Bpe = pe_.next()
                        pm, Bpm = pm_.next()
                        if BD < 2:
                            continue
                        P.act(pe[:, 0:bn], ps[:, 0:bn], AF.Exp, r=[psB], w=[Bpe], scale=att_scale)
                        if BD < 3:
                            continue
                        xo = b0 - kt * 128 + X0
                        P.tt("dve", pm[:, 0:bn], pe[:, 0:bn], mt[:, xo:xo + bn], ALU.mult, r=[Bpe, Bm], w=[Bpm])
                        if BD < 4:
                            continue
                        while len(pend) > 1:
                            pend.pop(0)()

                        def f(i=i, kt=kt, pm=pm, Bpm=Bpm, po=po, poB=poB, pd=pd, pdB=pdB, bn=bn, nk=len(kts), vh=vh, Bv=Bv):
                            P.mm(po[:, 0:bn], vh[:, kt, :], pm[:, 0:bn], i == 0, i == nk - 1, r=[Bv, Bpm], w=[poB])
                            P.mm(pd[:, 0:bn], ones[:], pm[:, 0:bn], i == 0, i == nk - 1, r=[Bo, Bpm], w=[pdB])
                        pend.append(f)
                    while pend:
                        pend.pop(0)()
                    if BD < 5:
                        continue
                    rd, Brd = rd_.next()
                    P.recip(rd[:, 0:bn], pd[:, 0:bn], r=[pdB], w=[Brd])
                    P.tt("dve", ya[:, b0:b0 + bn], po[:, 0:bn], rd[:, 0:bn], ALU.mult, r=[poB, Brd], w=[Bya])
                P.dma("sp", yaT[h * 128:(h + 1) * 128, :], ya[:], r=[Bya])
            P.phase_end()

        if cfg.get('stop') == 'B':
            return nc
        KB = max(1, min(16, 512 // (2 * GH)))
        NJ = 8
        NW = 24
        GC = min(GH, 16)
        PI_LO = 3.1415925
        for gh in range(G // GH):
            g0 = gh * GH
            with ExitStack() as ph:
                ident = ph.enter_context(SBT("ident", [128, 128], F32))
                identb = ph.enter_context(SBT("identb", [128, 128], BF16))
                Bid = Buf("ident")
                Bidb = Buf("identb")
                P.dma("sp", ident[:], c_ident, w=[Bid])
                P.dma("pool", identb[:], c_ident, w=[Bidb])
                Ws = ph.enter_context(SBT("Ws", [128, GH, 2, 128], BF16))
                BWs = Buf("Ws")
                Tz = ph.enter_context(SBT("Tz", [128, GH, 128], BF16))
                BTz = Buf("Tz")
                Et = ph.enter_context(SBT("Et", [128, GH, 2, 128], BF16))
                BEt = Buf("Et")
                Ac = ph.enter_context(SBT("Ac", [128, 2, GH, 2], F32))
                BAc = Buf("Ac")
                with ExitStack() as pt:
                    lam = pt.enter_context(SBT("lam", [128, 3, GH], F32))
                    Bl = Buf("lam")
                    P.dma("sp", lam[:], s_lam[:, :, g0:g0 + GH], w=[Bl])
                    ntab = pt.enter_context(SBT("ntab", [128, 26], F32))
                    Bnt = Buf("ntab")
                    P.dma("sp", ntab[:], c_ntab, w=[Bnt])
                    Bt = pt.enter_context(SBT("Bt", [128, 2, GH, 16], F32))
                    Ct = pt.enter_context(SBT("Ct", [128, 2, GH, 16], F32))
                    BBt, BCt = Buf("Bt"), Buf("Ct")
                    P.dma("sp", Bt[:], s_B[:, :, g0:g0 + GH, :], w=[BBt])
                    P.dma("sp", Ct[:], s_C[:, :, g0:g0 + GH, :], w=[BCt])
                    msk = pt.enter_context(SBT("msk", [128, 2, 128], F32))
                    Bmk = Buf("msk")
                    P.dma("sp", msk[:], c_mask, w=[Bmk])
                    Dd = pt.enter_context(SBT("Dd", [128, GH], F32))
                    BDd = Buf("Dd")
                    P.dma("sp", Dd[:], s_D[:, g0:g0 + GH], w=[BDd])
                    sm = pt.enter_context(SBT("sm", [128, 8, GH], F32))
                    Bsm = Buf("sm")
                    P.act(sm[:, 0, :], lam[:, 2, :], AF.Exp, r=[Bl], w=[Bsm])
                    P.tt("dve", sm[:, 1, :], lam[:, 0, :], sm[:, 0, :], ALU.mult, r=[Bl, Bsm], w=[Bsm])
                    P.tt("dve", sm[:, 2, :], lam[:, 1, :], sm[:, 0, :], ALU.mult, r=[Bl, Bsm], w=[Bsm])

                    wk = pt.enter_context(SBT("wk", [128, 4, GH, NW], F32))
                    wki = pt.enter_context(SBT("wki", [128, GH, NW], I32))
                    Bwk = Buf("wk")
                    pwm = pt.enter_context(SBT("pwm", [128, 2, GH, NW], F32))
                    pw2 = pt.enter_context(SBT("pw2", [128, 2, GH, 2], F32))
                    pws = pt.enter_context(SBT("pws", [128, 2, GH, NJ], F32))
                    Bpwm, Bpw2, Bpws = Buf("pwm"), Buf("pw2"), Buf("pws")

                    def cpow(col0, nj, dst, rowlen, Bout):
                        def v(i):
                            return bcast_ap(wk, i * GH * NW, [(NW, GH), (1, nj)], 4 * GH * NW)

                        def o(ri):
                            return bcast_ap(dst, ri * GH * rowlen, [(rowlen, GH), (1, nj)], 2 * GH * rowlen)
                        lrdt_b = bcast_ap(sm, 1 * GH, [(1, GH), (0, nj)], 8 * GH)
                        th_b = bcast_ap(sm, 2 * GH, [(1, GH), (0, nj)], 8 * GH)
                        nt_b = bcast_ap(ntab, col0, [(0, GH), (1, nj)], 26)
                        wki_v = bcast_ap(wki, 0, [(NW, GH), (1, nj)], GH * NW)
                        P.tt("dve", o(0), lrdt_b, nt_b, ALU.mult, r=[Bsm, Bnt], w=[Bout])
                        P.act(o(0), o(0), AF.Exp, r=[Bout], w=[Bout])
                        P.tt("dve", v(0), th_b, nt_b, ALU.mult, r=[Bsm, Bnt], w=[Bwk])
                        for (dsti, shift) in ((3, 0.0), (2, float(np.pi / 2))):
                            P.ts("dve", v(1), v(0), shift, None, ALU.add, r=[Bwk], w=[Bwk])
                            P.ts("dve", wki_v, v(1), 1.0 / TWO_PI, None, ALU.mult, r=[Bwk], w=[Bwk])
                            P.copy("dve", v(2), wki_v, r=[Bwk], w=[Bwk])
                            P.stt("dve", v(1), v(2), -TWO_PI, v(1), ALU.mult, ALU.add, r=[Bwk], w=[Bwk])
                            P.ts("dve", v(1), v(1), -PI_LO, PI_LO, ALU.max, ALU.min, r=[Bwk], w=[Bwk])
                            P.act(v(dsti), v(1), AF.Sin, r=[Bwk], w=[Bwk])
                        P.tt("dve", o(1), o(0), v(3), ALU.mult, r=[Bwk, Bout], w=[Bout])
                        P.tt("dve", o(0), o(0), v(2), ALU.mult, r=[Bwk, Bout], w=[Bout])

                    cpow(24, 2, pw2, 2, Bpw2)
                    cpow(0, NW, pwm, NW, Bpwm)
                    ar = bcast_ap(pw2, 0, [(2, GH)], 2 * GH * 2)
                    ai = bcast_ap(pw2, GH * 2, [(2, GH)], 2 * GH * 2)
                    a8r = bcast_ap(pw2, 1, [(2, GH)], 2 * GH * 2)
                    a8i = bcast_ap(pw2, GH * 2 + 1, [(2, GH)], 2 * GH * 2)
                    P.tt("dve", sm[:, 3, :], lam[:, 0, :], lam[:, 0, :], ALU.mult, r=[Bl], w=[Bsm])
                    P.tt("dve", sm[:, 6, :], lam[:, 1, :], lam[:, 1, :], ALU.mult, r=[Bl], w=[Bsm])
                    P.tt("dve", sm[:, 3, :], sm[:, 3, :], sm[:, 6, :], ALU.add, r=[Bsm], w=[Bsm])
                    P.recip(sm[:, 3, :], sm[:, 3, :], r=[Bsm], w=[Bsm])
                    P.ts("dve", sm[:, 6, :], ar, -1.0, None, ALU.add, r=[Bpw2], w=[Bsm])
                    P.tt("dve", sm[:, 4, :], sm[:, 6, :], lam[:, 0, :], ALU.mult, r=[Bsm, Bl], w=[Bsm])
                    P.tt("dve", sm[:, 7, :], ai, lam[:, 1, :], ALU.mult, r=[Bpw2, Bl], w=[Bsm])
                    P.tt("dve", sm[:, 4, :], sm[:, 4, :], sm[:, 7, :], ALU.add, r=[Bsm], w=[Bsm])
                    P.tt("dve", sm[:, 4, :], sm[:, 4, :], sm[:, 3, :], ALU.mult, r=[Bsm], w=[Bsm])
                    P.tt("dve", sm[:, 5, :], ai, lam[:, 0, :], ALU.mult, r=[Bpw2, Bl], w=[Bsm])
                    P.tt("dve", sm[:, 7, :], sm[:, 6, :], lam[:, 1, :], ALU.mult, r=[Bsm, Bl], w=[Bsm])
                    P.tt("dve", sm[:, 5, :], sm[:, 5, :], sm[:, 7, :], ALU.subtract, r=[Bsm], w=[Bsm])
                    P.tt("dve", sm[:, 5, :], sm[:, 5, :], sm[:, 3, :], ALU.mult, r=[Bsm], w=[Bsm])

                    def Acv(k, ri):
                        return bcast_ap(Ac, k * GH * 2 + ri, [(2, GH)], 2 * GH * 2)
                    P.copy("dve", Acv(0, 0), a8r, r=[Bpw2], w=[BAc])
                    P.copy("dve", Acv(0, 1), a8r, r=[Bpw2], w=[BAc])
                    P.copy("dve", Acv(1, 1), a8i, r=[Bpw2], w=[BAc])
                    P.ts("dve", Acv(1, 0), a8i, -1.0, None, ALU.mult, r=[Bpw2], w=[BAc])
                    fr_b = bcast_ap(sm, 4 * GH, [(1, GH), (0, NJ)], 8 * GH)
                    fi_b = bcast_ap(sm, 5 * GH, [(1, GH), (0, NJ)], 8 * GH)

                    def pm_(ri):
                        return bcast_ap(pwm, ri * GH * NW, [(NW, GH), (1, NJ)], 2 * GH * NW)

                    def ps_(ri):
                        return bcast_ap(pws, ri * GH * NJ, [(NJ, GH), (1, NJ)], 2 * GH * NJ)

                    def wv(i):
                        return bcast_ap(wk, i * GH * NW, [(NW, GH), (1, NJ)], 4 * GH * NW)
                    P.tt("dve", wv(0), pm_(0), fr_b, ALU.mult, r=[Bpwm, Bsm], w=[Bwk])
                    P.tt("dve", wv(1), pm_(1), fi_b, ALU.mult, r=[Bpwm, Bsm], w=[Bwk])
                    P.tt("dve", ps_(0), wv(0), wv(1), ALU.subtract, r=[Bwk], w=[Bpws])
                    P.tt("dve", wv(0), pm_(0), fi_b, ALU.mult, r=[Bpwm, Bsm], w=[Bwk])
                    P.tt("dve", wv(1), pm_(1), fr_b, ALU.mult, r=[Bpwm, Bsm], w=[Bwk])
                    P.tt("dve", ps_(1), wv(0), wv(1), ALU.add, r=[Bwk], w=[Bpws])

                    Xc = pt.enter_context(SBT("Xc", [128, GC, 2, 128], BF16))
                    Yc = pt.enter_context(SBT("Yc", [128, GC, 2, 128], BF16))
                    BXc, BYc = Buf("Xc"), Buf("Yc")
                    tmpx = sb_pool(pt, 1, "ctmpx", [128, 2, GC, 128], F32)
                    tmpy = sb_pool(pt, 1, "ctmpy", [128, 2, GC, 128], F32)

                    def cprod(eng, tmp, ptab, prow, pj0, gc, src, Bsrc, Bp, dst, doff, drow, Bdst, neg_imag):
                        tm, Btm = tmp.next()

                        def pv(ri):
                            return bcast_ap(ptab, ri * GH * prow + gc * GC * prow + pj0, [(prow, GC), (1, 8), (0, 16)], 2 * GH * prow)

                        def sv(ri):
                            return bcast_ap(src, ri * GH * 16 + gc * GC * 16, [(16, GC), (0, 8), (1, 16)], 2 * GH * 16)

                        def tv(i):
                            return bcast_ap(tm, i * GC * 128, [(128, GC), (16, 8), (1, 16)], 2 * GC * 128)

                        def dv(ri):
                            return bcast_ap(dst, doff + ri * 128, [(256, GC), (16, 8), (1, 16)], drow)
                        P.tt(eng, tv(0), pv(0), sv(0), ALU.mult, r=[Bp, Bsrc], w=[Btm])
                        P.tt(eng, tv(1), pv(1), sv(1), ALU.mult, r=[Bp, Bsrc], w=[Btm])
                        P.tt(eng, dv(0), tv(0), tv(1), ALU.subtract, r=[Btm], w=[Bdst])
                        P.tt(eng, tv(0), pv(0), sv(1), ALU.mult, r=[Bp, Bsrc], w=[Btm])
                        P.tt(eng, tv(1), pv(1), sv(0), ALU.mult, r=[Bp, Bsrc], w=[Btm])
                        if neg_imag and eng == "dve":
                            P.stt(eng, dv(1), tv(0), -1.0, tv(1), ALU.mult, ALU.subtract, r=[Btm], w=[Bdst])
                        elif neg_imag:
                            P.ts(eng, tv(0), tv(0), -1.0, None, ALU.mult, r=[Btm], w=[Btm])
                            P.tt(eng, dv(1), tv(0), tv(1), ALU.subtract, r=[Btm], w=[Bdst])
                        else:
                            P.tt(eng, dv(1), tv(0), tv(1), ALU.add, r=[Btm], w=[Bdst])

                    pst = Rot(psum_pool(pt, 4, "pt"))
                    tz1 = sb_pool(pt, 2, "tz1", [128, 4, 128], F32)
                    tz2 = sb_pool(pt, 2, "tz2", [128, 4, 128], F32)
                    for gc in range(GH // GC):
                        cprod("dve", tmpx, pws, NJ, 0, gc, Bt, BBt, Bpws, Xc, 0, GC * 256, BXc, False)
                        cprod("pool", tmpy, pwm, NW, 8, gc, Ct, BCt, Bpwm, Yc, 0, GC * 256, BYc, True)
                        cprod("dve", tmpx, pwm, NW, 16, gc, Ct, BCt, Bpwm, Et, gc * GC * 256, GH * 256, BEt, True)
                        for q4 in range(GC * 2 // 4):
                            ps, psB = pst.next()
                            for i in range(4):
                                idx = q4 * 4 + i
                                gl_, ri = idx // 2, idx % 2
                                P.mm(ps[:, i * 128:(i + 1) * 128], Xc[:, gl_, ri, :], identb[:], True, True, r=[BXc, Bidb], w=[psB])
                            P.copy(rr_eng(("act", "dve")), bcast_ap(Ws, gc * GC * 256 + q4 * 512, [(1, 512)], GH * 256), ps[:, 0:512],
                                   r=[psB], w=[BWs])
                        for g4 in range(GC // 4):
                            psd = []
                            for d in range(2):
                                ps, psB = pst.next()
                                lo, hi = d * 64, (d + 1) * 64
                                for i in range(4):
                                    gl_ = g4 * 4 + i
                                    P.mm(ps[:, i * 128:(i + 1) * 128], Xc[lo:hi, gl_, 0, :], Yc[lo:hi, gl_, 0, :], True, False, r=[BXc, BYc], w=[psB])
                                    P.mm(ps[:, i * 128:(i + 1) * 128], Xc[lo:hi, gl_, 1, :], Yc[lo:hi, gl_, 1, :], False, True, r=[BXc, BYc], w=[psB])
                                psd.append((ps, psB))
                            a1, B1 = tz1.next()
                            a2, B2 = tz2.next()
                            mk0 = bcast_ap(msk, 0, [(0, 4), (1, 128)], 256)
                            mk1 = bcast_ap(msk, 128, [(0, 4), (1, 128)], 256)
                            pv0 = bass.AP(psd[0][0], 0, [[512, 128], [128, 4], [1, 128]])
                            pv1 = bass.AP(psd[1][0], 0, [[512, 128], [128, 4], [1, 128]])
                            P.tt("dve", a1[:], pv0, mk0, ALU.mult, r=[psd[0][1], Bmk], w=[B1])
                            P.tt("dve", a2[:], pv1, mk1, ALU.mult, r=[psd[1][1], Bmk], w=[B2])
                            P.tt("pool", a1[:], a1[:], a2[:], ALU.add, r=[B1, B2], w=[B1])
                            for i in range(4):
                                g = gc * GC + g4 * 4 + i
                                P.stt("dve", Tz[:, g, :], ident[:], Dd[:, g:g + 1], a1[:, i, :], ALU.mult, ALU.add,
                                      r=[Bid, BDd, B1], w=[BTz])
                    P.phase_end()

                Ut = ph.enter_context(SBT("Ut", [128, GH, NK], BF16))
                BUt = Buf("Ut")
                with ExitStack() as pu:
                    sel = pu.enter_context(SBT("sel", [128, 64, 128], BF16))
                    Bsel = Buf("sel")
                    P.dma("pool", sel[:], c_sel, w=[Bsel])
                    utp = sb_pool(pu, 2, "utile", [128, T], BF16)
                    utdp = sb_pool(pu, 2, "utd", [128, 8, NK], BF16)
                    pst = Rot(psum_pool(pu, 4, "pu"))

                    def load_ut(t8):
                        ut, But = utp.next()
                        ft = (g0 // 8) + t8
                        P.dma("sp", ut[:], uT[ft * 128:(ft + 1) * 128, :], w=[But])
                        return ut, But
                    nxt = load_ut(0)
                    for t8 in range(GH // 8):
                        ut, But = nxt
                        if t8 + 1 < GH // 8:
                            nxt = load_ut(t8 + 1)
                        utd, Butd = utdp.next()
                        P.copy("pool", utd[:], bcast_ap(ut, 0, [(1, 8), (8, NK)], T), r=[But], w=[Butd])
                        for gl in range(8):
                            g = t8 * 8 + gl
                            ps, psB = pst.next()
                            for j in range(8):
                                P.mm(ps[:, 0:NK], sel[:, gl * 8 + j, :], utd[:, j, :], j == 0, j == 7,
                                     r=[Bsel, Butd], w=[psB])
                            P.copy(rr_eng(("act", "dve")), Ut[:, g, :], ps[:, 0:NK], r=[psB], w=[BUt])
                    P.phase_end()
                hist = ph.enter_context(SBT("hist", [128, GH, 2, NKO], BF16))
                Bhist = Buf("hist")

                with ExitStack() as pr:
                    KB = 16
                    gpb = 512 // (2 * KB)
                    nbk = GH * 2 * KB // 512
                    pssets = Rot([([pr.enter_context(PST("psS%d_%d" % (a_, b_), [128, 512], F32)) for b_ in range(nbk)], Buf("psS", True))
                                  for a_ in range(2)])
                    Ssb = {d: [pr.enter_context(SBT("Ssb%d_%d" % (d, s_), [128, GH, 2, KB], F32)) for s_ in range(2)] for d in (0, 1)}
                    BSsb = {d: [Buf("Ssb"), Buf("Ssb")] for d in (0, 1)}
                    st = pr.enter_context(SBT("st", [128, 2, GH, 2], F32))
                    w1 = pr.enter_context(SBT("w1r", [128, GH, 2], F32))
                    w2 = pr.enter_context(SBT("w2r", [128, GH, 2], F32))
                    HG = GH // 2
                    Bst = {(d, pp, h_): Buf("st") for d in (0, 1) for pp in (0, 1) for h_ in (0, 1)}
                    Bw1 = {(d, h_): Buf("w1") for d in (0, 1) for h_ in (0, 1)}
                    Bw2 = {(d, h_): Buf("w2") for d in (0, 1) for h_ in (0, 1)}
                    Bhd = {0: Buf("hist0"), 1: Buf("hist1")}
                    P.memset("dve", st[:], 0.0, w=list(Bst.values()))

                    def build_steps(d):
                        lo, hi = d * 64, (d + 1) * 64
                        if d == 1:
                            kblocks = [(kb, min(KB, NK - kb)) for kb in range(0, NK, KB)][::-1]
                        else:
                            kblocks = [(kb, min(KB, NKO - kb)) for kb in range(0, NKO, KB)]

                        def mms(bi):
                            kb, kn = kblocks[bi]
                            pset, pB = pssets.next()
                            for g in range(GH):
                                for ri in range(2):
                                    off = (g % gpb) * 2 * KB + ri * KB
                                    P.mm(pset[g // gpb][:, off:off + kn], Ws[:, g, ri, :], Ut[:, g, kb:kb + kn], True, True,
                                         r=[BWs, BUt], w=[pB])
                            S_, BS_ = Ssb[d][bi % 2], BSsb[d][bi % 2]
                            for j_ in range(nbk):
                                P.copy("act", bcast_ap(S_, j_ * 512, [(1, 512)], GH * 2 * KB, 64, lo), pset[j_][lo:hi, 0:512],
                                       r=[pB], w=[BS_])
                        steps = []
                        cur = 0
                        for bi, (kb, kn) in enumerate(kblocks):
                            S_, BS_ = Ssb[d][bi % 2], BSsb[d][bi % 2]
                            ks = list(range(kb, kb + kn))
                            if d == 1:
                                ks = ks[::-1]
                            for ki, k in enumerate(ks):
                                kk = k - kb
                                fns = []
                                if ki == 0:
                                    if bi == 0:
                                        fns.append(lambda: mms(0))
                                    if bi + 1 < len(kblocks):
                                        fns.append(lambda bi=bi: mms(bi + 1))
                                split = (k >= NKO)
                                parts = [(0, HG, 0), (HG, GH, 1)] if split else [(0, GH, None)]
                                c_, n_ = cur, 1 - cur
                                if k < NKO:
                                    fns.append(lambda k=k, c_=c_: P.copy(
                                        "pool", bcast_ap(hist, k, [(2 * NKO, GH), (NKO, 2)], GH * 2 * NKO, 64, lo),
                                        st[lo:hi, c_, :, :], r=[Bst[(d, c_, 0)], Bst[(d, c_, 1)]], w=[Bhd[d]]))

                                def bl(dct, key, h_):
                                    return [dct[key + (h_,)]] if h_ is not None else [dct[key + (0,)], dct[key + (1,)]]
                                for (gs, ge, h_) in parts:
                                    fns.append(lambda gs=gs, ge=ge, h_=h_, c_=c_: P.tt(
                                        "dve", w1[lo:hi, gs:ge, :], st[lo:hi, c_, gs:ge, :], Ac[lo:hi, 0, gs:ge, :], ALU.mult,
                                        r=bl(Bst, (d, c_), h_) + [BAc], w=bl(Bw1, (d,), h_)))
                                for (gs, ge, h_) in parts:
                                    fns.append(lambda gs=gs, ge=ge, h_=h_, c_=c_: P.tt(
                                        "dve", w2[lo:hi, gs:ge, :],
                                        bcast_ap(st, c_ * GH * 2 + gs * 2 + 1, [(2, ge - gs), (-1, 2)], 2 * GH * 2, 64, lo),
                                        Ac[lo:hi, 1, gs:ge, :], ALU.mult,
                                        r=bl(Bst, (d, c_), h_) + [BAc], w=bl(Bw2, (d,), h_)))
                                for (gs, ge, h_) in parts:
                                    fns.append(lambda gs=gs, ge=ge, h_=h_: P.tt(
                                        "dve", w1[lo:hi, gs:ge, :], w1[lo:hi, gs:ge, :], w2[lo:hi, gs:ge, :], ALU.add,
                                        r=bl(Bw1, (d,), h_) + bl(Bw2, (d,), h_), w=bl(Bw1, (d,), h_)))
                                for (gs, ge, h_) in parts:
                                    fns.append(lambda gs=gs, ge=ge, h_=h_, n_=n_, kk=kk, S_=S_, BS_=BS_: P.tt(
                                        "dve", st[lo:hi, n_, gs:ge, :], w1[lo:hi, gs:ge, :],
                                        bcast_ap(S_, gs * 2 * KB + kk, [(2 * KB, ge - gs), (KB, 2)], GH * 2 * KB, 64, lo), ALU.add,
                                        r=bl(Bw1, (d,), h_) + [BS_], w=bl(Bst, (d, n_), h_)))
                                steps.append((k, fns))
                                cur = n_
                        return steps

                    sa = build_steps(1)
                    sb = build_steps(0)
                    pre = [s_ for s_ in sa if s_[0] >= NKO]
                    pa = [s_ for s_ in sa if s_[0] < NKO]
                    for (_, fns) in pre:
                        for fn in fns:
                            fn()
                    assert len(pa) == len(sb) == NKO
                    for i in range(NKO):
                        fa, fb = pa[i][1], sb[i][1]
                        for j in range(max(len(fa), len(fb))):
                            if j < len(fa):
                                fa[j]()
                            if j < len(fb):
                                fb[j]()
                    P.phase_end()

                with ExitStack() as po_:
                    selT = po_.enter_context(SBT("selT", [128, 64, 128], BF16))
                    BselT = Buf("selT")
                    P.dma("pool", selT[:], c_selT, w=[BselT])
                    Zs = po_.enter_context(SBT("Zs", [128, GH, NKO], BF16))
                    BZs = Buf("Zs")
                    psy = Rot(psum_pool(po_, 3, "py"))
                    for g in range(GH):
                        ps, psB = psy.next()
                        P.mm(ps[:, 0:NKO], Tz[:, g, :], Ut[:, g, 0:NKO], True, False, r=[BTz, BUt], w=[psB])
                        P.mm(ps[:, 0:NKO], Et[:, g, 0, :], hist[:, g, 0, :], False, False, r=[BEt, Bhist], w=[psB])
                        P.mm(ps[:, 0:NKO], Et[:, g, 1, :], hist[:, g, 1, :], False, True, r=[BEt, Bhist], w=[psB])
                        P.act(Zs[:, g, :], ps[:, 0:NKO], AF.Gelu_apprx_tanh, r=[psB], w=[BZs])
                    psz = Rot(psum_pool(po_, 3, "pz"))
                    zst = sb_pool(po_, 2, "zst", [128, TE], BF16)
                    for t8 in range(GH // 8):
                        zs_, Bzs_ = zst.next()
                        for (b0, bn) in blocks_ext:
                            ps, psB = psz.next()
                            k0, kn = b0 // 8, bn // 8
                            for j in range(8):
                                for gl in range(8):
                                    g = t8 * 8 + gl
                                    P.mm(ps[:, j * kn:(j + 1) * kn], selT[:, gl * 8 + j, :], Zs[:, g, k0:k0 + kn],
                                         gl == 0, gl == 7, r=[BselT, BZs], w=[psB])
                            P.copy(rr_eng(("act", "dve")), bcast_ap(zs_, b0, [(8, kn), (1, 8)], TE),
                                   bass.AP(ps, 0, [[512, 128], [1, kn], [kn, 8]]), r=[psB], w=[Bzs_])
                        ft = (g0 // 8) + t8
                        P.dma("sp", zT[ft * 128:(ft + 1) * 128, :], zs_[:], r=[Bzs_])
                    P.phase_end()

        if cfg.get('stop') == 'C':
            return nc
        with ExitStack() as ph:
            ones, Bo, epsT, Be = make_consts(ph)
            gn = ph.enter_context(SBT("gn", [128, 5, DC], F32))
            Bg = Buf("gn")
            P.dma("sp", gn[:], gains, w=[Bg])
            mg = ph.enter_context(SBT("mg", [128, DC, TE], BF16))
            Bmg = Buf("mg")
            with ExitStack() as p0:
                ybT = p0.enter_context(SBT("ybT", [128, SC, TE], BF16))
                Byb = Buf("ybT")
                with ExitStack() as p1:
                    zTs = p1.enter_context(SBT("zTs", [128, SC, TE], BF16))
                    Bz = Buf("zTs")
                    P.dma("sp", zTs[:], zT.rearrange("(c p) t -> p c t", p=128), w=[Bz])
                    bg = p1.enter_context(SBT("bg", [128, SC], F32))
                    Bbg = Buf("bg")
                    P.dma("sp", bg[:], b_glu, w=[Bbg])
                    wpool = sb_pool(p1, 3, "wgl", [128, SC, 128], BF16)
                    pspool = Rot(psum_pool(p1, 4, "pg"))
                    sgp = sb_pool(p1, 2, "sgl", [128, 512], F32)

                    def ep_glu(m, sub, b0, bn, ps, psB):
                        sg, Bsg = sgp.next()
                        P.act(sg[:, 0:bn], ps[:, 0:bn], AF.Sigmoid, r=[psB, Bbg], w=[Bsg], bias=bg[:, m:m + 1])
                        P.tt("dve", ybT[:, m, b0:b0 + bn], sg[:, 0:bn], zTs[:, m, b0:b0 + bn], ALU.mult, r=[Bsg, Bz], w=[Byb])

                    linear(w_glu, range(SC), SC, 128, lambda c, b0, bn: zTs[:, c, b0:b0 + bn], [Bz], lambda m: blocks_ext,
                           ep_glu, wpool, pspool)
                    P.phase_end()

                with ExitStack() as p2:
                    yaS = p2.enter_context(SBT("yaS", [128, AC, TE], BF16))
                    Bya = Buf("yaS")
                    P.dma("sp", yaS[:], yaT.rearrange("(c p) t -> p c t", p=128), w=[Bya])
                    wa = sb_pool(p2, 2, "wa", [128, AC, 128], BF16)
                    wb = sb_pool(p2, 2, "wb", [128, SC, 128], BF16)
                    sgp = sb_pool(p2, 2, "sgab", [128, 2, TE], BF16)
                    pspool = Rot(psum_pool(p2, 6, "pm"))
                    t1p = sb_pool(p2, 2, "m1", [128, 512], F32)
                    t2p = sb_pool(p2, 2, "m2", [128, 512], F32)

                    def load_m(m):
                        wat, BwA = wa.next()
                        wbt, BwB = wb.next()
                        sg, Bsg = sgp.next()
                        P.dma("pool", wat[:], w_ba[m], w=[BwA])
                        P.dma("pool", wbt[:], w_bs[m], w=[BwB])
                        P.dma("sp", sg[:, 0, :], sgT[m * 128:(m + 1) * 128, :], w=[Bsg])
                        P.dma("sp", sg[:, 1, :], sgT[D + m * 128:D + (m + 1) * 128, :], w=[Bsg])
                        return (wat, BwA, wbt, BwB, sg, Bsg)

                    nxt = load_m(0)
                    for m in range(DC):
                        wat, BwA, wbt, BwB, sg, Bsg = nxt
                        if m + 1 < DC:
                            nxt = load_m(m + 1)
                        for (b0, bn) in blocks_ext:
                            psA, BA = pspool.next()
                            psB_, BB = pspool.next()
                            for c in range(AC):
                                P.mm(psA[:, 0:bn], wat[:, c, :], yaS[:, c, b0:b0 + bn], c == 0, c == AC - 1, r=[BwA, Bya], w=[BA])
                            for c in range(SC):
                                P.mm(psB_[:, 0:bn], wbt[:, c, :], ybT[:, c, b0:b0 + bn], c == 0, c == SC - 1, r=[BwB, Byb], w=[BB])
                            t1, B1 = t1p.next()
                            t2, B2 = t2p.next()
                            P.tt("dve", t1[:, 0:bn], psA[:, 0:bn], sg[:, 0, b0:b0 + bn], ALU.mult, r=[BA, Bsg], w=[B1])
                            P.tt("dve", t2[:, 0:bn], psB_[:, 0:bn], sg[:, 1, b0:b0 + bn], ALU.mult, r=[BB, Bsg], w=[B2])
                            P.tt("pool", mg[:, m, b0:b0 + bn], t1[:, 0:bn], t2[:, 0:bn], ALU.add, r=[B1, B2], w=[Bmg])
                    P.phase_end()

            with ExitStack() as ptmp:
                ssq = proj_ssq(ph, ptmp, w_out, DC, lambda c, b0, bn: mg[:, c, b0:b0 + bn], [Bmg], blocks_ext, oT, ones, Bo, "o")
                P.phase_end()
            rstd1, Br1 = rstd_from(ph, ssq, blocks_ext, TE, epsT, Be, "rstd1")
            ssq2 = ssq_tiles(ph, len(blocks_ext), "h")
            with ExitStack() as p3:
                op_ = sb_pool(p3, 2, "o_in", [128, TE], F32)
                xp_ = sb_pool(p3, 2, "x_in", [128, TE], F32)
                sqp = sb_pool(p3, 3, "hsq", [128, 512], BF16)
                hb_ = sb_pool(p3, 2, "h_bf", [128, TE], BF16)
                pending = []

                def load_r(m):
                    o, Bo_ = op_.next()
                    x, Bx_ = xp_.next()
                    P.dma("sp", o[:], oT[m * 128:(m + 1) * 128, :], w=[Bo_])
                    P.dma("sp", x[:], xT[m * 128:(m + 1) * 128, 0:TE], w=[Bx_])
                    return (o, Bo_, x, Bx_)

                nxt = load_r(0)
                for m in range(DC):
                    o, Bo_, x, Bx_ = nxt
                    if m + 1 < DC:
                        nxt = load_r(m + 1)
                    P.stt("dve", o[:], o[:], gn[:, 1, m:m + 1], rstd1[:], ALU.mult, ALU.mult, r=[Bo_, Bg, Br1], w=[Bo_])
                    P.tt("pool", o[:], o[:], x[:], ALU.add, r=[Bo_, Bx_], w=[Bo_])
                    P.dma("sp", h1T[m * 128:(m + 1) * 128, :], o[:], r=[Bo_])
                    hb, Bhb = hb_.next()
                    P.act(hb[:], o[:], AF.Copy, r=[Bo_, Bg], w=[Bhb], scale=gn[:, 2, m:m + 1])
                    P.dma("sp", hn2T[m * 128:(m + 1) * 128, :], hb[:], r=[Bhb])
                    for bi, (b0, bn) in enumerate(blocks_ext):
                        sq, Bsq = sqp.next()
                        P.act(sq[:, 0:bn], o[:, b0:b0 + bn], AF.Square, r=[Bo_], w=[Bsq])
                        while len(pending) > 2:
                            pending.pop(0)()
                        sp_, spB = ssq2[bi]
                        pending.append(lambda sp_=sp_, spB=spB, sq=sq, Bsq=Bsq, bn=bn, m=m:
                                       P.mm(sp_[:, 0:bn], ones[:], sq[:, 0:bn], m == 0, m == DC - 1, r=[Bo, Bsq], w=[spB]))
                while pending:
                    pending.pop(0)()
                P.phase_end()
            rstd2, Br2 = rstd_from(ph, ssq2, blocks_ext, TE, epsT, Be, "rstd2")
            P.dma("sp", rs2T, rstd2[:], r=[Br2])
            P.phase_end()

        if cfg.get('stop') == 'D':
            return nc
        with ExitStack() as ph:
            hn2 = ph.enter_context(SBT("hn2", [128, DC, TE], BF16))
            Bh2 = Buf("hn2")
            P.dma("sp", hn2[:], hn2T.rearrange("(c p) t -> p c t", p=128), w=[Bh2])
            cw = ph.enter_context(SBT("cw", [128, 3, FC], F32))
            cb = ph.enter_context(SBT("cb", [128, FC], F32))
            Bcw = Buf("cw")
            P.dma("sp", cw[:], conv_w, w=[Bcw])
            P.dma("sp", cb[:], conv_b, w=[Bcw])
            rs2 = ph.enter_context(SBT("rs2", [128, TE], F32))
            Brs2 = Buf("rs2")
            P.dma("sp", rs2[:], rs2T, w=[Brs2])
            wpool = sb_pool(ph, 4, "wup", [128, DC, 128], BF16)
            pspool = Rot(psum_pool(ph, 8, "pu"))
            Ap = sb_pool(ph, 2, "Asb", [128, TE + 2], F32)
            for (a_, Ba_) in Ap.items:
                P.memset("pool", a_[:], 0.0, w=[Ba_])
            cp_ = sb_pool(ph, 2, "cv", [128, TO], F32)
            gp_ = sb_pool(ph, 2, "gl", [128, TO], F32)
            op_ = sb_pool(ph, 2, "actst", [128, TO], BF16)

            def load_w(j):
                wa_, BwA = wpool.next()
                wb_, BwB = wpool.next()
                P.dma("pool", wa_[:], w_up[j], w=[BwA])
                P.dma("pool", wb_[:], w_up[FC + j], w=[BwB])
                return (wa_, BwA, wb_, BwB)

            nxt = load_w(0)
            for j in range(FC):
                wa_, BwA, wb_, BwB = nxt
                if j + 1 < FC:
                    nxt = load_w(j + 1)
                A, BA = Ap.next()
                for (b0, bn) in blocks_ext:
                    ps, psB = pspool.next()
                    for c in range(DC):
                        P.mm(ps[:, 0:bn], wa_[:, c, :], hn2[:, c, b0:b0 + bn], c == 0, c == DC - 1, r=[BwA, Bh2], w=[psB])
                    P.tt("dve", A[:, 1 + b0:1 + b0 + bn], ps[:, 0:bn], rs2[:, b0:b0 + bn], ALU.mult, r=[psB, Brs2], w=[BA])
                bps = []
                for (b0, bn) in blocks_own:
                    ps, psB = pspool.next()
                    for c in range(DC):
                        P.mm(ps[:, 0:bn], wb_[:, c, :], hn2[:, c, b0:b0 + bn], c == 0, c == DC - 1, r=[BwB, Bh2], w=[psB])
                    bps.append((ps, psB, b0, bn))
                cv, Bcv = cp_.next()
                gl, Bgl = gp_.next()
                ot, Bot = op_.next()
                P.ts("dve", cv[:], A[:, 0:TO], cw[:, 0, j:j + 1], None, ALU.mult, r=[BA, Bcw], w=[Bcv])
                P.stt("dve", cv[:], A[:, 1:TO + 1], cw[:, 1, j:j + 1], cv[:], ALU.mult, ALU.add, r=[BA, Bcw, Bcv], w=[Bcv])
                P.stt("dve", cv[:], A[:, 2:TO + 2], cw[:, 2, j:j + 1], cv[:], ALU.mult, ALU.add, r=[BA, Bcw, Bcv], w=[Bcv])
                P.act(gl[:], cv[:], AF.Gelu_apprx_tanh, r=[Bcv, Bcw], w=[Bgl], bias=cb[:, j:j + 1])
                P.tt("pool", gl[:], gl[:], rs2[:, 0:TO], ALU.mult, r=[Bgl, Brs2], w=[Bgl])
                for (ps, psB, b0, bn) in bps:
                    P.tt("dve", ot[:, b0:b0 + bn], ps[:, 0:bn], gl[:, b0:b0 + bn], ALU.mult, r=[psB, Bgl], w=[Bot])
                P.dma("sp", actT[j * 128:(j + 1) * 128, :], ot[:], r=[Bot])
            P.phase_end()

        if cfg.get('stop') == 'E':
            return nc
        with ExitStack() as ph:
            ones, Bo, epsT, Be = make_consts(ph)
            gn = ph.enter_context(SBT("gn", [128, 5, DC], F32))
            Bg = Buf("gn")
            P.dma("sp", gn[:], gains, w=[Bg])
            NBK = len(blocks_own)
            BW = max(bn for _, bn in blocks_own)
            ssqF = [(ph.enter_context(PST("ssqF%d" % i, [128, 512], F32)), Buf("ssqF", True)) for i in range(NBK)]
            rsF = [(ph.enter_context(SBT("rsF%d" % i, [128, BW], F32)), Buf("rsF")) for i in range(NBK)]
            aS = ph.enter_context(SBT("aS", [128, FC, BW], BF16))
            BaS = Buf("aS")
            wpool = sb_pool(ph, 3, "wdn", [128, FC, 128], BF16)
            pspool = Rot(psum_pool(ph, 4, "pf"))
            ost = sb_pool(ph, 2, "ostF", [128, BW], F32)
            sqp = sb_pool(ph, 3, "osqF", [128, 512], BF16)
            dp_ = sb_pool(ph, 3, "d_in", [128, BW], F32)
            hp_ = sb_pool(ph, 3, "h_in", [128, BW], F32)
            BdT = {}
            actTv = actT.rearrange("(c p) t -> p c t", p=128)

            def f2_load(bi, m):
                b0, bn = blocks_own[bi]
                d_, Bd_ = dp_.next()
                h_, Bh_ = hp_.next()
                P.dma("sp", d_[:, 0:bn], dT[m * 128:(m + 1) * 128, b0:b0 + bn], r=[BdT[(bi, m)]], w=[Bd_])
                P.dma("sp", h_[:, 0:bn], h1T[m * 128:(m + 1) * 128, b0:b0 + bn], w=[Bh_])
                return (d_, Bd_, h_, Bh_)

            def f2_fin(bi, m, ld):
                b0, bn = blocks_own[bi]
                d_, Bd_, h_, Bh_ = ld
                rs, Brs = rsF[bi]
                P.stt("dve", d_[:, 0:bn], d_[:, 0:bn], gn[:, 3, m:m + 1], rs[:, 0:bn], ALU.mult, ALU.mult, r=[Bd_, Bg, Brs], w=[Bd_])
                P.tt("pool", d_[:, 0:bn], d_[:, 0:bn], h_[:, 0:bn], ALU.add, r=[Bd_, Bh_], w=[Bd_])
                P.dma("sp", h2T[m * 128:(m + 1) * 128, b0:b0 + bn], d_[:, 0:bn], r=[Bd_])

            for bi, (b0, bn) in enumerate(blocks_own):
                P.dma("sp", aS[:, :, 0:bn], actTv[:, :, b0:b0 + bn], w=[BaS])
                pending = []
                state = {}
                sp, spB = ssqF[bi]

                def ep(m, sub, b0_, bn_, ps, psB, sp=sp, spB=spB):
                    o, Bos = ost.next()
                    state["o"] = (o, Bos)
                    P.copy("dve", o[:, 0:bn_], ps[:, 0:bn_], r=[psB], w=[Bos])
                    sq, Bsq = sqp.next()
                    P.act(sq[:, 0:bn_], ps[:, 0:bn_], AF.Square, r=[psB], w=[Bsq])
                    while pending:
                        pending.pop(0)()
                    pending.append(lambda: P.mm(sp[:, 0:bn_], ones[:], sq[:, 0:bn_], m == 0, m == DC - 1, r=[Bo, Bsq], w=[spB]))

                def post(m, bi=bi, b0=b0, bn=bn):
                    o, Bos = state["o"]
                    BdT[(bi, m)] = Buf("dT")
                    P.dma("sp", dT[m * 128:(m + 1) * 128, b0:b0 + bn], o[:, 0:bn], r=[Bos], w=[BdT[(bi, m)]])
                    if bi > 0:
                        if "ld" in state:
                            f2_fin(bi - 1, m - 1, state.pop("ld"))
                        state["ld"] = f2_load(bi - 1, m)

                linear(w_down, range(DC), FC, 128, lambda c, b0_, bn_: aS[:, c, 0:bn_], [BaS], lambda m, b0=b0, bn=bn: [(b0, bn)],
                       ep, wpool, pspool, post)
                while pending:
                    pending.pop(0)()
                if "ld" in state:
                    f2_fin(bi - 1, DC - 1, state.pop("ld"))
                rs, Brs = rsF[bi]
                P.act(rs[:, 0:bn], sp[:, 0:bn], AF.Sqrt, r=[spB, Be], w=[Brs], bias=epsT[:, 0:1], scale=1.0 / D)
                P.recip(rs[:, 0:bn], rs[:, 0:bn], r=[Brs], w=[Brs])
            last = NBK - 1
            ld = f2_load(last, 0)
            for m in range(DC):
                nx = f2_load(last, m + 1) if m + 1 < DC else None
                f2_fin(last, m, ld)
                ld = nx
            P.phase_end()

        if cfg.get('stop') == 'F':
            return nc
        with ExitStack() as ph:
            ones, Bo, epsT, Be = make_consts(ph)
            gn = ph.enter_context(SBT("gn", [128, 5, DC], F32))
            Bg = Buf("gn")
            P.dma("sp", gn[:], gains, w=[Bg])
            h2b = ph.enter_context(SBT("h2b", [128, DC, TO], BF16))
            Bh2b = Buf("h2b")
            P.dma("pool", h2b[:], h2T.rearrange("(c p) t -> p c t", p=128), w=[Bh2b])
            pTb = ph.enter_context(SBT("pTb", [128, PC, TO], BF16))
            BpT = Buf("pTb")
            P.dma("pool", pTb[:], pT.rearrange("(c p) t -> p c t", p=128), w=[BpT])
            wpl = ph.enter_context(SBT("wpl", [128, DC, PC, 128], BF16))
            Bwpl = Buf("wpl")
            P.dma("pool", wpl[:], w_ple.rearrange("m p c j -> p m c j"), w=[Bwpl])
            ssq = ssq_tiles(ph, len(blocks_own), "p")
            pspool = Rot(psum_pool(ph, 5, "pp"))
            sqp = sb_pool(ph, 3, "psq", [128, 512], BF16)
            pending = []
            for m in range(DC):
                for bi, (b0, bn) in enumerate(blocks_own):
                    ps, psB = pspool.next()
                    for c in range(PC):
                        P.mm(ps[:, 0:bn], wpl[:, m, c, :], pTb[:, c, b0:b0 + bn], c == 0, c == PC - 1, r=[Bwpl, BpT], w=[psB])
                    sq, Bsq = sqp.next()
                    P.act(sq[:, 0:bn], ps[:, 0:bn], AF.Square, r=[psB], w=[Bsq])
                    while len(pending) > 1:
                        pending.pop(0)()
                    sp_, spB = ssq[bi]
                    pending.append(lambda sp_=sp_, spB=spB, sq=sq, Bsq=Bsq, bn=bn, m=m:
                                   P.mm(sp_[:, 0:bn], ones[:], sq[:, 0:bn], m == 0, m == DC - 1, r=[Bo, Bsq], w=[spB]))
            while pending:
                pending.pop(0)()
            rstdp, Brp = rstd_from(ph, ssq, blocks_own, TO, epsT, Be, "rstdp")
            wpool = sb_pool(ph, 3, "wpg", [128, DC, 128], BF16)
            hp_ = sb_pool(ph, 2, "h2in", [128, TO], F32)
            sgp = sb_pool(ph, 2, "sgp", [128, 512], F32)
            tp_ = sb_pool(ph, 2, "tpl", [128, 512], F32)
            state = {}

            def ep_pg(m, sub, b0, bn, ps, psB):
                if b0 == 0:
                    h_, Bh_ = hp_.next()
                    P.dma("sp", h_[:], h2T[m * 128:(m + 1) * 128, :], w=[Bh_])
                    state["h"] = (h_, Bh_)
                h_, Bh_ = state["h"]
                ps2, ps2B = pspool.next()
                for c in range(PC):
                    P.mm(ps2[:, 0:bn], wpl[:, m, c, :], pTb[:, c, b0:b0 + bn], c == 0, c == PC - 1, r=[Bwpl, BpT], w=[ps2B])
                sg, Bsg = sgp.next()
                tp, Btp = tp_.next()
                P.act(sg[:, 0:bn], ps[:, 0:bn], AF.Sigmoid, r=[psB], w=[Bsg])
                P.tt("dve", tp[:, 0:bn], ps2[:, 0:bn], rstdp[:, b0:b0 + bn], ALU.mult, r=[ps2B, Brp], w=[Btp])
                P.tt("pool", tp[:, 0:bn], tp[:, 0:bn], sg[:, 0:bn], ALU.mult, r=[Btp, Bsg], w=[Btp])
                P.stt("dve", h_[:, b0:b0 + bn], tp[:, 0:bn], gn[:, 4, m:m + 1], h_[:, b0:b0 + bn], ALU.mult, ALU.add,
                      r=[Btp, Bg, Bh_], w=[Bh_])

            def post_pg(m):
                h_, Bh_ = state["h"]
                P.dma("sp", outT[m * 128:(m + 1) * 128, :], h_[:], r=[Bh_])

            linear(w_pg, range(DC), DC, 128, lambda c, b0, bn: h2b[:, c, b0:b0 + bn], [Bh2b], lambda m: blocks_own,
                   ep_pg, wpool, pspool, post_pg)
            P.phase_end()
    return nc


def tile_w(W, mw=128):
    K, N = W.shape
    return np.ascontiguousarray(W.reshape(K // 128, 128, N // mw, mw).transpose(2, 1, 0, 3))


def pvec(v):
    return np.ascontiguousarray(np.asarray(v, np.float32).reshape(-1, 128).T)


def make_constants(cfg):
    T = cfg["T"]
    TO = T // 2
    TE = TO + HALO
    c = {}
    j = np.arange(8, dtype=np.float32)
    n1 = np.zeros((128, 8), np.float32)
    n1[:64] = 7 - j
    n1[64:] = j
    nt = np.zeros((128, 26), np.float32)
    nt[:, 0:8] = n1
    nt[:, 8:16] = -n1
    nt[:, 16:24] = 8 - n1
    nt[:, 24] = 1
    nt[:, 25] = 8
    c["c_ntab"] = nt
    jj = np.arange(128) // 16
    mk = np.zeros((128, 2, 128), np.float32)
    mk[:, 0, :] = (jj[:, None] <= jj[None, :])
    mk[:, 1, :] = (jj[:, None] >= jj[None, :])
    c["c_mask"] = mk
    c["c_ident"] = np.eye(128, dtype=np.float32)
    pm = np.zeros((128, 128), np.float32)
    pm[(np.arange(128) + 64) % 128, np.arange(128)] = 1
    c["c_perm"] = pm
    sel = np.zeros((128, 64, 128), np.float32)
    selT = np.zeros((128, 64, 128), np.float32)
    for gl in range(8):
        for jx in range(8):
            for cc in range(16):
                sel[16 * gl + cc, gl * 8 + jx, 16 * jx + cc] = 1
                selT[16 * jx + cc, gl * 8 + jx, 16 * gl + cc] = 1
    c["c_sel"] = sel
    c["c_selT"] = selT
    X0 = T - 128
    MW = T - 128 + TE
    delta = np.arange(128)[:, None] - (np.arange(MW)[None, :] - X0)
    w = np.zeros_like(delta, dtype=np.float32)
    for (win, dil) in cfg["patterns"]:
        ns = win // (2 * dil)
        w += ((delta % dil == 0) & (np.abs(delta) <= ns * dil)).astype(np.float32)
    c["c_mtab"] = w
    return c


def prep_core_inputs(cfg, inp, b, half, shared):
    D, T, H, G = cfg["D"], cfg["T"], cfg["H"], cfg["G"]
    TO = T // 2
    idx = np.arange(T) if half == 0 else np.arange(T)[::-1]
    m = dict(shared)
    m["xT"] = np.ascontiguousarray(inp["x"][b][idx].T)
    m["xTt"] = np.ascontiguousarray(m["xT"].reshape(D // 128, 128, T // 128, 128).transpose(2, 1, 0, 3))
    m["pT"] = np.ascontiguousarray(inp["p"][0, b][idx[:TO]].T)
    inv_freq = (10000.0 ** (-np.arange(0, 128, 2, dtype=np.float32) / 128)).astype(np.float32)
    ang = idx.astype(np.float32)[:, None] * inv_freq[None, :]
    ang = np.concatenate([ang, ang], axis=-1)
    sgn = np.concatenate([-np.ones(64, np.float32), np.ones(64, np.float32)])
    m["cosT"] = np.ascontiguousarray(np.cos(ang).T.astype(np.float32))
    m["sinT"] = np.ascontiguousarray((np.sin(ang) * sgn[None, :]).T.astype(np.float32))
    dd = [0, 1] if half == 0 else [1, 0]
    lam = np.zeros((128, 3, G), np.float32)
    sB = np.zeros((128, 2, G, 16), np.float32)
    sC = np.zeros((128, 2, G, 16), np.float32)
    for d in range(2):
        s_ = dd[d]
        lam[d * 64:(d + 1) * 64, 0] = inp["ssm_lambda_re"][0, s_].T
        lam[d * 64:(d + 1) * 64, 1] = inp["ssm_lambda_im"][0, s_].T
        lam[d * 64:(d + 1) * 64, 2] = inp["ssm_log_dt"][0, s_][None, :]
        sB[d * 64:(d + 1) * 64, 0] = inp["ssm_b_re"][0, s_].transpose(1, 0, 2)
        sB[d * 64:(d + 1) * 64, 1] = inp["ssm_b_im"][0, s_].transpose(1, 0, 2)
        sC[d * 64:(d + 1) * 64, 0] = inp["ssm_c_re"][0, s_].transpose(2, 0, 1)
        sC[d * 64:(d + 1) * 64, 1] = inp["ssm_c_im"][0, s_].transpose(2, 0, 1)
    m["s_lam"], m["s_B"], m["s_C"] = lam, sB, sC
    cwv = inp["conv_w"][0]
    if half == 1:
        cwv = cwv[::-1]
    m["conv_w"] = np.ascontiguousarray(np.stack([pvec(cwv[t]) for t in range(3)], axis=1))
    return m


def prep_shared(cfg, inp):
    D, T, H, G, DFF = cfg["D"], cfg["T"], cfg["H"], cfg["G"], cfg["DFF"]
    AW, SW = H * 128, G * 16
    sh = dict(make_constants(cfg))
    w_in = inp["w_in"][0]
    wq, wk, wv = w_in[:, 0:AW], w_in[:, AW:2 * AW], w_in[:, 2 * AW:3 * AW]
    wu = w_in[:, 3 * AW:3 * AW + SW]
    wg = w_in[:, 3 * AW + SW:]
    sh["w_qk"] = tile_w(np.concatenate([wq, wk], axis=1))
    sh["w_v"] = tile_w(wv, min(AW, 512))
    sh["w_u"] = tile_w(wu)
    sh["w_g"] = tile_w(wg)
    sh["w_ba"] = tile_w(inp["w_branch_attn"][0])
    sh["w_bs"] = tile_w(inp["w_branch_ssm"][0])
    sh["w_out"] = tile_w(inp["w_out"][0])
    sh["w_glu"] = tile_w(inp["w_glu"][0])
    sh["b_glu"] = pvec(inp["b_glu"][0])
    sh["w_up"] = tile_w(inp["w_up"][0])
    sh["conv_b"] = pvec(inp["conv_b"][0])
    sh["w_down"] = tile_w(inp["w_down"][0])
    sh["w_ple"] = tile_w(inp["w_ple"][0])
    sh["w_pg"] = tile_w(inp["w_ple_gate"][0])
    sh["gains"] = np.ascontiguousarray(np.stack(
        [pvec(inp[k][0]) for k in ("g_mix_pre", "g_mix_post", "g_ffn_pre", "g_ffn_post", "g_ple")], axis=1))
    sh["s_D"] = np.ascontiguousarray(np.tile(inp["ssm_d"][0].reshape(G, 16).T, (8, 1)))
    return sh


def run_cfg(cfg, inp, dbg=False):
    inp = {k: np.asarray(v, np.float32) for k, v in inp.items()}
    nb = cfg["nb"]
    T, D = cfg["T"], cfg["D"]
    TO = T // 2
    shared = prep_shared(cfg, inp)
    in_maps = []
    for b in range(nb):
        for half in range(2):
            in_maps.append(prep_core_inputs(cfg, inp, b, half, shared))
    nc = build_program(cfg, dbg=dbg)
    res = run_bass_kernel_spmd(nc, in_maps, core_ids=list(range(2 * nb)))
    out = np.zeros((nb, T, D), np.float32)
    for b in range(nb):
        for half in range(2):
            idx = np.arange(T) if half == 0 else np.arange(T)[::-1]
            out[b, idx[:TO]] = res.results[b * 2 + half]["outT"].T
    return out, res


def kernel(**inputs):
    out, _ = run_cfg(FULL_CFG, inputs)
    return out
```
